# Optimizing a Trainium2 kernel written in Bass

```python
import math
import jax, jax.numpy as jnp
from jax import lax
import numpy as np

D_MODEL = 1024
BATCH = 4
SEQ = 4096
DEPTH = 2

A_WIDTH = D_MODEL // 2
A_GROUPS = 4
A_GROUP_DIM = A_WIDTH // A_GROUPS
CHUNK = 128
B_WIDTH = D_MODEL // 2
POOL_WINDOWS = (2, 4, 8, 16)
B_GROUP_DIM = B_WIDTH // len(POOL_WINDOWS)
HEAD_DIM = 64
C_HEADS = (3 * D_MODEL) // (8 * HEAD_DIM)
C_WIDTH = C_HEADS * 2 * HEAD_DIM
D_PATTERNS = ((128, 1), (512, 4), (2048, 16))
D_HEADS_PER_GROUP = D_MODEL // (4 * HEAD_DIM)
D_HEADS = D_HEADS_PER_GROUP * len(D_PATTERNS)
D_QKV_WIDTH = D_HEADS * HEAD_DIM
D_OUT_WIDTH = D_HEADS_PER_GROUP * HEAD_DIM
Q_BLOCK = 128
ROPE_THETA = 500000.0
ROPE_DIM = HEAD_DIM // 4
D_FF = 2816
CONV_WIDTH = 3
EPS = 1e-6
MAX_POS_OFFSET = 1024

kernel_name = "hybrid_gmlp_pool_diffattn_dilated_block"


def rms_norm(x, g, eps=EPS):
    xf = x.astype(jnp.float32)
    y = xf * lax.rsqrt(jnp.mean(xf * xf, axis=-1, keepdims=True) + eps)
    return (y * g.astype(jnp.float32)).astype(x.dtype)


def rope_tables(positions):
    inv = ROPE_THETA ** (-jnp.arange(0, ROPE_DIM, 2, dtype=jnp.float32) / ROPE_DIM)
    ang = positions.astype(jnp.float32)[..., None] * inv
    return jnp.cos(ang), jnp.sin(ang)


def apply_partial_rope(x, cos, sin):
    half = ROPE_DIM // 2
    xr = x[..., :ROPE_DIM].astype(jnp.float32)
    x1, x2 = xr[..., :half], xr[..., half:]
    c, s = cos[:, :, None, :], sin[:, :, None, :]
    rot = jnp.concatenate([x1 * c - x2 * s, x2 * c + x1 * s], axis=-1).astype(x.dtype)
    return jnp.concatenate([rot, x[..., ROPE_DIM:]], axis=-1)


def chunked_gmlp(z, v_gain, w_s, b_s):
    Bn, S, _ = z.shape
    u, v = jnp.split(z, 2, axis=-1)
    v = rms_norm(v, v_gain)
    nc = S // CHUNK
    v = v.reshape(Bn, nc, CHUNK, A_GROUPS, A_GROUP_DIM)
    causal = jnp.tril(jnp.ones((CHUNK, CHUNK), dtype=bool))
    w = jnp.where(causal[None], w_s, 0).astype(v.dtype)
    mixed = jnp.einsum('gts,bnsgc->bntgc', w, v) + b_s.T[:, :, None]
    return u * mixed.reshape(Bn, S, A_WIDTH)


def multiscale_pool(xb, pool_w, pool_scale):
    Bn, S, _ = xb.shape
    G = len(POOL_WINDOWS)
    xg = xb.reshape(Bn, S, G, B_GROUP_DIM)
    csum = jnp.pad(jnp.cumsum(xg.astype(jnp.float32), axis=1), ((0, 0), (1, 0), (0, 0), (0, 0)))
    t = jnp.arange(S)
    pooled = []
    for g, w in enumerate(POOL_WINDOWS):
        cg = csum[:, :, g]
        lagged = jnp.pad(cg, ((0, 0), (w - 1, 0), (0, 0)))[:, :S]
        count = jnp.minimum(t + 1, w).astype(jnp.float32)
        pooled.append((cg[:, 1:] - lagged) / count[None, :, None])
    pooled = jnp.stack(pooled, axis=2).astype(xb.dtype) - xg
    y = jnp.einsum('bsgc,gcd->bsgd', pooled, pool_w)
    return y.reshape(Bn, S, B_WIDTH) * pool_scale


def even_mixer(h, w_in, v_gain, w_s, b_s, pool_w, pool_scale, w_out):
    p = h @ w_in
    ya = chunked_gmlp(jax.nn.gelu(p[..., :2 * A_WIDTH]), v_gain, w_s, b_s)
    yb = multiscale_pool(p[..., 2 * A_WIDTH:], pool_w, pool_scale)
    return jnp.concatenate([ya, yb], axis=-1) @ w_out


def diff_attention(q, k, v, cos, sin, lq1, lk1, lq2, lk2, subln_gain, lambda_init):
    Bn, S, _ = q.shape
    f32 = jnp.float32
    q = apply_partial_rope(q.reshape(Bn, S, 2 * C_HEADS, HEAD_DIM), cos, sin)
    k = apply_partial_rope(k.reshape(Bn, S, 2 * C_HEADS, HEAD_DIM), cos, sin)
    v = v.reshape(Bn, S, C_HEADS, 2 * HEAD_DIM)
    lam = (jnp.exp(jnp.sum(lq1.astype(f32) * lk1.astype(f32)))
           - jnp.exp(jnp.sum(lq2.astype(f32) * lk2.astype(f32))) + lambda_init)
    nb = S // Q_BLOCK
    q_blocks = q.reshape(Bn, nb, Q_BLOCK, 2 * C_HEADS, HEAD_DIM).transpose(1, 0, 2, 3, 4)
    k_pos = jnp.arange(S)
    scale = HEAD_DIM ** -0.5

    def attend(args):
        qb, blk = args
        s = jnp.einsum('bqhd,bkhd->bhqk', qb, k).astype(f32) * scale
        q_pos = blk * Q_BLOCK + jnp.arange(Q_BLOCK)
        s = jnp.where(k_pos[None, :] <= q_pos[:, None], s, -jnp.inf)
        p = jax.nn.softmax(s, axis=-1).reshape(Bn, C_HEADS, 2, Q_BLOCK, S)
        a = p[:, :, 0] - lam * p[:, :, 1]
        return jnp.einsum('bhqk,bkhd->bqhd', a.astype(v.dtype), v)

    o = lax.map(attend, (q_blocks, jnp.arange(nb)))
    o = o.transpose(1, 0, 2, 3, 4).reshape(Bn, S, C_HEADS, 2 * HEAD_DIM)
    o = rms_norm(o, subln_gain) * (1.0 - lambda_init)
    return o.reshape(Bn, S, C_WIDTH)


def strided_band_attention(q, k, v, window, dilation):
    Bn, S, H, dh = q.shape
    nk = window // dilation
    L = S // dilation
    nb = -(-L // nk)
    Lp = nb * nk

    def strided(t):
        return t.reshape(Bn, L, dilation, H, dh).transpose(0, 2, 1, 3, 4)

    def banded(t):
        t = jnp.pad(strided(t), ((0, 0), (0, 0), (nk, Lp - L), (0, 0), (0, 0)))
        prev = t[:, :, :Lp].reshape(Bn, dilation, nb, nk, H, dh)
        cur = t[:, :, nk:].reshape(Bn, dilation, nb, nk, H, dh)
        return jnp.concatenate([prev, cur], axis=3)

    qs = jnp.pad(strided(q), ((0, 0), (0, 0), (0, Lp - L), (0, 0), (0, 0))).reshape(Bn, dilation, nb, nk, H, dh)
    kb, vb = banded(k), banded(v)
    a = jnp.arange(nk)[:, None]
    m = jnp.arange(2 * nk)[None, :]
    dist = a + nk - m
    k_pos = (jnp.arange(nb)[:, None, None] - 1) * nk + m[None]
    mask = (dist >= 0) & (dist <= nk) & (k_pos >= 0)
    s = jnp.einsum('brnqhd,brnkhd->brnhqk', qs, kb).astype(jnp.float32) * (HEAD_DIM ** -0.5)
    s = jnp.where(mask[:, None], s, -jnp.inf)
    lse = jax.nn.logsumexp(s, axis=-1)
    p = jnp.exp(s - lse[..., None])
    o = jnp.einsum('brnhqk,brnkhd->brnqhd', p.astype(v.dtype), vb)
    o = o.reshape(Bn, dilation, Lp, H, dh)[:, :, :L].transpose(0, 2, 1, 3, 4).reshape(Bn, S, H, dh)
    lse = lse.transpose(0, 1, 2, 4, 3).reshape(Bn, dilation, Lp, H)[:, :, :L]
    lse = lse.transpose(0, 2, 1, 3).reshape(Bn, S, H)
    return o, lse


def dilated_attention(q, k, v, cos, sin):
    Bn, S, _ = q.shape
    q = apply_partial_rope(q.reshape(Bn, S, D_HEADS, HEAD_DIM), cos, sin)
    k = apply_partial_rope(k.reshape(Bn, S, D_HEADS, HEAD_DIM), cos, sin)
    v = v.reshape(Bn, S, D_HEADS, HEAD_DIM)
    outs, lses = [], []
    for g, (window, dilation) in enumerate(D_PATTERNS):
        hs = slice(g * D_HEADS_PER_GROUP, (g + 1) * D_HEADS_PER_GROUP)
        o, lse = strided_band_attention(q[:, :, hs], k[:, :, hs], v[:, :, hs], window, dilation)
        outs.append(o)
        lses.append(lse)
    wts = jax.nn.softmax(jnp.stack(lses, axis=0), axis=0)
    o = jnp.sum(wts[..., None] * jnp.stack(outs, axis=0).astype(jnp.float32), axis=0)
    return o.astype(q.dtype).reshape(Bn, S, D_OUT_WIDTH)


def odd_mixer(h, cos, sin, w_in, lq1, lk1, lq2, lk2, subln_gain, w_out, layer):
    splits = [int(i) for i in np.cumsum([C_WIDTH, C_WIDTH, C_WIDTH, D_QKV_WIDTH, D_QKV_WIDTH])]
    cq, ck, cv, dq, dk, dv = jnp.split(h @ w_in, splits, axis=-1)
    lambda_init = 0.8 - 0.6 * math.exp(-0.3 * layer)
    yc = diff_attention(cq, ck, cv, cos, sin, lq1, lk1, lq2, lk2, subln_gain, lambda_init)
    yd = dilated_attention(dq, dk, dv, cos, sin)
    return jnp.concatenate([yc, yd], axis=-1) @ w_out


def conv_ffn(h, w_up, conv_w, conv_b, w_down):
    a, g = jnp.split(h @ w_up, 2, axis=-1)
    a = lax.conv_general_dilated(a, conv_w, window_strides=(1,), padding=((CONV_WIDTH - 1, 0),),
                                 dimension_numbers=('NWC', 'WIO', 'NWC'), feature_group_count=D_FF) + conv_b
    return (jax.nn.gelu(a) * g) @ w_down


def setup_inputs(seed: int = 0) -> dict:
    key = jax.random.key(seed)
    keys = iter(jax.random.split(key, 32))
    n_even, n_odd = (DEPTH + 1) // 2, DEPTH // 2
    f32 = jnp.float32

    def normal(shape, scale):
        return jax.random.normal(next(keys), shape, f32) * scale

    def gain(shape):
        return 1.0 + normal(shape, 0.02)

    x = normal((BATCH, SEQ, D_MODEL), 1.0)
    positions = (jnp.arange(SEQ, dtype=jnp.int32)[None, :]
                 + jax.random.randint(next(keys), (BATCH, 1), 0, MAX_POS_OFFSET, dtype=jnp.int32))
    odd_in = 3 * C_WIDTH + 3 * D_QKV_WIDTH
    return {
        'x': x,
        'positions': positions,
        'norm_mix': gain((DEPTH, D_MODEL)),
        'norm_ffn': gain((DEPTH, D_MODEL)),
        'final_norm': gain((D_MODEL,)),
        'even_w_in': normal((n_even, D_MODEL, 2 * A_WIDTH + B_WIDTH), D_MODEL ** -0.5),
        'gmlp_v_gain': gain((n_even, A_WIDTH)),
        'gmlp_w_s': normal((n_even, A_GROUPS, CHUNK, CHUNK), CHUNK ** -0.5),
        'gmlp_b_s': gain((n_even, A_GROUPS, CHUNK)),
        'pool_w': normal((n_even, len(POOL_WINDOWS), B_GROUP_DIM, B_GROUP_DIM), B_GROUP_DIM ** -0.5),
        'pool_scale': gain((n_even, B_WIDTH)),
        'even_w_out': normal((n_even, A_WIDTH + B_WIDTH, D_MODEL), (A_WIDTH + B_WIDTH) ** -0.5),
        'odd_w_in': normal((n_odd, D_MODEL, odd_in), D_MODEL ** -0.5),
        'lambda_q1': normal((n_odd, HEAD_DIM), 0.1),
        'lambda_k1': normal((n_odd, HEAD_DIM), 0.1),
        'lambda_q2': normal((n_odd, HEAD_DIM), 0.1),
        'lambda_k2': normal((n_odd, HEAD_DIM), 0.1),
        'subln_gain': gain((n_odd, 2 * HEAD_DIM)),
        'odd_w_out': normal((n_odd, C_WIDTH + D_OUT_WIDTH, D_MODEL), (C_WIDTH + D_OUT_WIDTH) ** -0.5),
        'ffn_w_up': normal((DEPTH, D_MODEL, 2 * D_FF), D_MODEL ** -0.5),
        'ffn_conv_w': normal((DEPTH, CONV_WIDTH, 1, D_FF), CONV_WIDTH ** -0.5),
        'ffn_conv_b': normal((DEPTH, D_FF), 0.02),
        'ffn_w_down': normal((DEPTH, D_FF, D_MODEL), D_FF ** -0.5),
    }


def reference(x, positions, norm_mix, norm_ffn, final_norm, even_w_in, gmlp_v_gain, gmlp_w_s, gmlp_b_s,
              pool_w, pool_scale, even_w_out, odd_w_in, lambda_q1, lambda_k1, lambda_q2, lambda_k2,
              subln_gain, odd_w_out, ffn_w_up, ffn_conv_w, ffn_conv_b, ffn_w_down):
    cos, sin = rope_tables(positions)
    h = x
    for layer in range(DEPTH):
        i = layer // 2
        hn = rms_norm(h, norm_mix[layer])
        if layer % 2 == 0:
            mix = even_mixer(hn, even_w_in[i], gmlp_v_gain[i], gmlp_w_s[i], gmlp_b_s[i],
                             pool_w[i], pool_scale[i], even_w_out[i])
        else:
            mix = odd_mixer(hn, cos, sin, odd_w_in[i], lambda_q1[i], lambda_k1[i], lambda_q2[i],
                            lambda_k2[i], subln_gain[i], odd_w_out[i], layer)
        h = h + mix
        h = h + conv_ffn(rms_norm(h, norm_ffn[layer]), ffn_w_up[layer], ffn_conv_w[layer],
                         ffn_conv_b[layer], ffn_w_down[layer])
    return rms_norm(h, final_norm)
```

```python
import numpy as np
import concourse.bass as bass
import concourse.mybir as mybir

F32 = mybir.dt.float32
BF16 = mybir.dt.bfloat16
I32 = mybir.dt.int32
AF = mybir.ActivationFunctionType
ALU = mybir.AluOpType
AX = mybir.AxisListType

ENGS = ("pe", "act", "dve", "pool", "sp")
SEM_CHUNK = 20000
SBUF_LO = 16512
SBUF_HI = 229344


class Op:
    __slots__ = ("eng", "fn", "deps", "idx", "dma", "semkey", "need_inc", "semval", "semid")

    def __init__(self, eng, fn, dma, semkey):
        self.eng = eng
        self.fn = fn
        self.deps = []
        self.dma = dma
        self.semkey = semkey
        self.need_inc = False
        self.semval = None
        self.semid = None


class Sched:
    def __init__(self, nc, same_engine_sync=True):
        self.nc = nc
        self.ops = {e: [] for e in ENGS}
        self.res = {}
        self.banks = {}
        self.persist_keys = set()
        self.same = same_engine_sync
        self.pending_barrier = {e: [] for e in ENGS}
        self.all_dma_since_barrier = []
        self.sb_off = SBUF_LO
        self.sb_stack = []
        self.nalloc = 0

    def push(self):
        self.sb_stack.append(self.sb_off)

    def pop(self):
        self.sb_off = self.sb_stack.pop()

    def sb(self, name, shape, dtype):
        esz = 4 if dtype in (F32, I32) else 2
        n = 1
        for s in shape[1:]:
            n *= s
        nbytes = (n * esz + 63) // 64 * 64
        off = self.sb_off
        assert off + nbytes <= SBUF_HI, f"SBUF overflow allocating {name}: {off}+{nbytes}"
        self.sb_off = off + nbytes
        self.nalloc += 1
        return self.nc.alloc_sbuf_tensor_at(f"{name}_{self.nalloc}", list(shape), dtype, offset=off)

    @staticmethod
    def _bank(key):
        if isinstance(key, tuple) and isinstance(key[0], str) and key[0].startswith("ps"):
            if key[0] in ("ps", "psA", "psG"):
                return key[1]
            if key[0] == "psb":
                return 6 + key[1]
            return 6
        return None

    def op(self, eng, fn, reads=(), writes=(), dma=False, semkey=None, persist=False):
        o = Op(eng, fn, dma, semkey)
        if dma:
            assert semkey is not None
        deps = []
        banks = set()
        for k in list(reads) + list(writes):
            b = self._bank(k)
            if b is not None:
                banks.add(b)
        reads = [k for k in reads if self._bank(k) is None]
        writes = [k for k in writes if self._bank(k) is None]
        for b in banks:
            st = self.banks.setdefault(b, {})
            for e2, o2 in st.items():
                if e2 != eng:
                    deps.append(o2)
            st[eng] = o
        for r in reads:
            st = self.res.get(r)
            if st is not None and st[0] is not None:
                deps.append(st[0])
        for w in writes:
            st = self.res.get(w)
            if st is not None:
                if st[0] is not None:
                    deps.append(st[0])
                deps.extend(st[1])
        deps.extend(self.pending_barrier[eng])
        self.pending_barrier[eng] = []
        for r in reads:
            st = self.res.setdefault(r, [None, []])
            st[1].append(o)
        for w in writes:
            self.res[w] = [o, []]
        o.deps = deps
        o.idx = len(self.ops[eng])
        self.ops[eng].append(o)
        if dma and not persist:
            self.all_dma_since_barrier.append(o)
        if persist:
            self.persist_keys.update(writes)
        return o

    def barrier(self):
        lasts = []
        for e in ENGS:
            if self.ops[e]:
                lasts.append(self.ops[e][-1])
        lasts.extend(self.all_dma_since_barrier)
        self.all_dma_since_barrier = []
        for e in ENGS:
            self.pending_barrier[e] = list(lasts)
        self.res = {k: [v[0], []] for k, v in self.res.items() if k in self.persist_keys}
        self.banks = {}

    def emit(self, final_wait_ops=()):
        nc = self.nc
        for e in ENGS:
            for o in self.ops[e]:
                for d in o.deps:
                    if d.dma:
                        d.need_inc = True
                    elif d.eng != o.eng or (self.same and o.eng != "pe"):
                        d.need_inc = True
        for o in final_wait_ops:
            o.need_inc = True
        import contextlib
        stack = contextlib.ExitStack()
        sem_objs = {}

        def get_sem(key):
            if key not in sem_objs:
                sem_objs[key] = stack.enter_context(nc.semaphore(f"s{len(sem_objs)}"))
            return sem_objs[key]

        dma_counts = {}
        for e in ENGS:
            cnt = 0
            for o in self.ops[e]:
                if o.dma:
                    c = dma_counts.get(o.semkey, 0) + 16
                    dma_counts[o.semkey] = c
                    o.semid = ("dma", o.semkey)
                    o.semval = c
                    assert c < 60000, f"dma sem overflow {o.semkey}"
                elif o.need_inc:
                    o.semid = ("eng", e, cnt // SEM_CHUNK)
                    o.semval = cnt % SEM_CHUNK + 1
                    cnt += 1
        for e in ENGS:
            for o in self.ops[e]:
                if o.semid is not None and (o.dma or o.need_inc):
                    get_sem(o.semid)
        self.nsems = len(sem_objs)
        engmap = {"pe": "tensor", "act": "scalar", "dve": "vector", "pool": "gpsimd", "sp": "sync"}
        with stack:
            with nc.Block() as block:
                def make(e):
                    def body(eng):
                        waited = {}
                        for o in self.ops[e]:
                            need = {}
                            for d in o.deps:
                                if not d.dma and d.eng == e and (not self.same or e == "pe"):
                                    continue
                                sid = d.semid
                                if sid is None:
                                    continue
                                if d.semval > need.get(sid, 0):
                                    need[sid] = d.semval
                            for sid, v in need.items():
                                if waited.get(sid, 0) >= v:
                                    continue
                                eng.wait_ge(get_sem(sid), v)
                                waited[sid] = v
                            ins = o.fn(eng)
                            if o.dma:
                                ins.then_inc(get_sem(o.semid), 16)
                            elif o.need_inc:
                                ins.then_inc(get_sem(o.semid), 1)
                        if e == "sp":
                            for key, c in dma_counts.items():
                                eng.wait_ge(get_sem(("dma", key)), c)
                    return body
                for e in ENGS:
                    if self.ops[e] or e == "sp":
                        getattr(block, engmap[e])(make(e))
        return nc

from concourse.bass_utils import run_bass_kernel_spmd

D = 1024
TT = 256
NLOC = 4096
OWN0 = 1792
NOWN = NLOC - OWN0
DFF = 2816
NFC = 22
LAMBDA_INIT = 0.8 - 0.6 * float(np.exp(-0.3 * 1))
C_ID, C_L, C_U, C_VIS, C_CTXB, C_RCC, C_RCO, C_INV, C_N = 0, 128, 256, 384, 385, 386, 450, 514, 528


class KB:
    def __init__(self, debug=False, phases="ABCDEF"):
        self.phases = phases
        self.nc = nc = bass.Bass("TRN2", target_bir_lowering=False)
        self.S = Sched(nc)
        self.debug = debug
        self.ps = [nc.alloc_psum_tensor(f"ps{i}", [128, 512], F32) for i in range(8)]
        self.psbv = [self.ps[6][:].bitcast(BF16), self.ps[7][:].bitcast(BF16)]

    def MM(self, out, lhsT, rhs, start=True, stop=True, r=(), w=(), maxn=None):
        n = rhs.shape[-1]
        if maxn is not None and n > maxn:
            npc = -(-n // maxn)
            step = -(-n // npc)
            o = None
            for a in range(0, n, step):
                bnd = min(n, a + step)
                o_, r_ = out[:, a:bnd], rhs[:, a:bnd]
                st_ = start and a == 0
                o = self.S.op("pe", lambda e, o_=o_, r_=r_, st_=st_: e.matmul(o_, lhsT=lhsT, rhs=r_, start=st_, stop=stop,
                                                                         skip_group_check=True), r, w)
            return o
        return self.S.op("pe", lambda e: e.matmul(out, lhsT=lhsT, rhs=rhs, start=start, stop=stop), r, w)

    def TR(self, out, in_, r=(), w=()):
        idb = self.idb
        return self.S.op("pe", lambda e: e.transpose(out=out, in_=in_, identity=idb[:]), list(r) + ["idb"], w)

    def ACT(self, out, in_, func, r=(), w=(), bias=None, scale=None, accum=None):
        kw = {}
        if bias is not None:
            kw["bias"] = bias
        if scale is not None:
            kw["scale"] = scale
        if accum is not None:
            kw["accum_out"] = accum
        return self.S.op("act", lambda e: e.activation(out=out, in_=in_, func=func, **kw), r, w)

    def TS(self, eng, out, in0, s1, s2, op0, op1=None, r=(), w=()):
        if op1 is None:
            return self.S.op(eng, lambda e: e.tensor_scalar(out=out, in0=in0, scalar1=s1, scalar2=None, op0=op0), r, w)
        return self.S.op(eng, lambda e: e.tensor_scalar(out=out, in0=in0, scalar1=s1, scalar2=s2, op0=op0, op1=op1), r, w)

    def TTo(self, eng, out, in0, in1, op, r=(), w=()):
        return self.S.op(eng, lambda e: e.tensor_tensor(out=out, in0=in0, in1=in1, op=op), r, w)

    def STT(self, eng, out, in0, scalar, in1, op0, op1, r=(), w=()):
        return self.S.op(eng, lambda e: e.scalar_tensor_tensor(out=out, in0=in0, scalar=scalar, in1=in1, op0=op0, op1=op1), r, w)

    def CP(self, eng, out, in_, r=(), w=()):
        if eng == "act":
            return self.S.op(eng, lambda e: e.copy(out=out, in_=in_), r, w)
        return self.S.op(eng, lambda e: e.tensor_copy(out=out, in_=in_), r, w)

    def RECIP(self, out, in_, r=(), w=()):
        return self.S.op("dve", lambda e: e.reciprocal(out=out, in_=in_), r, w)

    def MEMSET(self, eng, ap, val, w=()):
        return self.S.op(eng, lambda e: e.memset(ap, val), (), w)

    def DMA(self, eng, out, in_, key, r=(), w=(), slow=False, persist=False):
        if slow:
            return self.S.op(eng, lambda e: e.dma_start(out=out, in_=in_, allow_slow_non_contiguous=True), r, w, dma=True, semkey=key)
        return self.S.op(eng, lambda e: e.dma_start(out=out, in_=in_), r, w, dma=True, semkey=key, persist=persist)

    def col_load(self, dst, src_row, key):
        self.DMA("sp", dst, src_row.rearrange("o (c p) -> p (o c)", p=128), key, w=[key], slow=True)

    def rmsnorm_T(self, xt, j_n, hn, hnT, gcol, tag, slot, ps_half):
        S = self.S
        ss, rstd, junk = self.ss, self.rstd, self.junk
        xr = (tag, "x", slot)
        for j in range(j_n):
            self.ACT(junk[:], xt[:, j, :], AF.Square, r=[xr], w=["junk", ("ss", j)], accum=ss[:, j:j + 1])
        self.ACT(rstd[:, 0:j_n], ss[:, 0:j_n], AF.Sqrt, r=[("ss", j) for j in range(j_n)] + ["eps"], w=["rstd"],
                 bias=self.epsb[:], scale=1.0 / D)
        self.RECIP(rstd[:, 0:j_n], rstd[:, 0:j_n], r=["rstd"], w=["rstd"])
        for j in range(j_n):
            self.TS("dve", hn[:, j, :], xt[:, j, :], rstd[:, j:j + 1], None, ALU.mult, r=[xr, "rstd"], w=[("hn", j)])
        for kc in range(8):
            h = (kc + ps_half) % 2
            psb = self.psbv[h]
            for j in range(j_n):
                self.TR(psb[:, j * 128:(j + 1) * 128], hn[:, j, kc * 128:(kc + 1) * 128],
                        r=[("hn", j)], w=[("psb", h)])
            rr = [("psb", h), gcol[1]]
            if kc % 2 == 0:
                self.ACT(hnT[:, kc, 0:j_n * 128], psb[:, 0:j_n * 128], AF.Copy, r=rr,
                         w=[(tag, "hnT", slot, kc)], scale=gcol[0][:, kc:kc + 1])
            else:
                self.TS("dve", hnT[:, kc, 0:j_n * 128], psb[:, 0:j_n * 128], gcol[0][:, kc:kc + 1], None,
                        ALU.mult, r=rr, w=[(tag, "hnT", slot, kc)])

    def build(self):
        nc, S = self.nc, self.S
        din = lambda name, shape, dt=F32: nc.dram_tensor(name, list(shape), dt, kind="ExternalInput").ap()
        scr = lambda name, shape, dt: nc.dram_tensor(name, list(shape), dt, kind="Internal").ap()
        I = self.I = dict(
            xin=din("xin", [NLOC, D]), pos=din("pos", [1, NLOC], I32), cst=din("cst", [128, C_N]),
            norm_mix=din("norm_mix", [2, D]), norm_ffn=din("norm_ffn", [2, D]), final_norm=din("final_norm", [1, D]),
            even_w_in=din("even_w_in", [D, 1536]), gmlp_v_gain=din("gmlp_v_gain", [1, 512]),
            gmlp_w_s=din("gmlp_w_s", [4, 128, 128]), gmlp_b_s=din("gmlp_b_s", [1, 512]),
            pool_w=din("pool_w", [4, 128, 128]), pool_scale=din("pool_scale", [1, 512]),
            even_w_out=din("even_w_out", [D, D]), odd_w_in=din("odd_w_in", [D, 4608]),
            lq1=din("lq1", [1, 64]), lk1=din("lk1", [1, 64]), lq2=din("lq2", [1, 64]), lk2=din("lk2", [1, 64]),
            subln=din("subln", [1, 128]), odd_w_out=din("odd_w_out", [D, D]),
            ffn_w_up=din("ffn_w_up", [2, D, 2 * DFF]), ffn_conv_w=din("ffn_conv_w", [2, 3, DFF]),
            ffn_conv_b=din("ffn_conv_b", [2, DFF]), ffn_w_down=din("ffn_w_down", [2, DFF, D]),
        )
        self.out = nc.dram_tensor("out", [2048, D], F32, kind="ExternalOutput").ap()
        dbgk = "ExternalOutput" if self.debug else "Internal"
        self.h1s = nc.dram_tensor("h1s", [NLOC, D], F32, kind=dbgk).ap()
        self.hL0 = nc.dram_tensor("hL0", [NLOC, D], F32, kind=dbgk).ap()
        self.KTc = scr("KTc", [768, NLOC], BF16)
        self.QTc = scr("QTc", [768, NOWN], BF16)
        self.Vc = scr("Vc", [NLOC, 768], BF16)
        self.KTd = scr("KTd", [3, 256, NLOC], BF16)
        self.QTd = scr("QTd", [3, 256, NLOC], BF16)
        self.Vd = scr("Vd", [3, NLOC, 256], BF16)
        self.Y1T = nc.dram_tensor("Y1T", [D, NOWN], BF16, kind=dbgk).ap()

        C = self.C = S.sb("cst", [128, C_N], F32)
        self.DMA("sp", C[:], I["cst"], "cst", w=["cst"])
        self.idb = S.sb("idb", [128, 128], BF16)
        self.Lb = S.sb("Lb", [128, 128], BF16)
        self.Ub = S.sb("Ub", [128, 128], BF16)
        self.Lvb = S.sb("Lvb", [128, 128], BF16)
        self.Uvb = S.sb("Uvb", [128, 128], BF16)
        self.onesb = S.sb("onesb", [128, 128], BF16)
        self.epsb = S.sb("epsb", [128, 1], F32)
        self.zerob = S.sb("zerob", [128, 1], F32)
        self.ss = S.sb("ss", [128, 4], F32)
        self.rstd = S.sb("rstd", [128, 4], F32)
        self.junk = S.sb("junk", [128, 1024], BF16)
        self.CP("dve", self.idb[:], C[:, C_ID:C_ID + 128], r=["cst"], w=["idb"])
        self.CP("dve", self.Lb[:], C[:, C_L:C_L + 128], r=["cst"], w=["Lb"])
        self.CP("dve", self.Ub[:], C[:, C_U:C_U + 128], r=["cst"], w=["Ub"])
        self.TS("dve", self.Lvb[:], C[:, C_L:C_L + 128], C[:, C_VIS:C_VIS + 1], None, ALU.mult, r=["cst"], w=["Lvb"])
        self.TS("dve", self.Uvb[:], C[:, C_U:C_U + 128], C[:, C_VIS:C_VIS + 1], None, ALU.mult, r=["cst"], w=["Uvb"])
        self.MEMSET("pool", self.onesb[:], 1.0, w=["onesb"])
        self.MEMSET("pool", self.epsb[:], 1e-6, w=["eps"])
        self.MEMSET("pool", self.zerob[:], 0.0, w=["zerob"])
        S.barrier()

        ph = self.phases
        off0 = S.sb_off
        wup0 = self.prefetch_wup(0) if "B" in ph else None
        if "A" in ph:
            self.phase_A()
            S.barrier()
        if "B" in ph:
            self.phase_ffn(0, wup0)
            S.barrier()
        S.sb_off = off0
        if "C" in ph:
            self.phase_C()
            S.barrier()
        wup1 = self.prefetch_wup(1) if "F" in ph else None
        if "D" in ph:
            self.phase_D()
            S.barrier()
        if "E" in ph:
            self.phase_E()
            S.barrier()
        if "F" in ph:
            self.phase_ffn(1, wup1)
        S.emit()
        return nc

    def phase_A(self):
        nc, S, I, C, ps = self.nc, self.S, self.I, self.C, self.ps
        psb = self.psbv[0]
        S.push()
        win = S.sb("winA", [128, 8, 1536], BF16)
        wout = S.sb("woutA", [128, 8, 1024], BF16)
        for c in range(3):
            self.DMA("pool", win[:, :, c * 512:(c + 1) * 512],
                     I["even_w_in"].rearrange("(kc p) n -> p kc n", p=128)[:, :, c * 512:(c + 1) * 512],
                     ("winA", c), w=[("winA", c)])
        for c in range(2):
            self.DMA("pool", wout[:, :, c * 512:(c + 1) * 512],
                     I["even_w_out"].rearrange("(kc p) n -> p kc n", p=128)[:, :, c * 512:(c + 1) * 512],
                     ("woutA", c), w=[("woutA", c)])
        WIN = [("winA", c) for c in range(3)]
        WOUT = [("woutA", c) for c in range(2)]
        poolw = S.sb("poolw", [128, 4, 128], BF16)
        self.DMA("pool", poolw[:], I["pool_w"].rearrange("g c d -> c g d"), "poolw", w=["poolw"])
        bsrow = S.sb("bsrow", [1, 512], BF16)
        self.DMA("pool", bsrow[:], I["gmlp_b_s"], "bsrow", w=["bsrow"])
        wsf = S.sb("wsf", [128, 4, 128], F32)
        self.DMA("sp", wsf[:], I["gmlp_w_s"].rearrange("g t s -> t g s"), "wsf", w=["wsf"])
        wsb = S.sb("wsb", [128, 4, 128], BF16)
        WmT = S.sb("WmT", [128, 4, 128], BF16)
        for g in range(4):
            self.TTo("dve", wsb[:, g, :], wsf[:, g, :], C[:, C_L:C_L + 128], ALU.mult, r=["wsf"], w=[("wsb", g)])
            self.TR(psb[:, g * 128:(g + 1) * 128], wsb[:, g, :], r=[("wsb", g)], w=[("psbw", g)])
        self.CP("dve", WmT[:].rearrange("p g t -> p (g t)"), psb[:, 0:512], r=[("psbw", g) for g in range(4)], w=["WmT"])
        vgbc = S.sb("vgbc", [128, 512], F32)
        self.DMA("sp", vgbc[:], I["gmlp_v_gain"].partition_broadcast(128), "vgbc", w=["vgbc"])
        pscol = S.sb("pscol", [128, 4], F32)
        self.col_load(pscol[:], I["pool_scale"], "pscol")
        gcol = S.sb("gcolA", [128, 8], F32)
        self.col_load(gcol[:], I["norm_mix"][0:1, :], "gcolA")
        S.barrier()

        xt = [S.sb(f"xtA{i}", [128, 2, D], F32) for i in range(2)]
        hn = S.sb("hnA", [128, 2, D], BF16)
        hnT = [S.sb(f"hnTA{i}", [128, 8, TT], BF16) for i in range(2)]
        uT = S.sb("uT", [128, 4, TT], BF16)
        pbuf = [S.sb(f"pbuf{i}", [128, 4, 16 + TT], F32) for i in range(2)]
        tmpA = S.sb("tmpA", [128, 16 + TT], F32)
        tmpB = S.sb("tmpB", [128, 16 + TT], F32)
        tmpf = S.sb("tmpf", [128, 16], F32)
        pooled = S.sb("pooled", [128, 4, TT], BF16)
        vg = S.sb("vg", [128, 512], F32)
        vss = S.sb("vss", [128, 2], F32)
        vr = S.sb("vr", [128, 2], F32)
        vn = S.sb("vn", [128, 2, 512], BF16)
        yT = S.sb("yTA", [128, 8, TT], BF16)
        W = 16 + TT
        xin_v = I["xin"].rearrange("(t j p) d -> t p j d", p=128, j=2)
        h1s_v = self.h1s.rearrange("(t j p) d -> t p j d", p=128, j=2)
        NT = NLOC // TT
        self.MEMSET("pool", pbuf[1][:, :, TT:TT + 16], 0.0, w=[("pbuf", 1, g) for g in range(4)])
        self.DMA("sp", xt[0][:], xin_v[0], ("xtA", 0), w=[("A", "x", 0)])
        for t in range(NT):
            s = t % 2
            if t + 1 < NT:
                self.DMA("sp", xt[1 - s][:], xin_v[t + 1], ("xtA", 1 - s), w=[("A", "x", 1 - s)])
            if t == 1 and getattr(self, "wup_issue", None) is not None:
                self.wup_issue()
                self.wup_issue = None
            self.rmsnorm_T(xt[s], 2, hn, hnT[s], (gcol, "gcolA"), "A", s, 0)
            HT = [("A", "hnT", s, kc) for kc in range(8)]
            for fc in range(4):
                pt = ps[fc % 2]
                for kc in range(8):
                    self.MM(pt[:, 0:TT], win[:, kc, fc * 128:(fc + 1) * 128], hnT[s][:, kc, :], kc == 0, kc == 7,
                            r=WIN + HT, w=[("ps", fc % 2)])
                self.ACT(uT[:, fc, :], pt[:, 0:TT], AF.Gelu, r=[("ps", fc % 2)], w=[("uT", fc)])
            for g in range(4):
                pt = ps[2 + g % 2]
                for kc in range(8):
                    self.MM(pt[:, 0:TT], win[:, kc, 1024 + g * 128:1024 + (g + 1) * 128], hnT[s][:, kc, :], kc == 0, kc == 7,
                            r=WIN + HT, w=[("ps", 2 + g % 2)])
                self.ACT(pbuf[s][:, g, 16:W], pt[:, 0:TT], AF.Copy, r=[("ps", 2 + g % 2)], w=[("pbuf", s, g)])
                self.CP("pool", pbuf[s][:, g, 0:16], pbuf[1 - s][:, g, TT:W], r=[("pbuf", 1 - s, g)], w=[("pbufh", s, g)])
            for j in range(2):
                pt = ps[4 + j]
                for kc in range(8):
                    self.MM(pt[:, :], hnT[s][:, kc, j * 128:(j + 1) * 128], win[:, kc, 512:1024], kc == 0, kc == 7,
                            r=WIN + HT, w=[("ps", 4 + j)])
                self.ACT(vg[:], pt[:, :], AF.Gelu, r=[("ps", 4 + j)], w=["vg"])
                self.ACT(self.junk[:, 0:512], vg[:], AF.Square, r=["vg"], w=["junk", ("vss", j)], accum=vss[:, j:j + 1])
                self.ACT(vr[:, j:j + 1], vss[:, j:j + 1], AF.Sqrt, r=[("vss", j), "eps"], w=[("vr", j)], bias=self.epsb[:], scale=1.0 / 512)
                self.RECIP(vr[:, j:j + 1], vr[:, j:j + 1], r=[("vr", j)], w=[("vr", j)])
                self.STT("dve", vn[:, j, :], vg[:], vr[:, j:j + 1], vgbc[:], ALU.mult, ALU.mult, r=["vg", ("vr", j), "vgbc"], w=[("vn", j)])
            for g in range(4):
                pt = ps[g % 2]
                for j in range(2):
                    self.MM(pt[:, j * 128:(j + 1) * 128], vn[:, j, g * 128:(g + 1) * 128], WmT[:, g, :], True, False,
                            r=[("vn", j), "WmT"], w=[("ps", g % 2)])
                    self.MM(pt[:, j * 128:(j + 1) * 128], self.onesb[0:1, :], bsrow[0:1, g * 128:(g + 1) * 128], False, True,
                            r=["onesb", "bsrow"], w=[("ps", g % 2)])
                self.TTo("dve", yT[:, g, :], pt[:, 0:TT], uT[:, g, :], ALU.mult, r=[("ps", g % 2), ("uT", g)], w=[("yT", g)])
            for g in range(4):
                cur = pbuf[s][:, g, :]
                rr = [("pbuf", s, g), ("pbufh", s, g)]
                src, lag = cur, 1
                bufs = [tmpA, tmpB]
                for step in range(g + 1):
                    dst = bufs[step % 2]
                    lo = 2 * lag - 1
                    srcr = rr if step == 0 else [("ptmp", (step - 1) % 2)]
                    self.TTo("pool", dst[:, lo:W], src[:, lo:W], src[:, lo - lag:W - lag], ALU.add, r=srcr, w=[("ptmp", step % 2)])
                    src = dst
                    lag *= 2
                wdw = 2 ** (g + 1)
                last = [("ptmp", g % 2)]
                self.STT("dve", pooled[:, g, :], src[:, 16:W], 1.0 / wdw, cur[:, 16:W], ALU.mult, ALU.subtract, r=last + rr, w=[("pooled", g)])
                if t == 0 or t == 2048 // TT:
                    off = C_RCC if t == 0 else C_RCO
                    self.TTo("dve", tmpf[:], src[:, 16:32], C[:, off + g * 16: off + (g + 1) * 16], ALU.mult, r=last, w=["tmpf"])
                    self.TTo("dve", pooled[:, g, 0:16], tmpf[:], cur[:, 16:32], ALU.subtract, r=["tmpf"] + rr, w=[("pooled", g)])
                pt = ps[2 + g % 2]
                self.MM(pt[:, 0:TT], poolw[:, g, :], pooled[:, g, :], True, True, r=["poolw", ("pooled", g)], w=[("ps", 2 + g % 2)])
                self.ACT(yT[:, 4 + g, :], pt[:, 0:TT], AF.Copy, r=[("ps", 2 + g % 2), "pscol"], w=[("yT", 4 + g)], scale=pscol[:, g:g + 1])
            YT = [("yT", k) for k in range(8)]
            for j in range(2):
                for nh in range(2):
                    pt = ps[4 + (j * 2 + nh) % 2]
                    for kc in range(8):
                        self.MM(pt[:, :], yT[:, kc, j * 128:(j + 1) * 128], wout[:, kc, nh * 512:(nh + 1) * 512], kc == 0, kc == 7,
                                r=YT + WOUT, w=[("ps", 4 + (j * 2 + nh) % 2)])
                    self.TTo("dve", xt[s][:, j, nh * 512:(nh + 1) * 512], xt[s][:, j, nh * 512:(nh + 1) * 512], pt[:, :], ALU.add,
                             r=[("ps", 4 + (j * 2 + nh) % 2), ("A", "x", s)], w=[("A", "x", s)])
            self.DMA("sp", h1s_v[t], xt[s][:], ("h1st", s), r=[("A", "x", s)], w=[("h1s", t)])
        S.pop()

    def prefetch_wup(self, l):
        S, I = self.S, self.I
        wup = S.sb(f"wup{l}", [128, 8, 2 * DFF], BF16)
        wup_src = I["ffn_w_up"][l].rearrange("(kc p) n -> p kc n", p=128)

        def issue():
            for c in [0, 5, 6, 1, 7, 2, 8, 3, 9, 4, 10]:
                self.DMA("pool", wup[:, :, c * 512:(c + 1) * 512], wup_src[:, :, c * 512:(c + 1) * 512], ("wup", l, c), w=[("wup", c)], persist=True)
        self.wup_issue = issue
        return wup

    def phase_ffn(self, l, wup):
        nc, S, I, C, ps = self.nc, self.S, self.I, self.C, self.ps
        psb = self.psbv[0]
        if getattr(self, "wup_issue", None) is not None:
            self.wup_issue()
            self.wup_issue = None
        S.push()
        wdn = S.sb("wdn", [128, NFC, D], BF16)
        wdn_src = I["ffn_w_down"][l].rearrange("(fc p) n -> p fc n", p=128)
        for c in range(2):
            self.DMA("pool", wdn[:, c * 11:(c + 1) * 11, :], wdn_src[:, c * 11:(c + 1) * 11, :], ("wdn", c), w=[("wdn", c)])
        cw = S.sb("cw", [128, 3, NFC], F32)
        self.DMA("sp", cw[:], I["ffn_conv_w"][l].rearrange("j (fc p) -> p j fc", p=128), "cw", w=["cw"], slow=True)
        cb = S.sb("cb", [128, NFC], F32)
        self.col_load(cb[:], I["ffn_conv_b"][l:l + 1, :], "cb")
        gcol = S.sb("gcolF", [128, 8], F32)
        self.col_load(gcol[:], I["norm_ffn"][l:l + 1, :], "gcolF")
        ahalo = S.sb("ahalo", [128, NFC, 2], F32)
        self.MEMSET("pool", ahalo[:], 0.0, w=[("ahalo", fc) for fc in range(NFC)])
        if l == 1:
            wout = S.sb("woutO", [128, 8, D], BF16)
            for c in range(2):
                self.DMA("pool", wout[:, :, c * 512:(c + 1) * 512],
                         I["odd_w_out"].rearrange("(kc p) n -> p kc n", p=128)[:, :, c * 512:(c + 1) * 512],
                         ("woutO", c), w=[("woutO", c)])
            WOUT = [("woutO", c) for c in range(2)]
            fnbc = S.sb("fnbc", [128, D], F32)
            self.DMA("sp", fnbc[:], I["final_norm"].partition_broadcast(128), "fnbc", w=["fnbc"])
            yt = [S.sb(f"ytF{i}", [128, 8, TT], BF16) for i in range(2)]
            outt = S.sb("outt", [128, D], F32)
            fss = S.sb("fss", [128, 2], F32)
            frs = S.sb("frs", [128, 2], F32)
        ht = [S.sb(f"htF{i}", [128, 2, D], F32) for i in range(2)]
        hn = S.sb("hnF", [128, 2, D], BF16)
        hnT = [S.sb(f"hnTF{i}", [128, 8, TT], BF16) for i in range(2)]
        ND = 3
        QB = (0, 1, 6)
        abuf = [S.sb(f"abuf{i}", [128, TT + 2], F32) for i in range(ND)]
        t1 = [S.sb(f"t1{i}", [128, TT], F32) for i in range(ND)]
        mm_ = [S.sb(f"m{i}", [128, TT], BF16) for i in range(ND)]
        if l == 0:
            NT = NLOC // TT
            src_v = self.h1s.rearrange("(t j p) d -> t p j d", p=128, j=2)
            dst_v = self.hL0.rearrange("(t j p) d -> t p j d", p=128, j=2)
            src_res = lambda t: [("h1s", t)]
        else:
            NT = NOWN // TT
            src_v = self.hL0[OWN0:NLOC, :].rearrange("(t j p) d -> t p j d", p=128, j=2)
            dst_v = self.out.rearrange("(t j p) d -> t p j d", p=128, j=2)
            src_res = lambda t: []
            y_v = self.Y1T.rearrange("(kc p) n -> p kc n", p=128)
        self.final_stores = []

        def load(t):
            s = t % 2
            self.DMA("sp", ht[s][:], src_v[t], ("htF", s), r=src_res(t), w=[("F", "x", s)])
            if l == 1:
                self.DMA("sp", yt[s][:], y_v[:, :, t * TT:(t + 1) * TT], ("ytF", s), w=[("ytF", s)])
        load(0)
        for t in range(NT):
            s = t % 2
            if t + 1 < NT:
                load(t + 1)
            if l == 1:
                for j in range(2):
                    for nh in range(2):
                        k = 2 + j * 2 + nh
                        for kc in range(8):
                            self.MM(ps[k][:, :], yt[s][:, kc, j * 128:(j + 1) * 128], wout[:, kc, nh * 512:(nh + 1) * 512], kc == 0, kc == 7,
                                    r=[("ytF", s)] + WOUT, w=[("ps", k)])
                        self.TTo("dve", ht[s][:, j, nh * 512:(nh + 1) * 512], ht[s][:, j, nh * 512:(nh + 1) * 512], ps[k][:, :], ALU.add,
                                 r=[("ps", k), ("F", "x", s)], w=[("F", "x", s)])
            self.rmsnorm_T(ht[s], 2, hn, hnT[s], (gcol, "gcolF"), "F", s, 0)
            HT = [("F", "hnT", s, kc) for kc in range(8)]
            def down(fc):
                q = fc % ND
                for j in range(2):
                    for nh in range(2):
                        k = 2 + j * 2 + nh
                        self.MM(ps[k][:, :], mm_[q][:, j * 128:(j + 1) * 128], wdn[:, fc, nh * 512:(nh + 1) * 512], fc == 0, fc == NFC - 1,
                                r=[("m", q), ("wdn", fc // 11)], w=[("ps", k)])
            for fc in range(NFC):
                q = fc % ND
                qb = QB[q]
                pq = ps[qb]
                ca = (fc * 128) // 512
                cg = (DFF + fc * 128) // 512
                for kc in range(8):
                    self.MM(pq[:, 0:TT], wup[:, kc, fc * 128:(fc + 1) * 128], hnT[s][:, kc, :], kc == 0, kc == 7,
                            r=[("wup", ca)] + HT, w=[("psA", qb)])
                for kc in range(8):
                    self.MM(pq[:, TT:2 * TT], wup[:, kc, DFF + fc * 128:DFF + (fc + 1) * 128], hnT[s][:, kc, :], kc == 0, kc == 7,
                            r=[("wup", cg)] + HT, w=[("psG", qb)])
                if fc >= 2:
                    down(fc - 2)
                self.CP("pool", abuf[q][:, 0:2], ahalo[:, fc, :], r=[("ahalo", fc)], w=[("abufh", q)])
                self.ACT(abuf[q][:, 2:TT + 2], pq[:, 0:TT], AF.Copy, r=[("psA", qb)], w=[("abuf", q)])
                self.ACT(t1[q][:], pq[:, 0:TT], AF.Identity, r=[("psA", qb), "cw", "cb"], w=[("t1", q)],
                         bias=cb[:, fc:fc + 1], scale=cw[:, 2, fc:fc + 1])
                self.STT("dve", t1[q][:], abuf[q][:, 1:TT + 1], cw[:, 1, fc:fc + 1], t1[q][:], ALU.mult, ALU.add,
                         r=[("abuf", q), ("abufh", q), ("t1", q)], w=[("t1", q)])
                self.STT("dve", t1[q][:], abuf[q][:, 0:TT], cw[:, 0, fc:fc + 1], t1[q][:], ALU.mult, ALU.add,
                         r=[("abuf", q), ("abufh", q), ("t1", q)], w=[("t1", q)])
                self.CP("pool", ahalo[:, fc, :], abuf[q][:, TT:TT + 2], r=[("abuf", q)], w=[("ahalo", fc)])
                self.ACT(t1[q][:], t1[q][:], AF.Gelu, r=[("t1", q)], w=[("t1", q)])
                self.TTo("dve", mm_[q][:], t1[q][:], pq[:, TT:2 * TT], ALU.mult, r=[("t1", q), ("psG", qb)], w=[("m", q)])
            down(NFC - 2)
            down(NFC - 1)
            for j in range(2):
                for nh in range(2):
                    k = 2 + j * 2 + nh
                    self.TTo("dve", ht[s][:, j, nh * 512:(nh + 1) * 512], ht[s][:, j, nh * 512:(nh + 1) * 512], ps[k][:, :], ALU.add,
                             r=[("ps", k), ("F", "x", s)], w=[("F", "x", s)])
            if l == 0:
                self.DMA("sp", dst_v[t], ht[s][:], ("hL0st", s), r=[("F", "x", s)], w=[("hL0", t)])
            elif t >= 1:
                for j in range(2):
                    self.ACT(self.junk[:], ht[s][:, j, :], AF.Square, r=[("F", "x", s)], w=["junk", ("fss", j)], accum=fss[:, j:j + 1])
                self.ACT(frs[:], fss[:], AF.Sqrt, r=[("fss", 0), ("fss", 1), "eps"], w=["frs"], bias=self.epsb[:], scale=1.0 / D)
                self.RECIP(frs[:], frs[:], r=["frs"], w=["frs"])
                for j in range(2):
                    self.STT("dve", outt[:], ht[s][:, j, :], frs[:, j:j + 1], fnbc[:], ALU.mult, ALU.mult,
                             r=[("F", "x", s), "frs", "fnbc"], w=["outt"])
                    st = self.DMA("sp", dst_v[t - 1][:, j, :], outt[:], "outst", r=["outt"], w=[("out", t, j)])
                    self.final_stores.append(st)
        S.pop()

    def phase_C(self):
        nc, S, I, C, ps = self.nc, self.S, self.I, self.C, self.ps
        psb = self.psbv[0]
        S.push()
        hn1T = S.sb("hn1T", [128, 8, NLOC], BF16)
        winO = S.sb("winO", [128, 8, 4608], BF16)
        wsrc = I["odd_w_in"].rearrange("(kc p) n -> p kc n", p=128)
        for c in range(9):
            self.DMA("pool", winO[:, :, c * 512:(c + 1) * 512], wsrc[:, :, c * 512:(c + 1) * 512], ("winO", c), w=[("winO", c)])
        gcol = S.sb("gcolC", [128, 8], F32)
        self.col_load(gcol[:], I["norm_mix"][1:2, :], "gcolC")
        posi = S.sb("posi", [128, 3, 32], I32)
        pv = I["pos"]
        self.DMA("sp", posi[:, 0, :], pv.rearrange("o (n a) -> a (o n)", a=128), "posi0", w=[("posi", 0)], slow=True)
        self.DMA("sp", posi[:, 1, :].rearrange("p (r n) -> p r n", r=4), pv.rearrange("o (n a r) -> a (o r) n", a=128, r=4),
                 "posi1", w=[("posi", 1)], slow=True)
        self.DMA("sp", posi[:, 2, :].rearrange("p (r n) -> p r n", r=16), pv.rearrange("o (n a r) -> a (o r) n", a=128, r=16),
                 "posi2", w=[("posi", 2)], slow=True)
        posf = S.sb("posf", [128, 96], F32)
        self.CP("dve", posf[:], posi[:].rearrange("p o n -> p (o n)"), r=[("posi", i) for i in range(3)], w=["posf"])
        ang = S.sb("ang", [128, 96, 8], F32)
        ki = S.sb("ki", [128, 96, 8], I32)
        kf = S.sb("kf", [128, 96, 8], F32)
        gt = S.sb("gt", [128, 96, 8], F32)
        cs = S.sb("cs", [128, 96, 8], F32)
        sn = S.sb("sn", [128, 96, 8], F32)
        posb = bass.AP(posf, 0, [[96, 128], [1, 96], [0, 8]])
        invb = bass.AP(C, C_INV, [[C_N, 128], [0, 96], [1, 8]])
        self.TTo("dve", ang[:], posb, invb, ALU.mult, r=["posf", "cst"], w=["ang"])
        for tbl, shift, nm in ((sn, 0.0, "sn"), (cs, 0.25, "cs")):
            self.TS("dve", kf[:], ang[:], 1.0 / (2 * np.pi), shift, ALU.mult, ALU.add, r=["ang"], w=["kf"])
            self.CP("dve", ki[:], kf[:], r=["kf"], w=["ki"])
            self.CP("dve", gt[:], ki[:], r=["ki"], w=["gt"])
            self.TTo("dve", kf[:], kf[:], gt[:], ALU.subtract, r=["kf", "gt"], w=["kf"])
            self.TS("dve", gt[:], kf[:], 0.5, None, ALU.is_gt, r=["kf"], w=["gt"])
            self.TTo("dve", kf[:], kf[:], gt[:], ALU.subtract, r=["kf", "gt"], w=["kf"])
            self.TS("dve", gt[:], kf[:], -0.5, None, ALU.is_lt, r=["kf"], w=["gt"])
            self.TTo("dve", kf[:], kf[:], gt[:], ALU.add, r=["kf", "gt"], w=["kf"])
            self.ACT(tbl[:], kf[:], AF.Sin, r=["kf"], w=[nm], scale=6.28318)
        ht = [S.sb(f"htC{i}", [128, 2, D], F32) for i in range(2)]
        hn = S.sb("hnC", [128, 2, D], BF16)
        src_v = self.hL0.rearrange("(t j p) d -> t p j d", p=128, j=2)
        NT = NLOC // TT
        self.DMA("sp", ht[0][:], src_v[0], ("htC", 0), w=[("C", "x", 0)])
        for t in range(NT):
            s = t % 2
            if t + 1 < NT:
                self.DMA("sp", ht[1 - s][:], src_v[t + 1], ("htC", 1 - s), w=[("C", "x", 1 - s)])
            self.rmsnorm_T(ht[s], 2, hn, hn1T[:, :, t * TT:(t + 1) * TT], (gcol, "gcolC"), "C", s, 0)
        S.barrier()
        for i in range(32):
            self.MM(ps[i % 4][:, :], self.onesb[:], winO[:, i % 8, 0:512], True, True, r=["onesb", ("winO", 0)], w=[("ps", i % 4)])
        qk = [S.sb(f"qk{i}", [128, 384], BF16) for i in range(2)]
        rts = [S.sb(f"rt{i}", [128, 6, 8], F32) for i in range(4)]
        stg = [S.sb(f"stg{i}", [128, 3, 128], BF16) for i in range(4)]
        vst = [S.sb(f"vst{i}", [128, 384], BF16) for i in range(2)]
        cnt = dict(g=0, st=0, v=0)

        pend = []

        def flush():
            while pend:
                pend.pop(0)()

        def proj_group(lhs_fn, col0, ncols, kind, tbl, dst):
            g = cnt["g"]; cnt["g"] += 1
            pt = ps[g % 6]
            wkeys = [("winO", c) for c in range(col0 // 512, (col0 + ncols - 1) // 512 + 1)]
            for kc in range(8):
                self.MM(pt[:, 0:ncols], lhs_fn(kc), winO[:, kc, col0:col0 + ncols], kc == 0, kc == 7, r=wkeys, w=[("ps", g % 6)], maxn=256)
            flush()
            if kind == "v":
                vi = cnt["v"] % 2; cnt["v"] += 1
                self.ACT(vst[vi][:, 0:ncols], pt[:, 0:ncols], AF.Copy, r=[("ps", g % 6)], w=[("vst", vi)])
                self.DMA("sp", dst, vst[vi][:, 0:ncols], ("vst", vi), r=[("vst", vi)], w=[])
                return
            qi = g % 2
            nh = ncols // 64
            self.ACT(qk[qi][:, 0:ncols], pt[:, 0:ncols], AF.Copy, r=[("ps", g % 6)], w=[("qk", qi)])
            pv3 = pt[:, 0:ncols].rearrange("p (h e) -> p h e", e=64)
            qv3 = qk[qi][:, 0:ncols].rearrange("p (h e) -> p h e", e=64)
            x1, x2 = pv3[:, :, 0:8], pv3[:, :, 8:16]
            cb_ = bass.AP(cs, tbl * 8, [[768, 128], [0, nh], [1, 8]])
            sb_ = bass.AP(sn, tbl * 8, [[768, 128], [0, nh], [1, 8]])
            R = [rt[:, 0:nh, :] for rt in rts]
            self.TTo("dve", R[0], x1, cb_, ALU.mult, r=[("ps", g % 6), "cs"], w=["r0"])
            self.TTo("dve", R[1], x2, sb_, ALU.mult, r=[("ps", g % 6), "sn"], w=["r1"])
            self.TTo("dve", R[2], x2, cb_, ALU.mult, r=[("ps", g % 6), "cs"], w=["r2"])
            self.TTo("dve", R[3], x1, sb_, ALU.mult, r=[("ps", g % 6), "sn"], w=["r3"])
            self.TTo("dve", qv3[:, :, 0:8], R[0], R[1], ALU.subtract, r=["r0", "r1"], w=[("qk", qi)])
            self.TTo("dve", qv3[:, :, 8:16], R[2], R[3], ALU.add, r=["r2", "r3"], w=[("qk", qi)])
            nch = ncols // 128
            h = cnt["st"] % 2
            si = cnt["st"] % 4; cnt["st"] += 1

            def tail():
                pbh = self.psbv[h]
                for c in range(nch):
                    self.TR(pbh[:, c * 128:(c + 1) * 128], qk[qi][:, c * 128:(c + 1) * 128], r=[("qk", qi)], w=[("psb", h)])
                self.CP("act" if si % 2 else "dve", stg[si][:, 0:nch, :].rearrange("p c t -> p (c t)"), pbh[:, 0:nch * 128],
                        r=[("psb", h)], w=[("stg", si)])
                self.DMA("sp", dst, stg[si][:, 0:nch, :], ("stg", si), r=[("stg", si)], w=[])
            pend.append(tail)

        KTc_v = self.KTc.rearrange("(c p) t -> p c t", p=128)
        QTc_v = self.QTc.rearrange("(c p) t -> p c t", p=128)
        for blk in range(32):
            lf = lambda kc, blk=blk: hn1T[:, kc, blk * 128:(blk + 1) * 128]
            tbl = 0 * 32 + blk
            for gi in range(2):
                if blk >= OWN0 // 128:
                    qb = blk - OWN0 // 128
                    proj_group(lf, gi * 384, 384, "qk", tbl, QTc_v[:, gi * 3:(gi + 1) * 3, qb * 128:(qb + 1) * 128])
                proj_group(lf, 768 + gi * 384, 384, "qk", tbl, KTc_v[:, gi * 3:(gi + 1) * 3, blk * 128:(blk + 1) * 128])
                proj_group(lf, 1536 + gi * 384, 384, "v", tbl, self.Vc[blk * 128:(blk + 1) * 128, gi * 384:(gi + 1) * 384])
        for pi, d in enumerate((1, 4, 16)):
            L = NLOC // d
            nb = L // 128
            nk_lo = (13, 2, 0)[pi]
            nq_lo = (14, 3, 0)[pi]
            KTd_v = self.KTd[pi].rearrange("(c p) t -> p c t", p=128)
            QTd_v = self.QTd[pi].rearrange("(c p) t -> p c t", p=128)
            for r in range(d):
                for n in range(nk_lo, nb):
                    st0 = r + d * n * 128
                    lf = lambda kc, st0=st0, d=d: hn1T[:, kc, st0: st0 + d * 127 + 1: d]
                    tbl = pi * 32 + r * nb + n
                    u0 = r * L + n * 128
                    if n >= nq_lo:
                        proj_group(lf, 2304 + pi * 256, 256, "qk", tbl, QTd_v[:, :, u0:u0 + 128])
                    proj_group(lf, 3072 + pi * 256, 256, "qk", tbl, KTd_v[:, :, u0:u0 + 128])
                    proj_group(lf, 3840 + pi * 256, 256, "v", tbl, self.Vd[pi][u0:u0 + 128, :])
        flush()
        S.pop()

    def phase_D(self):
        nc, S, I, C, ps = self.nc, self.S, self.I, self.C, self.ps
        S.push()
        lqk = S.sb("lqk", [128, 4, 64], F32)
        for i, nm in enumerate(("lq1", "lk1", "lq2", "lk2")):
            self.DMA("sp", lqk[:, i, :], I[nm].partition_broadcast(128), ("lqk", i), w=[("lqk", i)])
        lp = S.sb("lp", [128, 2, 64], F32)
        lsum = S.sb("lsum", [128, 2], F32)
        neglam = S.sb("neglam", [128, 1], F32)
        for i in range(2):
            self.TTo("dve", lp[:, i, :], lqk[:, 2 * i, :], lqk[:, 2 * i + 1, :], ALU.mult, r=[("lqk", 2 * i), ("lqk", 2 * i + 1)], w=[("lp", i)])
            self.S.op("dve", lambda e, i=i: e.reduce_sum(out=lsum[:, i:i + 1], in_=lp[:, i, :], axis=AX.X), [("lp", i)], [("lsum", i)])
        self.ACT(lsum[:], lsum[:], AF.Exp, r=[("lsum", 0), ("lsum", 1)], w=["lexp"])
        self.TTo("dve", neglam[:], lsum[:, 1:2], lsum[:, 0:1], ALU.subtract, r=["lexp"], w=["neglam"])
        self.TS("dve", neglam[:], neglam[:], -LAMBDA_INIT, None, ALU.add, r=["neglam"], w=["neglam"])
        sgcol = S.sb("sgcol", [128, 1], F32)
        self.col_load(sgcol[:], I["subln"], "sgcol")
        self.TS("dve", sgcol[:], sgcol[:], 1.0 - LAMBDA_INIT, None, ALU.mult, r=["sgcol"], w=["sgcol"])

        KT = [S.sb(f"KTh{i}", [128, NLOC], BF16) for i in range(2)]
        QT = [[S.sb(f"QTh{i}_{sub}", [128, NOWN], BF16) for sub in range(2)] for i in range(2)]
        for i in range(2):
            for sub in range(2):
                self.MEMSET("pool", QT[i][sub][(1 - sub) * 64:(2 - sub) * 64, :], 0.0, w=[("QTz", i, sub)])
        V = [S.sb(f"Vh{i}", [128, 32, 128], BF16) for i in range(2)]
        E = [S.sb(f"E{i}", [128, 512], BF16) for i in range(3)]
        o_ = S.sb("o_", [128, 512], F32)
        sq = S.sb("sq", [128, 512], BF16)
        rs = S.sb("rs", [128, 512], F32)
        ycst = [S.sb(f"ycst{i}", [128, 512], BF16) for i in range(2)]
        Vc_v = self.Vc.rearrange("(n p) c -> p n c", p=128)

        def load(h):
            s = h % 2
            self.DMA("sp", KT[s][:], self.KTc[h * 128:(h + 1) * 128, :], ("KTh", s), w=[("KTh", s)])
            for sub in range(2):
                self.DMA("sp", QT[s][sub][sub * 64:(sub + 1) * 64, :], self.QTc[h * 128 + sub * 64:h * 128 + (sub + 1) * 64, :],
                         ("QTh", s), w=[("QTh", s, sub)])
            self.DMA("sp", V[s][:], Vc_v[:, :, h * 128:(h + 1) * 128], ("Vh", s), w=[("Vh", s)])
        load(0)
        supers = [(1792, 256), (2048, 512), (2560, 512), (3072, 512), (3584, 512)]
        oS = [S.sb(f"oS{i}", [128, 512], F32) for i in range(2)]
        rS = [S.sb(f"rS{i}", [128, 512], F32) for i in range(2)]
        state = dict(nst=0, pending=None)

        def fin_part1(nq):
            self.ACT(oS[0][:, 0:nq], ps[4][:, 0:nq], AF.Copy, r=[("ps", 4)], w=[("oS", 0)])
            self.TS("dve", rS[0][:, 0:nq], ps[6][:, 0:nq], 1e-30, None, ALU.max, r=[("ps", 6)], w=[("rS", 0)])
            self.ACT(oS[1][:, 0:nq], ps[5][:, 0:nq], AF.Copy, r=[("ps", 5)], w=[("oS", 1)])
            self.TS("dve", rS[1][:, 0:nq], ps[7][:, 0:nq], 1e-30, None, ALU.max, r=[("ps", 7)], w=[("rS", 1)])

        def fin_part2(h, qc0, nq):
            for sub in range(2):
                self.ACT(rS[sub][:, 0:nq], rS[sub][:, 0:nq], AF.Ln, r=[("rS", sub)], w=[("rS", sub)])
                self.ACT(rS[sub][:, 0:nq], rS[sub][:, 0:nq], AF.Exp, r=[("rS", sub)], w=[("rS", sub)], scale=-1.0)
                self.TTo("dve", oS[sub][:, 0:nq], oS[sub][:, 0:nq], rS[sub][:, 0:nq], ALU.mult, r=[("oS", sub), ("rS", sub)], w=[("oS", sub)])
            self.STT("dve", o_[:, 0:nq], oS[1][:, 0:nq], neglam[:, 0:1], oS[0][:, 0:nq], ALU.mult, ALU.add,
                     r=[("oS", 0), ("oS", 1), "neglam"], w=["o_"])
            self.ACT(sq[:, 0:nq], o_[:, 0:nq], AF.Square, r=["o_"], w=["sq"])
            self.MM(ps[3][:, 0:nq], self.onesb[:], sq[:, 0:nq], True, True, r=["onesb", "sq"], w=[("ps", 3)], maxn=256)
            self.ACT(rs[:, 0:nq], ps[3][:, 0:nq], AF.Ln, r=[("ps", 3), "eps"], w=["rs"], bias=self.epsb[:], scale=1.0 / 128)
            self.ACT(rs[:, 0:nq], rs[:, 0:nq], AF.Exp, r=["rs"], w=["rs"], scale=-0.5)
            yi = state["nst"] % 2
            state["nst"] += 1
            self.STT("dve", ycst[yi][:, 0:nq], o_[:, 0:nq], sgcol[:, 0:1], rs[:, 0:nq], ALU.mult, ALU.mult,
                     r=["o_", "sgcol", "rs"], w=[("ycst", yi)])
            self.DMA("sp", self.Y1T[h * 128:(h + 1) * 128, qc0:qc0 + nq], ycst[yi][:, 0:nq], ("ycst", yi), r=[("ycst", yi)], w=[])

        for h in range(6):
            s = h % 2
            if h + 1 < 6:
                load(h + 1)
            if h == 0:
                for i in range(32):
                    self.MM(ps[i % 3][:, :], self.onesb[:], KT[0][:, (i % 8) * 512:(i % 8 + 1) * 512], True, True,
                            r=["onesb", ("KTh", 0)], w=[("ps", i % 3)])
                if getattr(self, "wup_issue", None) is not None:
                    self.wup_issue()
                    self.wup_issue = None
            for (q0, nq) in supers:
                qc0 = q0 - OWN0
                nkb = (q0 + nq) // 128
                units = [(kb, sub) for kb in range(nkb) for sub in range(2)]

                def pv(i):
                    kb, sub = units[i]
                    ei = i % 3
                    n0 = max(0, kb * 128 - q0)
                    wd = nq - n0
                    self.MM(ps[4 + sub][:, n0:nq], V[s][:, kb, :], E[ei][:, 0:wd], kb == 0, kb == nkb - 1,
                            r=[("Vh", s), ("E", ei)], w=[("ps", 4 + sub)], maxn=256)
                    self.MM(ps[6 + sub][:, n0:nq], self.onesb[:], E[ei][:, 0:wd], kb == 0, kb == nkb - 1,
                            r=["onesb", ("E", ei)], w=[("ps", 6 + sub)], maxn=256)
                for i, (kb, sub) in enumerate(units):
                    n0 = max(0, kb * 128 - q0)
                    wd = nq - n0
                    diag = kb * 128 >= q0
                    ei = i % 3
                    sc = ps[ei]
                    self.MM(sc[:, 0:wd], KT[s][:, kb * 128:(kb + 1) * 128],
                            QT[s][sub][:, qc0 + n0:qc0 + nq], True, True,
                            r=[("KTh", s), ("QTh", s, sub), ("QTz", s, sub)], w=[("ps", ei)], maxn=256)
                    if i >= 2:
                        pv(i - 2)
                    bias = C[:, C_CTXB:C_CTXB + 1] if kb < 16 else self.zerob[:]
                    self.ACT(E[ei][:, 0:wd], sc[:, 0:wd], AF.Exp, r=[("ps", ei), "cst", "zerob"], w=[("E", ei)], bias=bias, scale=0.125)
                    if diag:
                        self.TTo("pool", E[ei][:, 0:128], E[ei][:, 0:128], self.Ub[:], ALU.mult, r=[("E", ei), "Ub"], w=[("E", ei)])
                    if i == 12 and state["pending"] is not None:
                        fin_part2(*state["pending"])
                        state["pending"] = None
                pv(len(units) - 2)
                pv(len(units) - 1)
                if state["pending"] is not None:
                    fin_part2(*state["pending"])
                    state["pending"] = None
                fin_part1(nq)
                state["pending"] = (h, qc0, nq)
        fin_part2(*state["pending"])
        S.pop()

    def phase_E(self):
        nc, S, I, C, ps = self.nc, self.S, self.I, self.C, self.ps
        S.push()
        M_LU = S.sb("M_LU", [128, 256], BF16)
        M_LvU = S.sb("M_LvU", [128, 256], BF16)
        M_LvUv = S.sb("M_LvUv", [128, 256], BF16)
        for (m, a, b, nm) in ((M_LU, self.Lb, self.Ub, "M_LU"), (M_LvU, self.Lvb, self.Ub, "M_LvU"), (M_LvUv, self.Lvb, self.Uvb, "M_LvUv")):
            self.CP("dve", m[:, 0:128], a[:], r=["Lb", "Lvb"], w=[(nm, 0)])
            self.CP("dve", m[:, 128:256], b[:], r=["Ub", "Uvb"], w=[(nm, 1)])
        MR = lambda nm: [(nm, 0), (nm, 1)]
        Kh = [S.sb(f"Kh{i}", [128, NLOC], BF16) for i in range(2)]
        Qh = [S.sb(f"Qh{i}", [128, NLOC], BF16) for i in range(2)]
        for i in range(2):
            self.MEMSET("pool", Kh[i][64:128, :], 0.0, w=[("Khz", i)])
            self.MEMSET("pool", Qh[i][64:128, :], 0.0, w=[("Qhz", i)])
        Vh = [S.sb(f"Vdh{i}", [128, 32, 64], BF16) for i in range(2)]
        E = [S.sb(f"Ed{i}", [128, 256], BF16) for i in range(2)]
        accn = S.sb("accn", [64, NOWN], F32)
        accd = S.sb("accd", [64, NOWN], F32)
        ydst = [S.sb(f"ydst{i}", [64, NOWN], BF16) for i in range(2)]
        jobs = [(hh, pi) for hh in range(4) for pi in range(3)]

        def load(ji):
            hh, pi = jobs[ji]
            s = ji % 2
            self.DMA("sp", Kh[s][0:64, :], self.KTd[pi][hh * 64:(hh + 1) * 64, :], ("Kh", s), w=[("Kh", s)])
            self.DMA("sp", Qh[s][0:64, :], self.QTd[pi][hh * 64:(hh + 1) * 64, :], ("Qh", s), w=[("Qh", s)])
            self.DMA("sp", Vh[s][:], self.Vd[pi].rearrange("(n p) c -> p n c", p=128)[:, :, hh * 64:(hh + 1) * 64], ("Vdh", s), w=[("Vdh", s)])
        load(0)
        for i in range(32):
            self.MM(ps[i % 2][:, :], self.onesb[:], Kh[0][:, (i % 8) * 512:(i % 8 + 1) * 512], True, True,
                    r=["onesb", ("Kh", 0), ("Khz", 0)], w=[("ps", i % 2)])
        bi = 0
        pend = []

        def flush():
            while pend:
                pend.pop(0)()
        for ji, (hh, pi) in enumerate(jobs):
            s = ji % 2
            if ji + 1 < len(jobs):
                load(ji + 1)
            d = (1, 4, 16)[pi]
            L = NLOC // d
            nb = L // 128
            nq_lo = (14, 3, 0)[pi]
            nctx = 16 // d
            for r in range(d):
                for n in range(nq_lo, nb):
                    u0 = r * L + n * 128
                    kbs = [n - 1, n] if n >= 1 else [n]
                    sl = bi % 2; bi += 1
                    sc = ps[sl]
                    for i, kbn in enumerate(kbs):
                        self.MM(sc[:, i * 128:(i + 1) * 128], Kh[s][:, r * L + kbn * 128: r * L + (kbn + 1) * 128], Qh[s][:, u0:u0 + 128],
                                i == 0, True, r=[("Kh", s), ("Qh", s), ("Khz", s), ("Qhz", s)], w=[("ps", sl)])
                    flush()
                    nw = len(kbs) * 128
                    self.ACT(E[sl][:, 0:nw], sc[:, 0:nw], AF.Exp, r=[("ps", sl)], w=[("Ed", sl)], scale=0.125)
                    if len(kbs) == 2:
                        pc, cc = (n - 1) < nctx, n < nctx
                        nm = "M_LvUv" if (pc and cc) else ("M_LvU" if pc else "M_LU")
                        mt = {"M_LU": M_LU, "M_LvU": M_LvU, "M_LvUv": M_LvUv}[nm]
                        self.TTo("pool", E[sl][:, 0:256], E[sl][:, 0:256], mt[:], ALU.mult, r=[("Ed", sl)] + MR(nm), w=[("Ed", sl)])
                    else:
                        self.TTo("pool", E[sl][:, 0:128], E[sl][:, 0:128], self.Uvb[:], ALU.mult, r=[("Ed", sl), "Uvb"], w=[("Ed", sl)])

                    def tail(s=s, sl=sl, kbs=kbs, r=r, n=n, nb=nb, d=d, pi=pi):
                        pn, pd = ps[2 + sl], ps[4 + sl]
                        for i, kbn in enumerate(kbs):
                            self.MM(pn[0:64, 0:128], Vh[s][:, r * nb + kbn, :], E[sl][:, i * 128:(i + 1) * 128], i == 0, i == len(kbs) - 1,
                                    r=[("Vdh", s), ("Ed", sl)], w=[("ps", 2 + sl)])
                        for i, kbn in enumerate(kbs):
                            self.MM(pd[0:64, 0:128], self.onesb[:, 0:64], E[sl][:, i * 128:(i + 1) * 128], i == 0, i == len(kbs) - 1,
                                    r=["onesb", ("Ed", sl)], w=[("ps", 4 + sl)])
                        a0 = max(0, OWN0 // d - n * 128)
                        if a0 >= 128:
                            return
                        col0 = r + d * (n * 128 + a0) - OWN0
                        cnt_ = 128 - a0
                        cols = slice(col0, col0 + d * (cnt_ - 1) + 1, d)
                        if pi == 0:
                            self.CP("act", accn[:, cols], pn[0:64, a0:128], r=[("ps", 2 + sl)], w=["accn"])
                            self.CP("dve", accd[:, cols], pd[0:64, a0:128], r=[("ps", 4 + sl)], w=["accd"])
                        else:
                            self.TTo("dve", accn[:, cols], accn[:, cols], pn[0:64, a0:128], ALU.add, r=[("ps", 2 + sl), "accn"], w=["accn"])
                            self.TTo("dve", accd[:, cols], accd[:, cols], pd[0:64, a0:128], ALU.add, r=[("ps", 4 + sl), "accd"], w=["accd"])
                    pend.append(tail)
            flush()
            if pi == 2:
                yi = hh % 2
                self.TS("dve", accd[:], accd[:], 1e-30, None, ALU.max, r=["accd"], w=["accd"])
                self.RECIP(accd[:], accd[:], r=["accd"], w=["accd"])
                self.TTo("dve", ydst[yi][:], accn[:], accd[:], ALU.mult, r=["accn", "accd"], w=[("ydst", yi)])
                self.DMA("sp", self.Y1T[768 + hh * 64:768 + (hh + 1) * 64, :], ydst[yi][:], ("ydst", yi), r=[("ydst", yi)], w=[])
        S.pop()


_CACHE = {}


def _consts(p):
    c = np.zeros((128, C_N), np.float32)
    pi_, fi = np.meshgrid(np.arange(128), np.arange(128), indexing="ij")
    c[:, C_ID:C_ID + 128] = (pi_ == fi)
    c[:, C_L:C_L + 128] = (fi <= pi_)
    c[:, C_U:C_U + 128] = (pi_ <= fi)
    c[:, C_VIS] = 1.0 if p == 1 else 0.0
    c[:, C_CTXB] = 0.0 if p == 1 else -30000.0
    t = np.arange(16)
    for g, w in enumerate((2, 4, 8, 16)):
        true_rc = 1.0 / np.minimum(t + 1, w)
        c[:, C_RCC + g * 16:C_RCC + (g + 1) * 16] = true_rc
        c[:, C_RCO + g * 16:C_RCO + (g + 1) * 16] = true_rc if p == 0 else 1.0 / w
    inv = np.float32(500000.0) ** (-(np.arange(0, 16, 2, dtype=np.float32)) / np.float32(16))
    c[:, C_INV:C_INV + 8] = inv.astype(np.float32)
    return c


def make_in_maps(inp):
    f = lambda a: np.ascontiguousarray(np.asarray(a))
    x = f(inp["x"]); pos = f(inp["positions"]).astype(np.int32)
    shared = dict(
        norm_mix=f(inp["norm_mix"]), norm_ffn=f(inp["norm_ffn"]), final_norm=f(inp["final_norm"]).reshape(1, D),
        even_w_in=f(inp["even_w_in"])[0], gmlp_v_gain=f(inp["gmlp_v_gain"]).reshape(1, 512),
        gmlp_w_s=f(inp["gmlp_w_s"])[0], gmlp_b_s=f(inp["gmlp_b_s"]).reshape(1, 512),
        pool_w=f(inp["pool_w"])[0], pool_scale=f(inp["pool_scale"]).reshape(1, 512),
        even_w_out=f(inp["even_w_out"])[0], odd_w_in=f(inp["odd_w_in"])[0],
        lq1=f(inp["lambda_q1"]).reshape(1, 64), lk1=f(inp["lambda_k1"]).reshape(1, 64),
        lq2=f(inp["lambda_q2"]).reshape(1, 64), lk2=f(inp["lambda_k2"]).reshape(1, 64),
        subln=f(inp["subln_gain"]).reshape(1, 128), odd_w_out=f(inp["odd_w_out"])[0],
        ffn_w_up=f(inp["ffn_w_up"]), ffn_conv_w=f(inp["ffn_conv_w"]).reshape(2, 3, DFF),
        ffn_conv_b=f(inp["ffn_conv_b"]), ffn_w_down=f(inp["ffn_w_down"]),
    )
    maps = []
    for core in range(8):
        b, p = core // 2, core % 2
        if p == 1:
            xin = x[b]
            ps_ = pos[b]
        else:
            xin = np.concatenate([np.zeros((2048, D), np.float32), x[b, :2048]], axis=0)
            ps_ = np.concatenate([np.zeros((2048,), np.int32), pos[b, :2048]], axis=0)
        m = dict(shared)
        m["xin"] = np.ascontiguousarray(xin)
        m["pos"] = np.ascontiguousarray(ps_.reshape(1, NLOC))
        m["cst"] = _consts(p)
        maps.append(m)
    return maps


def kernel(**inp):
    if "nc" not in _CACHE:
        _CACHE["nc"] = KB().build()
    nc = _CACHE["nc"]
    maps = make_in_maps(inp)
    res = run_bass_kernel_spmd(nc, maps, core_ids=list(range(8)))
    out = np.zeros((4, 4096, D), np.float32)
    for core in range(8):
        b, p = core // 2, core % 2
        out[b, p * 2048:(p + 1) * 2048] = res.results[core]["out"]
    return out
```

```python
import numpy as np
import concourse.bass as bass
import concourse.mybir as mybir

F32 = mybir.dt.float32
BF16 = mybir.dt.bfloat16
I32 = mybir.dt.int32
AF = mybir.ActivationFunctionType
ALU = mybir.AluOpType
AX = mybir.AxisListType

ENGS = ("pe", "act", "dve", "pool", "sp")
SEM_CHUNK = 20000
SBUF_LO = 16512
SBUF_HI = 229344


class Op:
    __slots__ = ("eng", "fn", "deps", "idx", "dma", "semkey", "need_inc", "semval", "semid")

    def __init__(self, eng, fn, dma, semkey):
        self.eng = eng
        self.fn = fn
        self.deps = []
        self.dma = dma
        self.semkey = semkey
        self.need_inc = False
        self.semval = None
        self.semid = None


class Sched:
    def __init__(self, nc, same_engine_sync=True):
        self.nc = nc
        self.ops = {e: [] for e in ENGS}
        self.res = {}
        self.banks = {}
        self.persist_keys = set()
        self.same = same_engine_sync
        self.pending_barrier = {e: [] for e in ENGS}
        self.all_dma_since_barrier = []
        self.sb_off = SBUF_LO
        self.sb_stack = []
        self.nalloc = 0

    def push(self):
        self.sb_stack.append(self.sb_off)

    def pop(self):
        self.sb_off = self.sb_stack.pop()

    def sb(self, name, shape, dtype):
        esz = 4 if dtype in (F32, I32) else 2
        n = 1
        for s in shape[1:]:
            n *= s
        nbytes = (n * esz + 63) // 64 * 64
        off = self.sb_off
        assert off + nbytes <= SBUF_HI, f"SBUF overflow allocating {name}: {off}+{nbytes}"
        self.sb_off = off + nbytes
        self.nalloc += 1
        return self.nc.alloc_sbuf_tensor_at(f"{name}_{self.nalloc}", list(shape), dtype, offset=off)

    @staticmethod
    def _bank(key):
        if isinstance(key, tuple) and isinstance(key[0], str) and key[0].startswith("ps"):
            if key[0] in ("ps", "psA", "psG"):
                return key[1]
            if key[0] == "psb":
                return 6 + key[1]
            return 6
        return None

    def op(self, eng, fn, reads=(), writes=(), dma=False, semkey=None, persist=False):
        o = Op(eng, fn, dma, semkey)
        if dma:
            assert semkey is not None
        deps = []
        banks = set()
        for k in list(reads) + list(writes):
            b = self._bank(k)
            if b is not None:
                banks.add(b)
        reads = [k for k in reads if self._bank(k) is None]
        writes = [k for k in writes if self._bank(k) is None]
        for b in banks:
            st = self.banks.setdefault(b, {})
            for e2, o2 in st.items():
                if e2 != eng:
                    deps.append(o2)
            st[eng] = o
        for r in reads:
            st = self.res.get(r)
            if st is not None and st[0] is not None:
                deps.append(st[0])
        for w in writes:
            st = self.res.get(w)
            if st is not None:
                if st[0] is not None:
                    deps.append(st[0])
                deps.extend(st[1])
        deps.extend(self.pending_barrier[eng])
        self.pending_barrier[eng] = []
        for r in reads:
            st = self.res.setdefault(r, [None, []])
            st[1].append(o)
        for w in writes:
            self.res[w] = [o, []]
        o.deps = deps
        o.idx = len(self.ops[eng])
        self.ops[eng].append(o)
        if dma and not persist:
            self.all_dma_since_barrier.append(o)
        if persist:
            self.persist_keys.update(writes)
        return o

    def barrier(self):
        lasts = []
        for e in ENGS:
            if self.ops[e]:
                lasts.append(self.ops[e][-1])
        lasts.extend(self.all_dma_since_barrier)
        self.all_dma_since_barrier = []
        for e in ENGS:
            self.pending_barrier[e] = list(lasts)
        self.res = {k: [v[0], []] for k, v in self.res.items() if k in self.persist_keys}
        self.banks = {}

    def emit(self, final_wait_ops=()):
        nc = self.nc
        for e in ENGS:
            for o in self.ops[e]:
                for d in o.deps:
                    if d.dma:
                        d.need_inc = True
                    elif d.eng != o.eng or (self.same and o.eng != "pe"):
                        d.need_inc = True
        for o in final_wait_ops:
            o.need_inc = True
        import contextlib
        stack = contextlib.ExitStack()
        sem_objs = {}

        def get_sem(key):
            if key not in sem_objs:
                sem_objs[key] = stack.enter_context(nc.semaphore(f"s{len(sem_objs)}"))
            return sem_objs[key]

        dma_counts = {}
        for e in ENGS:
            cnt = 0
            for o in self.ops[e]:
                if o.dma:
                    c = dma_counts.get(o.semkey, 0) + 16
                    dma_counts[o.semkey] = c
                    o.semid = ("dma", o.semkey)
                    o.semval = c
                    assert c < 60000, f"dma sem overflow {o.semkey}"
                elif o.need_inc:
                    o.semid = ("eng", e, cnt // SEM_CHUNK)
                    o.semval = cnt % SEM_CHUNK + 1
                    cnt += 1
        for e in ENGS:
            for o in self.ops[e]:
                if o.semid is not None and (o.dma or o.need_inc):
                    get_sem(o.semid)
        self.nsems = len(sem_objs)
        engmap = {"pe": "tensor", "act": "scalar", "dve": "vector", "pool": "gpsimd", "sp": "sync"}
        with stack:
            with nc.Block() as block:
                def make(e):
                    def body(eng):
                        waited = {}
                        for o in self.ops[e]:
                            need = {}
                            for d in o.deps:
                                if not d.dma and d.eng == e and (not self.same or e == "pe"):
                                    continue
                                sid = d.semid
                                if sid is None:
                                    continue
                                if d.semval > need.get(sid, 0):
                                    need[sid] = d.semval
                            for sid, v in need.items():
                                if waited.get(sid, 0) >= v:
                                    continue
                                eng.wait_ge(get_sem(sid), v)
                                waited[sid] = v
                            ins = o.fn(eng)
                            if o.dma:
                                ins.then_inc(get_sem(o.semid), 16)
                            elif o.need_inc:
                                ins.then_inc(get_sem(o.semid), 1)
                        if e == "sp":
                            for key, c in dma_counts.items():
                                eng.wait_ge(get_sem(("dma", key)), c)
                    return body
                for e in ENGS:
                    if self.ops[e] or e == "sp":
                        getattr(block, engmap[e])(make(e))
        return nc

from concourse.bass_utils import run_bass_kernel_spmd

D = 1024
TT = 256
NLOC = 4096
OWN0 = 1792
NOWN = NLOC - OWN0
DFF = 2816
NFC = 22
LAMBDA_INIT = 0.8 - 0.6 * float(np.exp(-0.3 * 1))
C_ID, C_L, C_U, C_VIS, C_CTXB, C_RCC, C_RCO, C_INV, C_N = 0, 128, 256, 384, 385, 386, 450, 514, 528


class KB:
    def __init__(self, debug=False, phases="ABCDEF"):
        self.phases = phases
        self.nc = nc = bass.Bass("TRN2", target_bir_lowering=False)
        self.S = Sched(nc)
        self.debug = debug
        self.ps = [nc.alloc_psum_tensor(f"ps{i}", [128, 512], F32) for i in range(8)]
        self.psbv = [self.ps[6][:].bitcast(BF16), self.ps[7][:].bitcast(BF16)]

    def MM(self, out, lhsT, rhs, start=True, stop=True, r=(), w=(), maxn=None):
        n = rhs.shape[-1]
        if maxn is not None and n > maxn:
            npc = -(-n // maxn)
            step = -(-n // npc)
            o = None
            for a in range(0, n, step):
                bnd = min(n, a + step)
                o_, r_ = out[:, a:bnd], rhs[:, a:bnd]
                st_ = start and a == 0
                o = self.S.op("pe", lambda e, o_=o_, r_=r_, st_=st_: e.matmul(o_, lhsT=lhsT, rhs=r_, start=st_, stop=stop,
                                                                         skip_group_check=True), r, w)
            return o
        return self.S.op("pe", lambda e: e.matmul(out, lhsT=lhsT, rhs=rhs, start=start, stop=stop), r, w)

    def TR(self, out, in_, r=(), w=()):
        idb = self.idb
        return self.S.op("pe", lambda e: e.transpose(out=out, in_=in_, identity=idb[:]), list(r) + ["idb"], w)

    def ACT(self, out, in_, func, r=(), w=(), bias=None, scale=None, accum=None):
        kw = {}
        if bias is not None:
            kw["bias"] = bias
        if scale is not None:
            kw["scale"] = scale
        if accum is not None:
            kw["accum_out"] = accum
        return self.S.op("act", lambda e: e.activation(out=out, in_=in_, func=func, **kw), r, w)

    def TS(self, eng, out, in0, s1, s2, op0, op1=None, r=(), w=()):
        if op1 is None:
            return self.S.op(eng, lambda e: e.tensor_scalar(out=out, in0=in0, scalar1=s1, scalar2=None, op0=op0), r, w)
        return self.S.op(eng, lambda e: e.tensor_scalar(out=out, in0=in0, scalar1=s1, scalar2=s2, op0=op0, op1=op1), r, w)

    def TTo(self, eng, out, in0, in1, op, r=(), w=()):
        return self.S.op(eng, lambda e: e.tensor_tensor(out=out, in0=in0, in1=in1, op=op), r, w)

    def STT(self, eng, out, in0, scalar, in1, op0, op1, r=(), w=()):
        return self.S.op(eng, lambda e: e.scalar_tensor_tensor(out=out, in0=in0, scalar=scalar, in1=in1, op0=op0, op1=op1), r, w)

    def CP(self, eng, out, in_, r=(), w=()):
        if eng == "act":
            return self.S.op(eng, lambda e: e.copy(out=out, in_=in_), r, w)
        return self.S.op(eng, lambda e: e.tensor_copy(out=out, in_=in_), r, w)

    def RECIP(self, out, in_, r=(), w=()):
        return self.S.op("dve", lambda e: e.reciprocal(out=out, in_=in_), r, w)

    def MEMSET(self, eng, ap, val, w=()):
        return self.S.op(eng, lambda e: e.memset(ap, val), (), w)

    def DMA(self, eng, out, in_, key, r=(), w=(), slow=False, persist=False):
        if slow:
            return self.S.op(eng, lambda e: e.dma_start(out=out, in_=in_, allow_slow_non_contiguous=True), r, w, dma=True, semkey=key)
        return self.S.op(eng, lambda e: e.dma_start(out=out, in_=in_), r, w, dma=True, semkey=key, persist=persist)

    def col_load(self, dst, src_row, key):
        self.DMA("sp", dst, src_row.rearrange("o (c p) -> p (o c)", p=128), key, w=[key], slow=True)

    def rmsnorm_T(self, xt, j_n, hn, hnT, gcol, tag, slot, ps_half):
        S = self.S
        ss, rstd, junk = self.ss, self.rstd, self.junk
        xr = (tag, "x", slot)
        for j in range(j_n):
            self.ACT(junk[:], xt[:, j, :], AF.Square, r=[xr], w=["junk", ("ss", j)], accum=ss[:, j:j + 1])
        self.ACT(rstd[:, 0:j_n], ss[:, 0:j_n], AF.Sqrt, r=[("ss", j) for j in range(j_n)] + ["eps"], w=["rstd"],
                 bias=self.epsb[:], scale=1.0 / D)
        self.RECIP(rstd[:, 0:j_n], rstd[:, 0:j_n], r=["rstd"], w=["rstd"])
        for j in range(j_n):
            self.TS("dve", hn[:, j, :], xt[:, j, :], rstd[:, j:j + 1], None, ALU.mult, r=[xr, "rstd"], w=[("hn", j)])
        for kc in range(8):
            h = (kc + ps_half) % 2
            psb = self.psbv[h]
            for j in range(j_n):
                self.TR(psb[:, j * 128:(j + 1) * 128], hn[:, j, kc * 128:(kc + 1) * 128],
                        r=[("hn", j)], w=[("psb", h)])
            rr = [("psb", h), gcol[1]]
            if kc % 2 == 0:
                self.ACT(hnT[:, kc, 0:j_n * 128], psb[:, 0:j_n * 128], AF.Copy, r=rr,
                         w=[(tag, "hnT", slot, kc)], scale=gcol[0][:, kc:kc + 1])
            else:
                self.TS("dve", hnT[:, kc, 0:j_n * 128], psb[:, 0:j_n * 128], gcol[0][:, kc:kc + 1], None,
                        ALU.mult, r=rr, w=[(tag, "hnT", slot, kc)])

    def build(self):
        nc, S = self.nc, self.S
        din = lambda name, shape, dt=F32: nc.dram_tensor(name, list(shape), dt, kind="ExternalInput").ap()
        scr = lambda name, shape, dt: nc.dram_tensor(name, list(shape), dt, kind="Internal").ap()
        I = self.I = dict(
            xin=din("xin", [NLOC, D]), pos=din("pos", [1, NLOC], I32), cst=din("cst", [128, C_N]),
            norm_mix=din("norm_mix", [2, D]), norm_ffn=din("norm_ffn", [2, D]), final_norm=din("final_norm", [1, D]),
            even_w_in=din("even_w_in", [D, 1536]), gmlp_v_gain=din("gmlp_v_gain", [1, 512]),
            gmlp_w_s=din("gmlp_w_s", [4, 128, 128]), gmlp_b_s=din("gmlp_b_s", [1, 512]),
            pool_w=din("pool_w", [4, 128, 128]), pool_scale=din("pool_scale", [1, 512]),
            even_w_out=din("even_w_out", [D, D]), odd_w_in=din("odd_w_in", [D, 4608]),
            lq1=din("lq1", [1, 64]), lk1=din("lk1", [1, 64]), lq2=din("lq2", [1, 64]), lk2=din("lk2", [1, 64]),
            subln=din("subln", [1, 128]), odd_w_out=din("odd_w_out", [D, D]),
            ffn_w_up=din("ffn_w_up", [2, D, 2 * DFF]), ffn_conv_w=din("ffn_conv_w", [2, 3, DFF]),
            ffn_conv_b=din("ffn_conv_b", [2, DFF]), ffn_w_down=din("ffn_w_down", [2, DFF, D]),
        )
        self.out = nc.dram_tensor("out", [2048, D], F32, kind="ExternalOutput").ap()
        dbgk = "ExternalOutput" if self.debug else "Internal"
        self.h1s = nc.dram_tensor("h1s", [NLOC, D], F32, kind=dbgk).ap()
        self.hL0 = nc.dram_tensor("hL0", [NLOC, D], F32, kind=dbgk).ap()
        self.KTc = scr("KTc", [768, NLOC], BF16)
        self.QTc = scr("QTc", [768, NOWN], BF16)
        self.Vc = scr("Vc", [NLOC, 768], BF16)
        self.KTd = scr("KTd", [3, 256, NLOC], BF16)
        self.QTd = scr("QTd", [3, 256, NLOC], BF16)
        self.Vd = scr("Vd", [3, NLOC, 256], BF16)
        self.Y1T = nc.dram_tensor("Y1T", [D, NOWN], BF16, kind=dbgk).ap()

        C = self.C = S.sb("cst", [128, C_N], F32)
        self.DMA("sp", C[:], I["cst"], "cst", w=["cst"])
        self.idb = S.sb("idb", [128, 128], BF16)
        self.Lb = S.sb("Lb", [128, 128], BF16)
        self.Ub = S.sb("Ub", [128, 128], BF16)
        self.Lvb = S.sb("Lvb", [128, 128], BF16)
        self.Uvb = S.sb("Uvb", [128, 128], BF16)
        self.onesb = S.sb("onesb", [128, 128], BF16)
        self.epsb = S.sb("epsb", [128, 1], F32)
        self.zerob = S.sb("zerob", [128, 1], F32)
        self.ss = S.sb("ss", [128, 4], F32)
        self.rstd = S.sb("rstd", [128, 4], F32)
        self.junk = S.sb("junk", [128, 1024], BF16)
        self.CP("dve", self.idb[:], C[:, C_ID:C_ID + 128], r=["cst"], w=["idb"])
        self.CP("dve", self.Lb[:], C[:, C_L:C_L + 128], r=["cst"], w=["Lb"])
        self.CP("dve", self.Ub[:], C[:, C_U:C_U + 128], r=["cst"], w=["Ub"])
        self.TS("dve", self.Lvb[:], C[:, C_L:C_L + 128], C[:, C_VIS:C_VIS + 1], None, ALU.mult, r=["cst"], w=["Lvb"])
        self.TS("dve", self.Uvb[:], C[:, C_U:C_U + 128], C[:, C_VIS:C_VIS + 1], None, ALU.mult, r=["cst"], w=["Uvb"])
        self.MEMSET("pool", self.onesb[:], 1.0, w=["onesb"])
        self.MEMSET("pool", self.epsb[:], 1e-6, w=["eps"])
        self.MEMSET("pool", self.zerob[:], 0.0, w=["zerob"])
        S.barrier()

        ph = self.phases
        off0 = S.sb_off
        wup0 = self.prefetch_wup(0) if "B" in ph else None
        if "A" in ph:
            self.phase_A()
            S.barrier()
        if "B" in ph:
            self.phase_ffn(0, wup0)
            S.barrier()
        S.sb_off = off0
        if "C" in ph:
            self.phase_C()
            S.barrier()
        wup1 = self.prefetch_wup(1) if "F" in ph else None
        if "D" in ph:
            self.phase_D()
            S.barrier()
        if "E" in ph:
            self.phase_E()
            S.barrier()
        if "F" in ph:
            self.phase_ffn(1, wup1)
        S.emit()
        return nc

    def phase_A(self):
        nc, S, I, C, ps = self.nc, self.S, self.I, self.C, self.ps
        psb = self.psbv[0]
        S.push()
        win = S.sb("winA", [128, 8, 1536], BF16)
        wout = S.sb("woutA", [128, 8, 1024], BF16)
        for c in range(3):
            self.DMA("pool", win[:, :, c * 512:(c + 1) * 512],
                     I["even_w_in"].rearrange("(kc p) n -> p kc n", p=128)[:, :, c * 512:(c + 1) * 512],
                     ("winA", c), w=[("winA", c)])
        for c in range(2):
            self.DMA("pool", wout[:, :, c * 512:(c + 1) * 512],
                     I["even_w_out"].rearrange("(kc p) n -> p kc n", p=128)[:, :, c * 512:(c + 1) * 512],
                     ("woutA", c), w=[("woutA", c)])
        WIN = [("winA", c) for c in range(3)]
        WOUT = [("woutA", c) for c in range(2)]
        poolw = S.sb("poolw", [128, 4, 128], BF16)
        self.DMA("pool", poolw[:], I["pool_w"].rearrange("g c d -> c g d"), "poolw", w=["poolw"])
        bsrow = S.sb("bsrow", [1, 512], BF16)
        self.DMA("pool", bsrow[:], I["gmlp_b_s"], "bsrow", w=["bsrow"])
        wsf = S.sb("wsf", [128, 4, 128], F32)
        self.DMA("sp", wsf[:], I["gmlp_w_s"].rearrange("g t s -> t g s"), "wsf", w=["wsf"])
        wsb = S.sb("wsb", [128, 4, 128], BF16)
        WmT = S.sb("WmT", [128, 4, 128], BF16)
        for g in range(4):
            self.TTo("dve", wsb[:, g, :], wsf[:, g, :], C[:, C_L:C_L + 128], ALU.mult, r=["wsf"], w=[("wsb", g)])
            self.TR(psb[:, g * 128:(g + 1) * 128], wsb[:, g, :], r=[("wsb", g)], w=[("psbw", g)])
        self.CP("dve", WmT[:].rearrange("p g t -> p (g t)"), psb[:, 0:512], r=[("psbw", g) for g in range(4)], w=["WmT"])
        vgbc = S.sb("vgbc", [128, 512], F32)
        self.DMA("sp", vgbc[:], I["gmlp_v_gain"].partition_broadcast(128), "vgbc", w=["vgbc"])
        pscol = S.sb("pscol", [128, 4], F32)
        self.col_load(pscol[:], I["pool_scale"], "pscol")
        gcol = S.sb("gcolA", [128, 8], F32)
        self.col_load(gcol[:], I["norm_mix"][0:1, :], "gcolA")
        S.barrier()

        xt = [S.sb(f"xtA{i}", [128, 2, D], F32) for i in range(2)]
        hn = S.sb("hnA", [128, 2, D], BF16)
        hnT = [S.sb(f"hnTA{i}", [128, 8, TT], BF16) for i in range(2)]
        uT = S.sb("uT", [128, 4, TT], BF16)
        pbuf = [S.sb(f"pbuf{i}", [128, 4, 16 + TT], F32) for i in range(2)]
        tmpA = S.sb("tmpA", [128, 16 + TT], F32)
        tmpB = S.sb("tmpB", [128, 16 + TT], F32)
        tmpf = S.sb("tmpf", [128, 16], F32)
        pooled = S.sb("pooled", [128, 4, TT], BF16)
        vg = S.sb("vg", [128, 512], F32)
        vss = S.sb("vss", [128, 2], F32)
        vr = S.sb("vr", [128, 2], F32)
        vn = S.sb("vn", [128, 2, 512], BF16)
        yT = S.sb("yTA", [128, 8, TT], BF16)
        W = 16 + TT
        xin_v = I["xin"].rearrange("(t j p) d -> t p j d", p=128, j=2)
        h1s_v = self.h1s.rearrange("(t j p) d -> t p j d", p=128, j=2)
        NT = NLOC // TT
        self.MEMSET("pool", pbuf[1][:, :, TT:TT + 16], 0.0, w=[("pbuf", 1, g) for g in range(4)])
        self.DMA("sp", xt[0][:], xin_v[0], ("xtA", 0), w=[("A", "x", 0)])
        for t in range(NT):
            s = t % 2
            if t + 1 < NT:
                self.DMA("sp", xt[1 - s][:], xin_v[t + 1], ("xtA", 1 - s), w=[("A", "x", 1 - s)])
            if t == 1 and getattr(self, "wup_issue", None) is not None:
                self.wup_issue()
                self.wup_issue = None
            self.rmsnorm_T(xt[s], 2, hn, hnT[s], (gcol, "gcolA"), "A", s, 0)
            HT = [("A", "hnT", s, kc) for kc in range(8)]
            for fc in range(4):
                pt = ps[fc % 2]
                for kc in range(8):
                    self.MM(pt[:, 0:TT], win[:, kc, fc * 128:(fc + 1) * 128], hnT[s][:, kc, :], kc == 0, kc == 7,
                            r=WIN + HT, w=[("ps", fc % 2)])
                self.ACT(uT[:, fc, :], pt[:, 0:TT], AF.Gelu, r=[("ps", fc % 2)], w=[("uT", fc)])
            for g in range(4):
                pt = ps[2 + g % 2]
                for kc in range(8):
                    self.MM(pt[:, 0:TT], win[:, kc, 1024 + g * 128:1024 + (g + 1) * 128], hnT[s][:, kc, :], kc == 0, kc == 7,
                            r=WIN + HT, w=[("ps", 2 + g % 2)])
                self.ACT(pbuf[s][:, g, 16:W], pt[:, 0:TT], AF.Copy, r=[("ps", 2 + g % 2)], w=[("pbuf", s, g)])
                self.CP("pool", pbuf[s][:, g, 0:16], pbuf[1 - s][:, g, TT:W], r=[("pbuf", 1 - s, g)], w=[("pbufh", s, g)])
            for j in range(2):
                pt = ps[4 + j]
                for kc in range(8):
                    self.MM(pt[:, :], hnT[s][:, kc, j * 128:(j + 1) * 128], win[:, kc, 512:1024], kc == 0, kc == 7,
                            r=WIN + HT, w=[("ps", 4 + j)])
                self.ACT(vg[:], pt[:, :], AF.Gelu, r=[("ps", 4 + j)], w=["vg"])
                self.ACT(self.junk[:, 0:512], vg[:], AF.Square, r=["vg"], w=["junk", ("vss", j)], accum=vss[:, j:j + 1])
                self.ACT(vr[:, j:j + 1], vss[:, j:j + 1], AF.Sqrt, r=[("vss", j), "eps"], w=[("vr", j)], bias=self.epsb[:], scale=1.0 / 512)
                self.RECIP(vr[:, j:j + 1], vr[:, j:j + 1], r=[("vr", j)], w=[("vr", j)])
                self.STT("dve", vn[:, j, :], vg[:], vr[:, j:j + 1], vgbc[:], ALU.mult, ALU.mult, r=["vg", ("vr", j), "vgbc"], w=[("vn", j)])
            for g in range(4):
                pt = ps[g % 2]
                for j in range(2):
                    self.MM(pt[:, j * 128:(j + 1) * 128], vn[:, j, g * 128:(g + 1) * 128], WmT[:, g, :], True, False,
                            r=[("vn", j), "WmT"], w=[("ps", g % 2)])
                    self.MM(pt[:, j * 128:(j + 1) * 128], self.onesb[0:1, :], bsrow[0:1, g * 128:(g + 1) * 128], False, True,
                            r=["onesb", "bsrow"], w=[("ps", g % 2)])
                self.TTo("dve", yT[:, g, :], pt[:, 0:TT], uT[:, g, :], ALU.mult, r=[("ps", g % 2), ("uT", g)], w=[("yT", g)])
            for g in range(4):
                cur = pbuf[s][:, g, :]
                rr = [("pbuf", s, g), ("pbufh", s, g)]
                src, lag = cur, 1
                bufs = [tmpA, tmpB]
                for step in range(g + 1):
                    dst = bufs[step % 2]
                    lo = 2 * lag - 1
                    srcr = rr if step == 0 else [("ptmp", (step - 1) % 2)]
                    self.TTo("pool", dst[:, lo:W], src[:, lo:W], src[:, lo - lag:W - lag], ALU.add, r=srcr, w=[("ptmp", step % 2)])
                    src = dst
                    lag *= 2
                wdw = 2 ** (g + 1)
                last = [("ptmp", g % 2)]
                self.STT("dve", pooled[:, g, :], src[:, 16:W], 1.0 / wdw, cur[:, 16:W], ALU.mult, ALU.subtract, r=last + rr, w=[("pooled", g)])
                if t == 0 or t == 2048 // TT:
                    off = C_RCC if t == 0 else C_RCO
                    self.TTo("dve", tmpf[:], src[:, 16:32], C[:, off + g * 16: off + (g + 1) * 16], ALU.mult, r=last, w=["tmpf"])
                    self.TTo("dve", pooled[:, g, 0:16], tmpf[:], cur[:, 16:32], ALU.subtract, r=["tmpf"] + rr, w=[("pooled", g)])
                pt = ps[2 + g % 2]
                self.MM(pt[:, 0:TT], poolw[:, g, :], pooled[:, g, :], True, True, r=["poolw", ("pooled", g)], w=[("ps", 2 + g % 2)])
                self.ACT(yT[:, 4 + g, :], pt[:, 0:TT], AF.Copy, r=[("ps", 2 + g % 2), "pscol"], w=[("yT", 4 + g)], scale=pscol[:, g:g + 1])
            YT = [("yT", k) for k in range(8)]
            for j in range(2):
                for nh in range(2):
                    pt = ps[4 + (j * 2 + nh) % 2]
                    for kc in range(8):
                        self.MM(pt[:, :], yT[:, kc, j * 128:(j + 1) * 128], wout[:, kc, nh * 512:(nh + 1) * 512], kc == 0, kc == 7,
                                r=YT + WOUT, w=[("ps", 4 + (j * 2 + nh) % 2)])
                    self.TTo("dve", xt[s][:, j, nh * 512:(nh + 1) * 512], xt[s][:, j, nh * 512:(nh + 1) * 512], pt[:, :], ALU.add,
                             r=[("ps", 4 + (j * 2 + nh) % 2), ("A", "x", s)], w=[("A", "x", s)])
            self.DMA("sp", h1s_v[t], xt[s][:], ("h1st", s), r=[("A", "x", s)], w=[("h1s", t)])
        S.pop()

    def prefetch_wup(self, l):
        S, I = self.S, self.I
        wup = S.sb(f"wup{l}", [128, 8, 2 * DFF], BF16)
        wup_src = I["ffn_w_up"][l].rearrange("(kc p) n -> p kc n", p=128)

        def issue():
            for c in [0, 5, 6, 1, 7, 2, 8, 3, 9, 4, 10]:
                self.DMA("pool", wup[:, :, c * 512:(c + 1) * 512], wup_src[:, :, c * 512:(c + 1) * 512], ("wup", l, c), w=[("wup", c)], persist=True)
        self.wup_issue = issue
        return wup

    def phase_ffn(self, l, wup):
        nc, S, I, C, ps = self.nc, self.S, self.I, self.C, self.ps
        psb = self.psbv[0]
        if getattr(self, "wup_issue", None) is not None:
            self.wup_issue()
            self.wup_issue = None
        S.push()
        wdn = S.sb("wdn", [128, NFC, D], BF16)
        wdn_src = I["ffn_w_down"][l].rearrange("(fc p) n -> p fc n", p=128)
        for c in range(2):
            self.DMA("pool", wdn[:, c * 11:(c + 1) * 11, :], wdn_src[:, c * 11:(c + 1) * 11, :], ("wdn", c), w=[("wdn", c)])
        cw = S.sb("cw", [128, 3, NFC], F32)
        self.DMA("sp", cw[:], I["ffn_conv_w"][l].rearrange("j (fc p) -> p j fc", p=128), "cw", w=["cw"], slow=True)
        cb = S.sb("cb", [128, NFC], F32)
        self.col_load(cb[:], I["ffn_conv_b"][l:l + 1, :], "cb")
        gcol = S.sb("gcolF", [128, 8], F32)
        self.col_load(gcol[:], I["norm_ffn"][l:l + 1, :], "gcolF")
        ahalo = S.sb("ahalo", [128, NFC, 2], F32)
        self.MEMSET("pool", ahalo[:], 0.0, w=[("ahalo", fc) for fc in range(NFC)])
        if l == 1:
            wout = S.sb("woutO", [128, 8, D], BF16)
            for c in range(2):
                self.DMA("pool", wout[:, :, c * 512:(c + 1) * 512],
                         I["odd_w_out"].rearrange("(kc p) n -> p kc n", p=128)[:, :, c * 512:(c + 1) * 512],
                         ("woutO", c), w=[("woutO", c)])
            WOUT = [("woutO", c) for c in range(2)]
            fnbc = S.sb("fnbc", [128, D], F32)
            self.DMA("sp", fnbc[:], I["final_norm"].partition_broadcast(128), "fnbc", w=["fnbc"])
            yt = [S.sb(f"ytF{i}", [128, 8, TT], BF16) for i in range(2)]
            outt = S.sb("outt", [128, D], F32)
            fss = S.sb("fss", [128, 2], F32)
            frs = S.sb("frs", [128, 2], F32)
        ht = [S.sb(f"htF{i}", [128, 2, D], F32) for i in range(2)]
        hn = S.sb("hnF", [128, 2, D], BF16)
        hnT = [S.sb(f"hnTF{i}", [128, 8, TT], BF16) for i in range(2)]
        ND = 3
        QB = (0, 1, 6)
        abuf = [S.sb(f"abuf{i}", [128, TT + 2], F32) for i in range(ND)]
        t1 = [S.sb(f"t1{i}", [128, TT], F32) for i in range(ND)]
        mm_ = [S.sb(f"m{i}", [128, TT], BF16) for i in range(ND)]
        if l == 0:
            NT = NLOC // TT
            src_v = self.h1s.rearrange("(t j p) d -> t p j d", p=128, j=2)
            dst_v = self.hL0.rearrange("(t j p) d -> t p j d", p=128, j=2)
            src_res = lambda t: [("h1s", t)]
        else:
            NT = NOWN // TT
            src_v = self.hL0[OWN0:NLOC, :].rearrange("(t j p) d -> t p j d", p=128, j=2)
            dst_v = self.out.rearrange("(t j p) d -> t p j d", p=128, j=2)
            src_res = lambda t: []
            y_v = self.Y1T.rearrange("(kc p) n -> p kc n", p=128)
        self.final_stores = []

        def load(t):
            s = t % 2
            self.DMA("sp", ht[s][:], src_v[t], ("htF", s), r=src_res(t), w=[("F", "x", s)])
            if l == 1:
                self.DMA("sp", yt[s][:], y_v[:, :, t * TT:(t + 1) * TT], ("ytF", s), w=[("ytF", s)])
        load(0)
        for t in range(NT):
            s = t % 2
            if t + 1 < NT:
                load(t + 1)
            if l == 1:
                for j in range(2):
                    for nh in range(2):
                        k = 2 + j * 2 + nh
                        for kc in range(8):
                            self.MM(ps[k][:, :], yt[s][:, kc, j * 128:(j + 1) * 128], wout[:, kc, nh * 512:(nh + 1) * 512], kc == 0, kc == 7,
                                    r=[("ytF", s)] + WOUT, w=[("ps", k)])
                        self.TTo("dve", ht[s][:, j, nh * 512:(nh + 1) * 512], ht[s][:, j, nh * 512:(nh + 1) * 512], ps[k][:, :], ALU.add,
                                 r=[("ps", k), ("F", "x", s)], w=[("F", "x", s)])
            self.rmsnorm_T(ht[s], 2, hn, hnT[s], (gcol, "gcolF"), "F", s, 0)
            HT = [("F", "hnT", s, kc) for kc in range(8)]
            def down(fc):
                q = fc % ND
                for j in range(2):
                    for nh in range(2):
                        k = 2 + j * 2 + nh
                        self.MM(ps[k][:, :], mm_[q][:, j * 128:(j + 1) * 128], wdn[:, fc, nh * 512:(nh + 1) * 512], fc == 0, fc == NFC - 1,
                                r=[("m", q), ("wdn", fc // 11)], w=[("ps", k)])
            halo_only = (l == 1 and t == 0)
            for fc in range(NFC):
                q = fc % ND
                qb = QB[q]
                pq = ps[qb]
                ca = (fc * 128) // 512
                cg = (DFF + fc * 128) // 512
                for kc in range(8):
                    self.MM(pq[:, 0:TT], wup[:, kc, fc * 128:(fc + 1) * 128], hnT[s][:, kc, :], kc == 0, kc == 7,
                            r=[("wup", ca)] + HT, w=[("psA", qb)])
                if halo_only:
                    self.ACT(abuf[q][:, 2:TT + 2], pq[:, 0:TT], AF.Copy, r=[("psA", qb)], w=[("abuf", q)])
                    self.CP("pool", ahalo[:, fc, :], abuf[q][:, TT:TT + 2], r=[("abuf", q)], w=[("ahalo", fc)])
                    continue
                for kc in range(8):
                    self.MM(pq[:, TT:2 * TT], wup[:, kc, DFF + fc * 128:DFF + (fc + 1) * 128], hnT[s][:, kc, :], kc == 0, kc == 7,
                            r=[("wup", cg)] + HT, w=[("psG", qb)])
                if fc >= 2:
                    down(fc - 2)
                self.CP("pool", abuf[q][:, 0:2], ahalo[:, fc, :], r=[("ahalo", fc)], w=[("abufh", q)])
                self.ACT(abuf[q][:, 2:TT + 2], pq[:, 0:TT], AF.Copy, r=[("psA", qb)], w=[("abuf", q)])
                self.ACT(t1[q][:], pq[:, 0:TT], AF.Identity, r=[("psA", qb), "cw", "cb"], w=[("t1", q)],
                         bias=cb[:, fc:fc + 1], scale=cw[:, 2, fc:fc + 1])
                self.STT("dve", t1[q][:], abuf[q][:, 1:TT + 1], cw[:, 1, fc:fc + 1], t1[q][:], ALU.mult, ALU.add,
                         r=[("abuf", q), ("abufh", q), ("t1", q)], w=[("t1", q)])
                self.STT("dve", t1[q][:], abuf[q][:, 0:TT], cw[:, 0, fc:fc + 1], t1[q][:], ALU.mult, ALU.add,
                         r=[("abuf", q), ("abufh", q), ("t1", q)], w=[("t1", q)])
                self.CP("pool", ahalo[:, fc, :], abuf[q][:, TT:TT + 2], r=[("abuf", q)], w=[("ahalo", fc)])
                self.ACT(t1[q][:], t1[q][:], AF.Gelu, r=[("t1", q)], w=[("t1", q)])
                self.TTo("dve", mm_[q][:], t1[q][:], pq[:, TT:2 * TT], ALU.mult, r=[("t1", q), ("psG", qb)], w=[("m", q)])
            if halo_only:
                continue
            down(NFC - 2)
            down(NFC - 1)
            for j in range(2):
                for nh in range(2):
                    k = 2 + j * 2 + nh
                    self.TTo("dve", ht[s][:, j, nh * 512:(nh + 1) * 512], ht[s][:, j, nh * 512:(nh + 1) * 512], ps[k][:, :], ALU.add,
                             r=[("ps", k), ("F", "x", s)], w=[("F", "x", s)])
            if l == 0:
                self.DMA("sp", dst_v[t], ht[s][:], ("hL0st", s), r=[("F", "x", s)], w=[("hL0", t)])
            elif t >= 1:
                for j in range(2):
                    self.ACT(self.junk[:], ht[s][:, j, :], AF.Square, r=[("F", "x", s)], w=["junk", ("fss", j)], accum=fss[:, j:j + 1])
                self.ACT(frs[:], fss[:], AF.Sqrt, r=[("fss", 0), ("fss", 1), "eps"], w=["frs"], bias=self.epsb[:], scale=1.0 / D)
                self.RECIP(frs[:], frs[:], r=["frs"], w=["frs"])
                for j in range(2):
                    self.STT("dve", outt[:], ht[s][:, j, :], frs[:, j:j + 1], fnbc[:], ALU.mult, ALU.mult,
                             r=[("F", "x", s), "frs", "fnbc"], w=["outt"])
                    st = self.DMA("sp", dst_v[t - 1][:, j, :], outt[:], "outst", r=["outt"], w=[("out", t, j)])
                    self.final_stores.append(st)
        S.pop()

    def phase_C(self):
        nc, S, I, C, ps = self.nc, self.S, self.I, self.C, self.ps
        psb = self.psbv[0]
        S.push()
        hn1T = S.sb("hn1T", [128, 8, NLOC], BF16)
        winO = S.sb("winO", [128, 8, 4608], BF16)
        wsrc = I["odd_w_in"].rearrange("(kc p) n -> p kc n", p=128)
        for c in range(9):
            self.DMA("pool", winO[:, :, c * 512:(c + 1) * 512], wsrc[:, :, c * 512:(c + 1) * 512], ("winO", c), w=[("winO", c)])
        gcol = S.sb("gcolC", [128, 8], F32)
        self.col_load(gcol[:], I["norm_mix"][1:2, :], "gcolC")
        posi = S.sb("posi", [128, 3, 32], I32)
        pv = I["pos"]
        self.DMA("sp", posi[:, 0, :], pv.rearrange("o (n a) -> a (o n)", a=128), "posi0", w=[("posi", 0)], slow=True)
        self.DMA("sp", posi[:, 1, :].rearrange("p (r n) -> p r n", r=4), pv.rearrange("o (n a r) -> a (o r) n", a=128, r=4),
                 "posi1", w=[("posi", 1)], slow=True)
        self.DMA("sp", posi[:, 2, :].rearrange("p (r n) -> p r n", r=16), pv.rearrange("o (n a r) -> a (o r) n", a=128, r=16),
                 "posi2", w=[("posi", 2)], slow=True)
        posf = S.sb("posf", [128, 96], F32)
        self.CP("dve", posf[:], posi[:].rearrange("p o n -> p (o n)"), r=[("posi", i) for i in range(3)], w=["posf"])
        ang = S.sb("ang", [128, 96, 8], F32)
        ki = S.sb("ki", [128, 96, 8], I32)
        kf = S.sb("kf", [128, 96, 8], F32)
        gt = S.sb("gt", [128, 96, 8], F32)
        cs = S.sb("cs", [128, 96, 8], F32)
        sn = S.sb("sn", [128, 96, 8], F32)
        posb = bass.AP(posf, 0, [[96, 128], [1, 96], [0, 8]])
        invb = bass.AP(C, C_INV, [[C_N, 128], [0, 96], [1, 8]])
        self.TTo("dve", ang[:], posb, invb, ALU.mult, r=["posf", "cst"], w=["ang"])
        for tbl, shift, nm in ((sn, 0.0, "sn"), (cs, 0.25, "cs")):
            self.TS("dve", kf[:], ang[:], 1.0 / (2 * np.pi), shift, ALU.mult, ALU.add, r=["ang"], w=["kf"])
            self.CP("dve", ki[:], kf[:], r=["kf"], w=["ki"])
            self.CP("dve", gt[:], ki[:], r=["ki"], w=["gt"])
            self.TTo("dve", kf[:], kf[:], gt[:], ALU.subtract, r=["kf", "gt"], w=["kf"])
            self.TS("dve", gt[:], kf[:], 0.5, None, ALU.is_gt, r=["kf"], w=["gt"])
            self.TTo("dve", kf[:], kf[:], gt[:], ALU.subtract, r=["kf", "gt"], w=["kf"])
            self.TS("dve", gt[:], kf[:], -0.5, None, ALU.is_lt, r=["kf"], w=["gt"])
            self.TTo("dve", kf[:], kf[:], gt[:], ALU.add, r=["kf", "gt"], w=["kf"])
            self.ACT(tbl[:], kf[:], AF.Sin, r=["kf"], w=[nm], scale=6.28318)
        ht = [S.sb(f"htC{i}", [128, 2, D], F32) for i in range(2)]
        hn = S.sb("hnC", [128, 2, D], BF16)
        src_v = self.hL0.rearrange("(t j p) d -> t p j d", p=128, j=2)
        NT = NLOC // TT
        self.DMA("sp", ht[0][:], src_v[0], ("htC", 0), w=[("C", "x", 0)])
        for t in range(NT):
            s = t % 2
            if t + 1 < NT:
                self.DMA("sp", ht[1 - s][:], src_v[t + 1], ("htC", 1 - s), w=[("C", "x", 1 - s)])
            self.rmsnorm_T(ht[s], 2, hn, hn1T[:, :, t * TT:(t + 1) * TT], (gcol, "gcolC"), "C", s, 0)
        S.barrier()
        qk = [S.sb(f"qk{i}", [128, 384], BF16) for i in range(2)]
        rts = [S.sb(f"rt{i}", [128, 6, 8], F32) for i in range(4)]
        stg = [S.sb(f"stg{i}", [128, 3, 128], BF16) for i in range(4)]
        vst = [S.sb(f"vst{i}", [128, 384], BF16) for i in range(2)]
        cnt = dict(g=0, st=0, v=0)

        pend = []

        def flush():
            while pend:
                pend.pop(0)()

        def proj_group(lhs_fn, col0, ncols, kind, tbl, dst):
            g = cnt["g"]; cnt["g"] += 1
            pt = ps[g % 6]
            wkeys = [("winO", c) for c in range(col0 // 512, (col0 + ncols - 1) // 512 + 1)]
            for kc in range(8):
                self.MM(pt[:, 0:ncols], lhs_fn(kc), winO[:, kc, col0:col0 + ncols], kc == 0, kc == 7, r=wkeys, w=[("ps", g % 6)], maxn=256)
            flush()
            if kind == "v":
                vi = cnt["v"] % 2; cnt["v"] += 1
                self.ACT(vst[vi][:, 0:ncols], pt[:, 0:ncols], AF.Copy, r=[("ps", g % 6)], w=[("vst", vi)])
                self.DMA("sp", dst, vst[vi][:, 0:ncols], ("vst", vi), r=[("vst", vi)], w=[])
                return
            qi = g % 2
            nh = ncols // 64
            self.ACT(qk[qi][:, 0:ncols], pt[:, 0:ncols], AF.Copy, r=[("ps", g % 6)], w=[("qk", qi)])
            pv3 = pt[:, 0:ncols].rearrange("p (h e) -> p h e", e=64)
            qv3 = qk[qi][:, 0:ncols].rearrange("p (h e) -> p h e", e=64)
            x1, x2 = pv3[:, :, 0:8], pv3[:, :, 8:16]
            cb_ = bass.AP(cs, tbl * 8, [[768, 128], [0, nh], [1, 8]])
            sb_ = bass.AP(sn, tbl * 8, [[768, 128], [0, nh], [1, 8]])
            R = [rt[:, 0:nh, :] for rt in rts]
            self.TTo("dve", R[0], x1, cb_, ALU.mult, r=[("ps", g % 6), "cs"], w=["r0"])
            self.TTo("dve", R[1], x2, sb_, ALU.mult, r=[("ps", g % 6), "sn"], w=["r1"])
            self.TTo("dve", R[2], x2, cb_, ALU.mult, r=[("ps", g % 6), "cs"], w=["r2"])
            self.TTo("dve", R[3], x1, sb_, ALU.mult, r=[("ps", g % 6), "sn"], w=["r3"])
            self.TTo("dve", qv3[:, :, 0:8], R[0], R[1], ALU.subtract, r=["r0", "r1"], w=[("qk", qi)])
            self.TTo("dve", qv3[:, :, 8:16], R[2], R[3], ALU.add, r=["r2", "r3"], w=[("qk", qi)])
            nch = ncols // 128
            h = cnt["st"] % 2
            si = cnt["st"] % 4; cnt["st"] += 1

            def tail():
                pbh = self.psbv[h]
                for c in range(nch):
                    self.TR(pbh[:, c * 128:(c + 1) * 128], qk[qi][:, c * 128:(c + 1) * 128], r=[("qk", qi)], w=[("psb", h)])
                self.CP("act" if si % 2 else "dve", stg[si][:, 0:nch, :].rearrange("p c t -> p (c t)"), pbh[:, 0:nch * 128],
                        r=[("psb", h)], w=[("stg", si)])
                self.DMA("sp", dst, stg[si][:, 0:nch, :], ("stg", si), r=[("stg", si)], w=[])
            pend.append(tail)

        KTc_v = self.KTc.rearrange("(c p) t -> p c t", p=128)
        QTc_v = self.QTc.rearrange("(c p) t -> p c t", p=128)
        for blk in range(32):
            lf = lambda kc, blk=blk: hn1T[:, kc, blk * 128:(blk + 1) * 128]
            tbl = 0 * 32 + blk
            for gi in range(2):
                if blk >= OWN0 // 128:
                    qb = blk - OWN0 // 128
                    proj_group(lf, gi * 384, 384, "qk", tbl, QTc_v[:, gi * 3:(gi + 1) * 3, qb * 128:(qb + 1) * 128])
                proj_group(lf, 768 + gi * 384, 384, "qk", tbl, KTc_v[:, gi * 3:(gi + 1) * 3, blk * 128:(blk + 1) * 128])
                proj_group(lf, 1536 + gi * 384, 384, "v", tbl, self.Vc[blk * 128:(blk + 1) * 128, gi * 384:(gi + 1) * 384])
        for pi, d in enumerate((1, 4, 16)):
            L = NLOC // d
            nb = L // 128
            nk_lo = (13, 2, 0)[pi]
            nq_lo = (14, 3, 0)[pi]
            KTd_v = self.KTd[pi].rearrange("(c p) t -> p c t", p=128)
            QTd_v = self.QTd[pi].rearrange("(c p) t -> p c t", p=128)
            for r in range(d):
                for n in range(nk_lo, nb):
                    st0 = r + d * n * 128
                    lf = lambda kc, st0=st0, d=d: hn1T[:, kc, st0: st0 + d * 127 + 1: d]
                    tbl = pi * 32 + r * nb + n
                    u0 = r * L + n * 128
                    if n >= nq_lo:
                        proj_group(lf, 2304 + pi * 256, 256, "qk", tbl, QTd_v[:, :, u0:u0 + 128])
                    proj_group(lf, 3072 + pi * 256, 256, "qk", tbl, KTd_v[:, :, u0:u0 + 128])
                    proj_group(lf, 3840 + pi * 256, 256, "v", tbl, self.Vd[pi][u0:u0 + 128, :])
        flush()
        S.pop()

    def phase_D(self):
        nc, S, I, C, ps = self.nc, self.S, self.I, self.C, self.ps
        S.push()
        lqk = S.sb("lqk", [128, 4, 64], F32)
        for i, nm in enumerate(("lq1", "lk1", "lq2", "lk2")):
            self.DMA("sp", lqk[:, i, :], I[nm].partition_broadcast(128), ("lqk", i), w=[("lqk", i)])
        lp = S.sb("lp", [128, 2, 64], F32)
        lsum = S.sb("lsum", [128, 2], F32)
        neglam = S.sb("neglam", [128, 1], F32)
        for i in range(2):
            self.TTo("dve", lp[:, i, :], lqk[:, 2 * i, :], lqk[:, 2 * i + 1, :], ALU.mult, r=[("lqk", 2 * i), ("lqk", 2 * i + 1)], w=[("lp", i)])
            self.S.op("dve", lambda e, i=i: e.reduce_sum(out=lsum[:, i:i + 1], in_=lp[:, i, :], axis=AX.X), [("lp", i)], [("lsum", i)])
        self.ACT(lsum[:], lsum[:], AF.Exp, r=[("lsum", 0), ("lsum", 1)], w=["lexp"])
        self.TTo("dve", neglam[:], lsum[:, 1:2], lsum[:, 0:1], ALU.subtract, r=["lexp"], w=["neglam"])
        self.TS("dve", neglam[:], neglam[:], -LAMBDA_INIT, None, ALU.add, r=["neglam"], w=["neglam"])
        sgcol = S.sb("sgcol", [128, 1], F32)
        self.col_load(sgcol[:], I["subln"], "sgcol")
        self.TS("dve", sgcol[:], sgcol[:], 1.0 - LAMBDA_INIT, None, ALU.mult, r=["sgcol"], w=["sgcol"])

        KT = [S.sb(f"KTh{i}", [128, NLOC], BF16) for i in range(2)]
        QT = [[S.sb(f"QTh{i}_{sub}", [128, NOWN], BF16) for sub in range(2)] for i in range(2)]
        for i in range(2):
            for sub in range(2):
                self.MEMSET("pool", QT[i][sub][(1 - sub) * 64:(2 - sub) * 64, :], 0.0, w=[("QTz", i, sub)])
        V = [S.sb(f"Vh{i}", [128, 32, 128], BF16) for i in range(2)]
        E = [S.sb(f"E{i}", [128, 512], BF16) for i in range(3)]
        o_ = S.sb("o_", [128, 512], F32)
        sq = S.sb("sq", [128, 512], BF16)
        rs = S.sb("rs", [128, 512], F32)
        ycst = [S.sb(f"ycst{i}", [128, 512], BF16) for i in range(2)]
        Vc_v = self.Vc.rearrange("(n p) c -> p n c", p=128)

        def load(h):
            s = h % 2
            self.DMA("sp", KT[s][:], self.KTc[h * 128:(h + 1) * 128, :], ("KTh", s), w=[("KTh", s)])
            for sub in range(2):
                self.DMA("sp", QT[s][sub][sub * 64:(sub + 1) * 64, :], self.QTc[h * 128 + sub * 64:h * 128 + (sub + 1) * 64, :],
                         ("QTh", s), w=[("QTh", s, sub)])
            self.DMA("sp", V[s][:], Vc_v[:, :, h * 128:(h + 1) * 128], ("Vh", s), w=[("Vh", s)])
        load(0)
        supers = [(1792, 256), (2048, 512), (2560, 512), (3072, 512), (3584, 512)]
        oS = [S.sb(f"oS{i}", [128, 512], F32) for i in range(2)]
        rS = [S.sb(f"rS{i}", [128, 512], F32) for i in range(2)]
        state = dict(nst=0, pending=None)

        def fin_part1(nq):
            self.ACT(oS[0][:, 0:nq], ps[4][:, 0:nq], AF.Copy, r=[("ps", 4)], w=[("oS", 0)])
            self.TS("dve", rS[0][:, 0:nq], ps[6][:, 0:nq], 1e-30, None, ALU.max, r=[("ps", 6)], w=[("rS", 0)])
            self.ACT(oS[1][:, 0:nq], ps[5][:, 0:nq], AF.Copy, r=[("ps", 5)], w=[("oS", 1)])
            self.TS("dve", rS[1][:, 0:nq], ps[7][:, 0:nq], 1e-30, None, ALU.max, r=[("ps", 7)], w=[("rS", 1)])

        def fin_part2(h, qc0, nq):
            for sub in range(2):
                self.ACT(rS[sub][:, 0:nq], rS[sub][:, 0:nq], AF.Ln, r=[("rS", sub)], w=[("rS", sub)])
                self.ACT(rS[sub][:, 0:nq], rS[sub][:, 0:nq], AF.Exp, r=[("rS", sub)], w=[("rS", sub)], scale=-1.0)
                self.TTo("dve", oS[sub][:, 0:nq], oS[sub][:, 0:nq], rS[sub][:, 0:nq], ALU.mult, r=[("oS", sub), ("rS", sub)], w=[("oS", sub)])
            self.STT("dve", o_[:, 0:nq], oS[1][:, 0:nq], neglam[:, 0:1], oS[0][:, 0:nq], ALU.mult, ALU.add,
                     r=[("oS", 0), ("oS", 1), "neglam"], w=["o_"])
            self.ACT(sq[:, 0:nq], o_[:, 0:nq], AF.Square, r=["o_"], w=["sq"])
            self.MM(ps[3][:, 0:nq], self.onesb[:], sq[:, 0:nq], True, True, r=["onesb", "sq"], w=[("ps", 3)], maxn=256)
            self.ACT(rs[:, 0:nq], ps[3][:, 0:nq], AF.Ln, r=[("ps", 3), "eps"], w=["rs"], bias=self.epsb[:], scale=1.0 / 128)
            self.ACT(rs[:, 0:nq], rs[:, 0:nq], AF.Exp, r=["rs"], w=["rs"], scale=-0.5)
            yi = state["nst"] % 2
            state["nst"] += 1
            self.STT("dve", ycst[yi][:, 0:nq], o_[:, 0:nq], sgcol[:, 0:1], rs[:, 0:nq], ALU.mult, ALU.mult,
                     r=["o_", "sgcol", "rs"], w=[("ycst", yi)])
            self.DMA("sp", self.Y1T[h * 128:(h + 1) * 128, qc0:qc0 + nq], ycst[yi][:, 0:nq], ("ycst", yi), r=[("ycst", yi)], w=[])

        for h in range(6):
            s = h % 2
            if h + 1 < 6:
                load(h + 1)
            if h == 0:
                for i in range(32):
                    self.MM(ps[i % 3][:, :], self.onesb[:], KT[0][:, (i % 8) * 512:(i % 8 + 1) * 512], True, True,
                            r=["onesb", ("KTh", 0)], w=[("ps", i % 3)])
                if getattr(self, "wup_issue", None) is not None:
                    self.wup_issue()
                    self.wup_issue = None
            for (q0, nq) in supers:
                qc0 = q0 - OWN0
                nkb = (q0 + nq) // 128
                units = [(kb, sub) for kb in range(nkb) for sub in range(2)]

                def pv(i):
                    kb, sub = units[i]
                    ei = i % 3
                    n0 = max(0, kb * 128 - q0)
                    wd = nq - n0
                    self.MM(ps[4 + sub][:, n0:nq], V[s][:, kb, :], E[ei][:, 0:wd], kb == 0, kb == nkb - 1,
                            r=[("Vh", s), ("E", ei)], w=[("ps", 4 + sub)], maxn=256)
                    self.MM(ps[6 + sub][:, n0:nq], self.onesb[:], E[ei][:, 0:wd], kb == 0, kb == nkb - 1,
                            r=["onesb", ("E", ei)], w=[("ps", 6 + sub)], maxn=256)
                for i, (kb, sub) in enumerate(units):
                    n0 = max(0, kb * 128 - q0)
                    wd = nq - n0
                    diag = kb * 128 >= q0
                    ei = i % 3
                    sc = ps[ei]
                    self.MM(sc[:, 0:wd], KT[s][:, kb * 128:(kb + 1) * 128],
                            QT[s][sub][:, qc0 + n0:qc0 + nq], True, True,
                            r=[("KTh", s), ("QTh", s, sub), ("QTz", s, sub)], w=[("ps", ei)], maxn=256)
                    if i >= 2:
                        pv(i - 2)
                    bias = C[:, C_CTXB:C_CTXB + 1] if kb < 16 else self.zerob[:]
                    self.ACT(E[ei][:, 0:wd], sc[:, 0:wd], AF.Exp, r=[("ps", ei), "cst", "zerob"], w=[("E", ei)], bias=bias, scale=0.125)
                    if diag:
                        self.TTo("pool", E[ei][:, 0:128], E[ei][:, 0:128], self.Ub[:], ALU.mult, r=[("E", ei), "Ub"], w=[("E", ei)])
                    if i == 12 and state["pending"] is not None:
                        fin_part2(*state["pending"])
                        state["pending"] = None
                pv(len(units) - 2)
                pv(len(units) - 1)
                if state["pending"] is not None:
                    fin_part2(*state["pending"])
                    state["pending"] = None
                fin_part1(nq)
                state["pending"] = (h, qc0, nq)
        fin_part2(*state["pending"])
        S.pop()

    def phase_E(self):
        nc, S, I, C, ps = self.nc, self.S, self.I, self.C, self.ps
        S.push()
        M_LU = S.sb("M_LU", [128, 256], BF16)
        M_LvU = S.sb("M_LvU", [128, 256], BF16)
        M_LvUv = S.sb("M_LvUv", [128, 256], BF16)
        for (m, a, b, nm) in ((M_LU, self.Lb, self.Ub, "M_LU"), (M_LvU, self.Lvb, self.Ub, "M_LvU"), (M_LvUv, self.Lvb, self.Uvb, "M_LvUv")):
            self.CP("dve", m[:, 0:128], a[:], r=["Lb", "Lvb"], w=[(nm, 0)])
            self.CP("dve", m[:, 128:256], b[:], r=["Ub", "Uvb"], w=[(nm, 1)])
        MR = lambda nm: [(nm, 0), (nm, 1)]
        Kh = [S.sb(f"Kh{i}", [128, NLOC], BF16) for i in range(2)]
        Qh = [S.sb(f"Qh{i}", [128, NLOC], BF16) for i in range(2)]
        for i in range(2):
            self.MEMSET("pool", Kh[i][64:128, :], 0.0, w=[("Khz", i)])
            self.MEMSET("pool", Qh[i][64:128, :], 0.0, w=[("Qhz", i)])
        Vh = [S.sb(f"Vdh{i}", [128, 32, 64], BF16) for i in range(2)]
        E = [S.sb(f"Ed{i}", [128, 256], BF16) for i in range(2)]
        accn = S.sb("accn", [64, NOWN], F32)
        accd = S.sb("accd", [64, NOWN], F32)
        ydst = [S.sb(f"ydst{i}", [64, NOWN], BF16) for i in range(2)]
        jobs = [(hh, pi) for hh in range(4) for pi in range(3)]

        def load(ji):
            hh, pi = jobs[ji]
            s = ji % 2
            self.DMA("sp", Kh[s][0:64, :], self.KTd[pi][hh * 64:(hh + 1) * 64, :], ("Kh", s), w=[("Kh", s)])
            self.DMA("sp", Qh[s][0:64, :], self.QTd[pi][hh * 64:(hh + 1) * 64, :], ("Qh", s), w=[("Qh", s)])
            self.DMA("sp", Vh[s][:], self.Vd[pi].rearrange("(n p) c -> p n c", p=128)[:, :, hh * 64:(hh + 1) * 64], ("Vdh", s), w=[("Vdh", s)])
        load(0)
        bi = 0
        pend = []

        def flush():
            while pend:
                pend.pop(0)()
        for ji, (hh, pi) in enumerate(jobs):
            s = ji % 2
            if ji + 1 < len(jobs):
                load(ji + 1)
            d = (1, 4, 16)[pi]
            L = NLOC // d
            nb = L // 128
            nq_lo = (14, 3, 0)[pi]
            nctx = 16 // d
            for r in range(d):
                for n in range(nq_lo, nb):
                    u0 = r * L + n * 128
                    kbs = [n - 1, n] if n >= 1 else [n]
                    sl = bi % 2; bi += 1
                    sc = ps[sl]
                    for i, kbn in enumerate(kbs):
                        self.MM(sc[:, i * 128:(i + 1) * 128], Kh[s][:, r * L + kbn * 128: r * L + (kbn + 1) * 128], Qh[s][:, u0:u0 + 128],
                                i == 0, True, r=[("Kh", s), ("Qh", s), ("Khz", s), ("Qhz", s)], w=[("ps", sl)])
                    flush()
                    nw = len(kbs) * 128
                    self.ACT(E[sl][:, 0:nw], sc[:, 0:nw], AF.Exp, r=[("ps", sl)], w=[("Ed", sl)], scale=0.125)
                    if len(kbs) == 2:
                        pc, cc = (n - 1) < nctx, n < nctx
                        nm = "M_LvUv" if (pc and cc) else ("M_LvU" if pc else "M_LU")
                        mt = {"M_LU": M_LU, "M_LvU": M_LvU, "M_LvUv": M_LvUv}[nm]
                        self.TTo("pool", E[sl][:, 0:256], E[sl][:, 0:256], mt[:], ALU.mult, r=[("Ed", sl)] + MR(nm), w=[("Ed", sl)])
                    else:
                        self.TTo("pool", E[sl][:, 0:128], E[sl][:, 0:128], self.Uvb[:], ALU.mult, r=[("Ed", sl), "Uvb"], w=[("Ed", sl)])

                    def tail(s=s, sl=sl, kbs=kbs, r=r, n=n, nb=nb, d=d, pi=pi):
                        pn, pd = ps[2 + sl], ps[4 + sl]
                        for i, kbn in enumerate(kbs):
                            self.MM(pn[0:64, 0:128], Vh[s][:, r * nb + kbn, :], E[sl][:, i * 128:(i + 1) * 128], i == 0, i == len(kbs) - 1,
                                    r=[("Vdh", s), ("Ed", sl)], w=[("ps", 2 + sl)])
                        for i, kbn in enumerate(kbs):
                            self.MM(pd[0:64, 0:128], self.onesb[:, 0:64], E[sl][:, i * 128:(i + 1) * 128], i == 0, i == len(kbs) - 1,
                                    r=["onesb", ("Ed", sl)], w=[("ps", 4 + sl)])
                        a0 = max(0, OWN0 // d - n * 128)
                        if a0 >= 128:
                            return
                        col0 = r + d * (n * 128 + a0) - OWN0
                        cnt_ = 128 - a0
                        cols = slice(col0, col0 + d * (cnt_ - 1) + 1, d)
                        if pi == 0:
                            self.CP("act", accn[:, cols], pn[0:64, a0:128], r=[("ps", 2 + sl)], w=["accn"])
                            self.CP("dve", accd[:, cols], pd[0:64, a0:128], r=[("ps", 4 + sl)], w=["accd"])
                        else:
                            self.TTo("dve", accn[:, cols], accn[:, cols], pn[0:64, a0:128], ALU.add, r=[("ps", 2 + sl), "accn"], w=["accn"])
                            self.TTo("dve", accd[:, cols], accd[:, cols], pd[0:64, a0:128], ALU.add, r=[("ps", 4 + sl), "accd"], w=["accd"])
                    pend.append(tail)
            flush()
            if pi == 2:
                yi = hh % 2
                self.TS("dve", accd[:], accd[:], 1e-30, None, ALU.max, r=["accd"], w=["accd"])
                self.RECIP(accd[:], accd[:], r=["accd"], w=["accd"])
                self.TTo("dve", ydst[yi][:], accn[:], accd[:], ALU.mult, r=["accn", "accd"], w=[("ydst", yi)])
                self.DMA("sp", self.Y1T[768 + hh * 64:768 + (hh + 1) * 64, :], ydst[yi][:], ("ydst", yi), r=[("ydst", yi)], w=[])
        S.pop()


_CACHE = {}


def _consts(p):
    c = np.zeros((128, C_N), np.float32)
    pi_, fi = np.meshgrid(np.arange(128), np.arange(128), indexing="ij")
    c[:, C_ID:C_ID + 128] = (pi_ == fi)
    c[:, C_L:C_L + 128] = (fi <= pi_)
    c[:, C_U:C_U + 128] = (pi_ <= fi)
    c[:, C_VIS] = 1.0 if p == 1 else 0.0
    c[:, C_CTXB] = 0.0 if p == 1 else -30000.0
    t = np.arange(16)
    for g, w in enumerate((2, 4, 8, 16)):
        true_rc = 1.0 / np.minimum(t + 1, w)
        c[:, C_RCC + g * 16:C_RCC + (g + 1) * 16] = true_rc
        c[:, C_RCO + g * 16:C_RCO + (g + 1) * 16] = true_rc if p == 0 else 1.0 / w
    inv = np.float32(500000.0) ** (-(np.arange(0, 16, 2, dtype=np.float32)) / np.float32(16))
    c[:, C_INV:C_INV + 8] = inv.astype(np.float32)
    return c


def make_in_maps(inp):
    f = lambda a: np.ascontiguousarray(np.asarray(a))
    x = f(inp["x"]); pos = f(inp["positions"]).astype(np.int32)
    shared = dict(
        norm_mix=f(inp["norm_mix"]), norm_ffn=f(inp["norm_ffn"]), final_norm=f(inp["final_norm"]).reshape(1, D),
        even_w_in=f(inp["even_w_in"])[0], gmlp_v_gain=f(inp["gmlp_v_gain"]).reshape(1, 512),
        gmlp_w_s=f(inp["gmlp_w_s"])[0], gmlp_b_s=f(inp["gmlp_b_s"]).reshape(1, 512),
        pool_w=f(inp["pool_w"])[0], pool_scale=f(inp["pool_scale"]).reshape(1, 512),
        even_w_out=f(inp["even_w_out"])[0], odd_w_in=f(inp["odd_w_in"])[0],
        lq1=f(inp["lambda_q1"]).reshape(1, 64), lk1=f(inp["lambda_k1"]).reshape(1, 64),
        lq2=f(inp["lambda_q2"]).reshape(1, 64), lk2=f(inp["lambda_k2"]).reshape(1, 64),
        subln=f(inp["subln_gain"]).reshape(1, 128), odd_w_out=f(inp["odd_w_out"])[0],
        ffn_w_up=f(inp["ffn_w_up"]), ffn_conv_w=f(inp["ffn_conv_w"]).reshape(2, 3, DFF),
        ffn_conv_b=f(inp["ffn_conv_b"]), ffn_w_down=f(inp["ffn_w_down"]),
    )
    maps = []
    for core in range(8):
        b, p = core // 2, core % 2
        if p == 1:
            xin = x[b]
            ps_ = pos[b]
        else:
            xin = np.concatenate([np.zeros((2048, D), np.float32), x[b, :2048]], axis=0)
            ps_ = np.concatenate([np.zeros((2048,), np.int32), pos[b, :2048]], axis=0)
        m = dict(shared)
        m["xin"] = np.ascontiguousarray(xin)
        m["pos"] = np.ascontiguousarray(ps_.reshape(1, NLOC))
        m["cst"] = _consts(p)
        maps.append(m)
    return maps


def kernel(**inp):
    if "nc" not in _CACHE:
        _CACHE["nc"] = KB().build()
    nc = _CACHE["nc"]
    maps = make_in_maps(inp)
    res = run_bass_kernel_spmd(nc, maps, core_ids=list(range(8)))
    out = np.zeros((4, 4096, D), np.float32)
    for core in range(8):
        b, p = core // 2, core % 2
        out[b, p * 2048:(p + 1) * 2048] = res.results[core]["out"]
    return out
```

```python
import numpy as np
import concourse.bass as bass
import concourse.mybir as mybir

F32 = mybir.dt.float32
BF16 = mybir.dt.bfloat16
I32 = mybir.dt.int32
AF = mybir.ActivationFunctionType
ALU = mybir.AluOpType
AX = mybir.AxisListType

ENGS = ("pe", "act", "dve", "pool", "sp")
SEM_CHUNK = 20000
SBUF_LO = 16512
SBUF_HI = 229344


class Op:
    __slots__ = ("eng", "fn", "deps", "idx", "dma", "semkey", "need_inc", "semval", "semid")

    def __init__(self, eng, fn, dma, semkey):
        self.eng = eng
        self.fn = fn
        self.deps = []
        self.dma = dma
        self.semkey = semkey
        self.need_inc = False
        self.semval = None
        self.semid = None


class Sched:
    def __init__(self, nc, same_engine_sync=True):
        self.nc = nc
        self.ops = {e: [] for e in ENGS}
        self.res = {}
        self.banks = {}
        self.persist_keys = set()
        self.same = same_engine_sync
        self.pending_barrier = {e: [] for e in ENGS}
        self.all_dma_since_barrier = []
        self.sb_off = SBUF_LO
        self.sb_stack = []
        self.nalloc = 0

    def push(self):
        self.sb_stack.append(self.sb_off)

    def pop(self):
        self.sb_off = self.sb_stack.pop()

    def sb(self, name, shape, dtype):
        esz = 4 if dtype in (F32, I32) else 2
        n = 1
        for s in shape[1:]:
            n *= s
        nbytes = (n * esz + 63) // 64 * 64
        off = self.sb_off
        assert off + nbytes <= SBUF_HI, f"SBUF overflow allocating {name}: {off}+{nbytes}"
        self.sb_off = off + nbytes
        self.nalloc += 1
        return self.nc.alloc_sbuf_tensor_at(f"{name}_{self.nalloc}", list(shape), dtype, offset=off)

    @staticmethod
    def _bank(key):
        if isinstance(key, tuple) and isinstance(key[0], str) and key[0].startswith("ps"):
            if key[0] in ("ps", "psA", "psG"):
                return key[1]
            if key[0] == "psb":
                return 6 + key[1]
            return 6
        return None

    def op(self, eng, fn, reads=(), writes=(), dma=False, semkey=None, persist=False):
        o = Op(eng, fn, dma, semkey)
        if dma:
            assert semkey is not None
        deps = []
        banks = set()
        for k in list(reads) + list(writes):
            b = self._bank(k)
            if b is not None:
                banks.add(b)
        reads = [k for k in reads if self._bank(k) is None]
        writes = [k for k in writes if self._bank(k) is None]
        for b in banks:
            st = self.banks.setdefault(b, {})
            for e2, o2 in st.items():
                if e2 != eng:
                    deps.append(o2)
            st[eng] = o
        for r in reads:
            st = self.res.get(r)
            if st is not None and st[0] is not None:
                deps.append(st[0])
        for w in writes:
            st = self.res.get(w)
            if st is not None:
                if st[0] is not None:
                    deps.append(st[0])
                deps.extend(st[1])
        deps.extend(self.pending_barrier[eng])
        self.pending_barrier[eng] = []
        for r in reads:
            st = self.res.setdefault(r, [None, []])
            st[1].append(o)
        for w in writes:
            self.res[w] = [o, []]
        o.deps = deps
        o.idx = len(self.ops[eng])
        self.ops[eng].append(o)
        if dma and not persist:
            self.all_dma_since_barrier.append(o)
        if persist:
            self.persist_keys.update(writes)
        return o

    def barrier(self):
        lasts = []
        for e in ENGS:
            if self.ops[e]:
                lasts.append(self.ops[e][-1])
        lasts.extend(self.all_dma_since_barrier)
        self.all_dma_since_barrier = []
        for e in ENGS:
            self.pending_barrier[e] = list(lasts)
        self.res = {k: [v[0], []] for k, v in self.res.items() if k in self.persist_keys}
        self.banks = {}

    def emit(self, final_wait_ops=()):
        nc = self.nc
        for e in ENGS:
            for o in self.ops[e]:
                for d in o.deps:
                    if d.dma:
                        d.need_inc = True
                    elif d.eng != o.eng or (self.same and o.eng != "pe"):
                        d.need_inc = True
        for o in final_wait_ops:
            o.need_inc = True
        import contextlib
        stack = contextlib.ExitStack()
        sem_objs = {}

        def get_sem(key):
            if key not in sem_objs:
                sem_objs[key] = stack.enter_context(nc.semaphore(f"s{len(sem_objs)}"))
            return sem_objs[key]

        dma_counts = {}
        for e in ENGS:
            cnt = 0
            for o in self.ops[e]:
                if o.dma:
                    c = dma_counts.get(o.semkey, 0) + 16
                    dma_counts[o.semkey] = c
                    o.semid = ("dma", o.semkey)
                    o.semval = c
                    assert c < 60000, f"dma sem overflow {o.semkey}"
                elif o.need_inc:
                    o.semid = ("eng", e, cnt // SEM_CHUNK)
                    o.semval = cnt % SEM_CHUNK + 1
                    cnt += 1
        for e in ENGS:
            for o in self.ops[e]:
                if o.semid is not None and (o.dma or o.need_inc):
                    get_sem(o.semid)
        self.nsems = len(sem_objs)
        engmap = {"pe": "tensor", "act": "scalar", "dve": "vector", "pool": "gpsimd", "sp": "sync"}
        with stack:
            with nc.Block() as block:
                def make(e):
                    def body(eng):
                        waited = {}
                        for o in self.ops[e]:
                            need = {}
                            for d in o.deps:
                                if not d.dma and d.eng == e and (not self.same or e == "pe"):
                                    continue
                                sid = d.semid
                                if sid is None:
                                    continue
                                if d.semval > need.get(sid, 0):
                                    need[sid] = d.semval
                            for sid, v in need.items():
                                if waited.get(sid, 0) >= v:
                                    continue
                                eng.wait_ge(get_sem(sid), v)
                                waited[sid] = v
                            ins = o.fn(eng)
                            if o.dma:
                                ins.then_inc(get_sem(o.semid), 16)
                            elif o.need_inc:
                                ins.then_inc(get_sem(o.semid), 1)
                        if e == "sp":
                            for key, c in dma_counts.items():
                                eng.wait_ge(get_sem(("dma", key)), c)
                    return body
                for e in ENGS:
                    if self.ops[e] or e == "sp":
                        getattr(block, engmap[e])(make(e))
        return nc

from concourse.bass_utils import run_bass_kernel_spmd

D = 1024
TT = 256
NLOC = 4096
OWN0 = 1792
NOWN = NLOC - OWN0
DFF = 2816
NFC = 22
LAMBDA_INIT = 0.8 - 0.6 * float(np.exp(-0.3 * 1))
C_ID, C_L, C_U, C_VIS, C_CTXB, C_RCC, C_RCO, C_INV, C_N = 0, 128, 256, 384, 385, 386, 450, 514, 528


class KB:
    def __init__(self, debug=False, phases="ABCDEF"):
        self.phases = phases
        self.nc = nc = bass.Bass("TRN2", target_bir_lowering=False)
        self.S = Sched(nc)
        self.debug = debug
        self.ps = [nc.alloc_psum_tensor(f"ps{i}", [128, 512], F32) for i in range(8)]
        self.psbv = [self.ps[6][:].bitcast(BF16), self.ps[7][:].bitcast(BF16)]

    def MM(self, out, lhsT, rhs, start=True, stop=True, r=(), w=(), maxn=None):
        n = rhs.shape[-1]
        if maxn is not None and n > maxn:
            npc = -(-n // maxn)
            step = -(-n // npc)
            o = None
            for a in range(0, n, step):
                bnd = min(n, a + step)
                o_, r_ = out[:, a:bnd], rhs[:, a:bnd]
                st_ = start and a == 0
                o = self.S.op("pe", lambda e, o_=o_, r_=r_, st_=st_: e.matmul(o_, lhsT=lhsT, rhs=r_, start=st_, stop=stop,
                                                                         skip_group_check=True), r, w)
            return o
        return self.S.op("pe", lambda e: e.matmul(out, lhsT=lhsT, rhs=rhs, start=start, stop=stop), r, w)

    def TR(self, out, in_, r=(), w=()):
        idb = self.idb
        return self.S.op("pe", lambda e: e.transpose(out=out, in_=in_, identity=idb[:]), list(r) + ["idb"], w)

    def ACT(self, out, in_, func, r=(), w=(), bias=None, scale=None, accum=None):
        kw = {}
        if bias is not None:
            kw["bias"] = bias
        if scale is not None:
            kw["scale"] = scale
        if accum is not None:
            kw["accum_out"] = accum
        return self.S.op("act", lambda e: e.activation(out=out, in_=in_, func=func, **kw), r, w)

    def TS(self, eng, out, in0, s1, s2, op0, op1=None, r=(), w=()):
        if op1 is None:
            return self.S.op(eng, lambda e: e.tensor_scalar(out=out, in0=in0, scalar1=s1, scalar2=None, op0=op0), r, w)
        return self.S.op(eng, lambda e: e.tensor_scalar(out=out, in0=in0, scalar1=s1, scalar2=s2, op0=op0, op1=op1), r, w)

    def TTo(self, eng, out, in0, in1, op, r=(), w=()):
        return self.S.op(eng, lambda e: e.tensor_tensor(out=out, in0=in0, in1=in1, op=op), r, w)

    def STT(self, eng, out, in0, scalar, in1, op0, op1, r=(), w=()):
        return self.S.op(eng, lambda e: e.scalar_tensor_tensor(out=out, in0=in0, scalar=scalar, in1=in1, op0=op0, op1=op1), r, w)

    def CP(self, eng, out, in_, r=(), w=()):
        if eng == "act":
            return self.S.op(eng, lambda e: e.copy(out=out, in_=in_), r, w)
        return self.S.op(eng, lambda e: e.tensor_copy(out=out, in_=in_), r, w)

    def RECIP(self, out, in_, r=(), w=()):
        return self.S.op("dve", lambda e: e.reciprocal(out=out, in_=in_), r, w)

    def MEMSET(self, eng, ap, val, w=()):
        return self.S.op(eng, lambda e: e.memset(ap, val), (), w)

    def DMA(self, eng, out, in_, key, r=(), w=(), slow=False, persist=False):
        if slow:
            return self.S.op(eng, lambda e: e.dma_start(out=out, in_=in_, allow_slow_non_contiguous=True), r, w, dma=True, semkey=key)
        return self.S.op(eng, lambda e: e.dma_start(out=out, in_=in_), r, w, dma=True, semkey=key, persist=persist)

    def col_load(self, dst, src_row, key):
        self.DMA("sp", dst, src_row.rearrange("o (c p) -> p (o c)", p=128), key, w=[key], slow=True)

    def rmsnorm_T(self, xt, j_n, hn, hnT, gcol, tag, slot, ps_half):
        S = self.S
        ss, rstd, junk = self.ss, self.rstd, self.junk
        xr = (tag, "x", slot)
        for j in range(j_n):
            self.ACT(junk[:], xt[:, j, :], AF.Square, r=[xr], w=["junk", ("ss", j)], accum=ss[:, j:j + 1])
        self.ACT(rstd[:, 0:j_n], ss[:, 0:j_n], AF.Sqrt, r=[("ss", j) for j in range(j_n)] + ["eps"], w=["rstd"],
                 bias=self.epsb[:], scale=1.0 / D)
        self.RECIP(rstd[:, 0:j_n], rstd[:, 0:j_n], r=["rstd"], w=["rstd"])
        for j in range(j_n):
            self.TS("dve", hn[:, j, :], xt[:, j, :], rstd[:, j:j + 1], None, ALU.mult, r=[xr, "rstd"], w=[("hn", j)])
        for kc in range(8):
            h = (kc + ps_half) % 2
            psb = self.psbv[h]
            for j in range(j_n):
                self.TR(psb[:, j * 128:(j + 1) * 128], hn[:, j, kc * 128:(kc + 1) * 128],
                        r=[("hn", j)], w=[("psb", h)])
            rr = [("psb", h), gcol[1]]
            if kc % 2 == 0:
                self.ACT(hnT[:, kc, 0:j_n * 128], psb[:, 0:j_n * 128], AF.Copy, r=rr,
                         w=[(tag, "hnT", slot, kc)], scale=gcol[0][:, kc:kc + 1])
            else:
                self.TS("dve", hnT[:, kc, 0:j_n * 128], psb[:, 0:j_n * 128], gcol[0][:, kc:kc + 1], None,
                        ALU.mult, r=rr, w=[(tag, "hnT", slot, kc)])

    def build(self):
        nc, S = self.nc, self.S
        din = lambda name, shape, dt=F32: nc.dram_tensor(name, list(shape), dt, kind="ExternalInput").ap()
        scr = lambda name, shape, dt: nc.dram_tensor(name, list(shape), dt, kind="Internal").ap()
        I = self.I = dict(
            xin=din("xin", [NLOC, D]), pos=din("pos", [1, NLOC], I32), cst=din("cst", [128, C_N]),
            norm_mix=din("norm_mix", [2, D]), norm_ffn=din("norm_ffn", [2, D]), final_norm=din("final_norm", [1, D]),
            even_w_in=din("even_w_in", [D, 1536]), gmlp_v_gain=din("gmlp_v_gain", [1, 512]),
            gmlp_w_s=din("gmlp_w_s", [4, 128, 128]), gmlp_b_s=din("gmlp_b_s", [1, 512]),
            pool_w=din("pool_w", [4, 128, 128]), pool_scale=din("pool_scale", [1, 512]),
            even_w_out=din("even_w_out", [D, D]), odd_w_in=din("odd_w_in", [D, 4608]),
            lq1=din("lq1", [1, 64]), lk1=din("lk1", [1, 64]), lq2=din("lq2", [1, 64]), lk2=din("lk2", [1, 64]),
            subln=din("subln", [1, 128]), odd_w_out=din("odd_w_out", [D, D]),
            ffn_w_up=din("ffn_w_up", [2, D, 2 * DFF]), ffn_conv_w=din("ffn_conv_w", [2, 3, DFF]),
            ffn_conv_b=din("ffn_conv_b", [2, DFF]), ffn_w_down=din("ffn_w_down", [2, DFF, D]),
        )
        self.out = nc.dram_tensor("out", [2048, D], F32, kind="ExternalOutput").ap()
        dbgk = "ExternalOutput" if self.debug else "Internal"
        self.h1s = nc.dram_tensor("h1s", [NLOC, D], F32, kind=dbgk).ap()
        self.hL0 = nc.dram_tensor("hL0", [NLOC, D], F32, kind=dbgk).ap()
        self.KTc = scr("KTc", [768, NLOC], BF16)
        self.QTc = scr("QTc", [768, NOWN], BF16)
        self.Vc = scr("Vc", [NLOC, 768], BF16)
        self.KTd = scr("KTd", [3, 256, NLOC], BF16)
        self.QTd = scr("QTd", [3, 256, NLOC], BF16)
        self.Vd = scr("Vd", [3, NLOC, 256], BF16)
        self.Y1T = nc.dram_tensor("Y1T", [D, NOWN], BF16, kind=dbgk).ap()

        C = self.C = S.sb("cst", [128, C_N], F32)
        self.DMA("sp", C[:], I["cst"], "cst", w=["cst"])
        self.idb = S.sb("idb", [128, 128], BF16)
        self.Lb = S.sb("Lb", [128, 128], BF16)
        self.Ub = S.sb("Ub", [128, 128], BF16)
        self.Lvb = S.sb("Lvb", [128, 128], BF16)
        self.Uvb = S.sb("Uvb", [128, 128], BF16)
        self.onesb = S.sb("onesb", [128, 128], BF16)
        self.epsb = S.sb("epsb", [128, 1], F32)
        self.zerob = S.sb("zerob", [128, 1], F32)
        self.ss = S.sb("ss", [128, 4], F32)
        self.rstd = S.sb("rstd", [128, 4], F32)
        self.junk = S.sb("junk", [128, 1024], BF16)
        self.CP("dve", self.idb[:], C[:, C_ID:C_ID + 128], r=["cst"], w=["idb"])
        self.CP("dve", self.Lb[:], C[:, C_L:C_L + 128], r=["cst"], w=["Lb"])
        self.CP("dve", self.Ub[:], C[:, C_U:C_U + 128], r=["cst"], w=["Ub"])
        self.TS("dve", self.Lvb[:], C[:, C_L:C_L + 128], C[:, C_VIS:C_VIS + 1], None, ALU.mult, r=["cst"], w=["Lvb"])
        self.TS("dve", self.Uvb[:], C[:, C_U:C_U + 128], C[:, C_VIS:C_VIS + 1], None, ALU.mult, r=["cst"], w=["Uvb"])
        self.MEMSET("pool", self.onesb[:], 1.0, w=["onesb"])
        self.MEMSET("pool", self.epsb[:], 1e-6, w=["eps"])
        self.MEMSET("pool", self.zerob[:], 0.0, w=["zerob"])
        S.barrier()

        ph = self.phases
        off0 = S.sb_off
        wup0 = self.prefetch_wup(0) if "B" in ph else None
        if "A" in ph:
            self.phase_A()
            S.barrier()
        if "B" in ph:
            self.phase_ffn(0, wup0)
            S.barrier()
        S.sb_off = off0
        if "C" in ph:
            self.phase_C()
            S.barrier()
        wup1 = self.prefetch_wup(1) if "F" in ph else None
        if "D" in ph:
            self.phase_D()
            S.barrier()
        if "E" in ph:
            self.phase_E()
            S.barrier()
        if "F" in ph:
            self.phase_ffn(1, wup1)
        S.emit()
        return nc

    def phase_A(self):
        nc, S, I, C, ps = self.nc, self.S, self.I, self.C, self.ps
        psb = self.psbv[0]
        S.push()
        win = S.sb("winA", [128, 8, 1536], BF16)
        wout = S.sb("woutA", [128, 8, 1024], BF16)
        for c in range(3):
            self.DMA("pool", win[:, :, c * 512:(c + 1) * 512],
                     I["even_w_in"].rearrange("(kc p) n -> p kc n", p=128)[:, :, c * 512:(c + 1) * 512],
                     ("winA", c), w=[("winA", c)])
        for c in range(2):
            self.DMA("pool", wout[:, :, c * 512:(c + 1) * 512],
                     I["even_w_out"].rearrange("(kc p) n -> p kc n", p=128)[:, :, c * 512:(c + 1) * 512],
                     ("woutA", c), w=[("woutA", c)])
        WIN = [("winA", c) for c in range(3)]
        WOUT = [("woutA", c) for c in range(2)]
        poolw = S.sb("poolw", [128, 4, 128], BF16)
        self.DMA("pool", poolw[:], I["pool_w"].rearrange("g c d -> c g d"), "poolw", w=["poolw"])
        bsrow = S.sb("bsrow", [1, 512], BF16)
        self.DMA("pool", bsrow[:], I["gmlp_b_s"], "bsrow", w=["bsrow"])
        wsf = S.sb("wsf", [128, 4, 128], F32)
        self.DMA("sp", wsf[:], I["gmlp_w_s"].rearrange("g t s -> t g s"), "wsf", w=["wsf"])
        wsb = S.sb("wsb", [128, 4, 128], BF16)
        WmT = S.sb("WmT", [128, 4, 128], BF16)
        for g in range(4):
            self.TTo("dve", wsb[:, g, :], wsf[:, g, :], C[:, C_L:C_L + 128], ALU.mult, r=["wsf"], w=[("wsb", g)])
            self.TR(psb[:, g * 128:(g + 1) * 128], wsb[:, g, :], r=[("wsb", g)], w=[("psbw", g)])
        self.CP("dve", WmT[:].rearrange("p g t -> p (g t)"), psb[:, 0:512], r=[("psbw", g) for g in range(4)], w=["WmT"])
        vgbc = S.sb("vgbc", [128, 512], F32)
        self.DMA("sp", vgbc[:], I["gmlp_v_gain"].partition_broadcast(128), "vgbc", w=["vgbc"])
        pscol = S.sb("pscol", [128, 4], F32)
        self.col_load(pscol[:], I["pool_scale"], "pscol")
        gcol = S.sb("gcolA", [128, 8], F32)
        self.col_load(gcol[:], I["norm_mix"][0:1, :], "gcolA")
        S.barrier()

        xt = [S.sb(f"xtA{i}", [128, 2, D], F32) for i in range(2)]
        hn = S.sb("hnA", [128, 2, D], BF16)
        hnT = [S.sb(f"hnTA{i}", [128, 8, TT], BF16) for i in range(2)]
        uT = S.sb("uT", [128, 4, TT], BF16)
        pbuf = [S.sb(f"pbuf{i}", [128, 4, 16 + TT], F32) for i in range(2)]
        tmpA = S.sb("tmpA", [128, 16 + TT], F32)
        tmpB = S.sb("tmpB", [128, 16 + TT], F32)
        tmpf = S.sb("tmpf", [128, 16], F32)
        pooled = S.sb("pooled", [128, 4, TT], BF16)
        vg = S.sb("vg", [128, 512], F32)
        vss = S.sb("vss", [128, 2], F32)
        vr = S.sb("vr", [128, 2], F32)
        vn = S.sb("vn", [128, 2, 512], BF16)
        yT = S.sb("yTA", [128, 8, TT], BF16)
        W = 16 + TT
        xin_v = I["xin"].rearrange("(t j p) d -> t p j d", p=128, j=2)
        h1s_v = self.h1s.rearrange("(t j p) d -> t p j d", p=128, j=2)
        NT = NLOC // TT
        self.MEMSET("pool", pbuf[1][:, :, TT:TT + 16], 0.0, w=[("pbuf", 1, g) for g in range(4)])
        self.DMA("sp", xt[0][:], xin_v[0], ("xtA", 0), w=[("A", "x", 0)])
        for t in range(NT):
            s = t % 2
            if t + 1 < NT:
                self.DMA("sp", xt[1 - s][:], xin_v[t + 1], ("xtA", 1 - s), w=[("A", "x", 1 - s)])
            if t == 1 and getattr(self, "wup_issue", None) is not None:
                self.wup_issue()
                self.wup_issue = None
            self.rmsnorm_T(xt[s], 2, hn, hnT[s], (gcol, "gcolA"), "A", s, 0)
            HT = [("A", "hnT", s, kc) for kc in range(8)]
            for fc in range(4):
                pt = ps[fc % 2]
                for kc in range(8):
                    self.MM(pt[:, 0:TT], win[:, kc, fc * 128:(fc + 1) * 128], hnT[s][:, kc, :], kc == 0, kc == 7,
                            r=WIN + HT, w=[("ps", fc % 2)])
                self.ACT(uT[:, fc, :], pt[:, 0:TT], AF.Gelu, r=[("ps", fc % 2)], w=[("uT", fc)])
            for g in range(4):
                pt = ps[2 + g % 2]
                for kc in range(8):
                    self.MM(pt[:, 0:TT], win[:, kc, 1024 + g * 128:1024 + (g + 1) * 128], hnT[s][:, kc, :], kc == 0, kc == 7,
                            r=WIN + HT, w=[("ps", 2 + g % 2)])
                self.ACT(pbuf[s][:, g, 16:W], pt[:, 0:TT], AF.Copy, r=[("ps", 2 + g % 2)], w=[("pbuf", s, g)])
                self.CP("pool", pbuf[s][:, g, 0:16], pbuf[1 - s][:, g, TT:W], r=[("pbuf", 1 - s, g)], w=[("pbufh", s, g)])
            for j in range(2):
                pt = ps[4 + j]
                for kc in range(8):
                    self.MM(pt[:, :], hnT[s][:, kc, j * 128:(j + 1) * 128], win[:, kc, 512:1024], kc == 0, kc == 7,
                            r=WIN + HT, w=[("ps", 4 + j)])
                self.ACT(vg[:], pt[:, :], AF.Gelu, r=[("ps", 4 + j)], w=["vg"])
                self.ACT(self.junk[:, 0:512], vg[:], AF.Square, r=["vg"], w=["junk", ("vss", j)], accum=vss[:, j:j + 1])
                self.ACT(vr[:, j:j + 1], vss[:, j:j + 1], AF.Sqrt, r=[("vss", j), "eps"], w=[("vr", j)], bias=self.epsb[:], scale=1.0 / 512)
                self.RECIP(vr[:, j:j + 1], vr[:, j:j + 1], r=[("vr", j)], w=[("vr", j)])
                self.STT("dve", vn[:, j, :], vg[:], vr[:, j:j + 1], vgbc[:], ALU.mult, ALU.mult, r=["vg", ("vr", j), "vgbc"], w=[("vn", j)])
            for g in range(4):
                pt = ps[g % 2]
                for j in range(2):
                    self.MM(pt[:, j * 128:(j + 1) * 128], vn[:, j, g * 128:(g + 1) * 128], WmT[:, g, :], True, False,
                            r=[("vn", j), "WmT"], w=[("ps", g % 2)])
                    self.MM(pt[:, j * 128:(j + 1) * 128], self.onesb[0:1, :], bsrow[0:1, g * 128:(g + 1) * 128], False, True,
                            r=["onesb", "bsrow"], w=[("ps", g % 2)])
                self.TTo("dve", yT[:, g, :], pt[:, 0:TT], uT[:, g, :], ALU.mult, r=[("ps", g % 2), ("uT", g)], w=[("yT", g)])
            for g in range(4):
                cur = pbuf[s][:, g, :]
                rr = [("pbuf", s, g), ("pbufh", s, g)]
                src, lag = cur, 1
                bufs = [tmpA, tmpB]
                for step in range(g + 1):
                    dst = bufs[step % 2]
                    lo = 2 * lag - 1
                    srcr = rr if step == 0 else [("ptmp", (step - 1) % 2)]
                    self.TTo("pool", dst[:, lo:W], src[:, lo:W], src[:, lo - lag:W - lag], ALU.add, r=srcr, w=[("ptmp", step % 2)])
                    src = dst
                    lag *= 2
                wdw = 2 ** (g + 1)
                last = [("ptmp", g % 2)]
                self.STT("dve", pooled[:, g, :], src[:, 16:W], 1.0 / wdw, cur[:, 16:W], ALU.mult, ALU.subtract, r=last + rr, w=[("pooled", g)])
                if t == 0 or t == 2048 // TT:
                    off = C_RCC if t == 0 else C_RCO
                    self.TTo("dve", tmpf[:], src[:, 16:32], C[:, off + g * 16: off + (g + 1) * 16], ALU.mult, r=last, w=["tmpf"])
                    self.TTo("dve", pooled[:, g, 0:16], tmpf[:], cur[:, 16:32], ALU.subtract, r=["tmpf"] + rr, w=[("pooled", g)])
                pt = ps[2 + g % 2]
                self.MM(pt[:, 0:TT], poolw[:, g, :], pooled[:, g, :], True, True, r=["poolw", ("pooled", g)], w=[("ps", 2 + g % 2)])
                self.ACT(yT[:, 4 + g, :], pt[:, 0:TT], AF.Copy, r=[("ps", 2 + g % 2), "pscol"], w=[("yT", 4 + g)], scale=pscol[:, g:g + 1])
            YT = [("yT", k) for k in range(8)]
            for j in range(2):
                for nh in range(2):
                    pt = ps[4 + (j * 2 + nh) % 2]
                    for kc in range(8):
                        self.MM(pt[:, :], yT[:, kc, j * 128:(j + 1) * 128], wout[:, kc, nh * 512:(nh + 1) * 512], kc == 0, kc == 7,
                                r=YT + WOUT, w=[("ps", 4 + (j * 2 + nh) % 2)])
                    self.TTo("dve", xt[s][:, j, nh * 512:(nh + 1) * 512], xt[s][:, j, nh * 512:(nh + 1) * 512], pt[:, :], ALU.add,
                             r=[("ps", 4 + (j * 2 + nh) % 2), ("A", "x", s)], w=[("A", "x", s)])
            self.DMA("sp", h1s_v[t], xt[s][:], ("h1st", s), r=[("A", "x", s)], w=[("h1s", t)])
        S.pop()

    def prefetch_wup(self, l):
        S, I = self.S, self.I
        wup = S.sb(f"wup{l}", [128, 8, 2 * DFF], BF16)
        wup_src = I["ffn_w_up"][l].rearrange("(kc p) n -> p kc n", p=128)

        def issue():
            for c in [0, 5, 6, 1, 7, 2, 8, 3, 9, 4, 10]:
                self.DMA("pool", wup[:, :, c * 512:(c + 1) * 512], wup_src[:, :, c * 512:(c + 1) * 512], ("wup", l, c), w=[("wup", c)], persist=True)
        self.wup_issue = issue
        return wup

    def phase_ffn(self, l, wup):
        nc, S, I, C, ps = self.nc, self.S, self.I, self.C, self.ps
        psb = self.psbv[0]
        if getattr(self, "wup_issue", None) is not None:
            self.wup_issue()
            self.wup_issue = None
        S.push()
        wdn = S.sb("wdn", [128, NFC, D], BF16)
        wdn_src = I["ffn_w_down"][l].rearrange("(fc p) n -> p fc n", p=128)
        for c in range(2):
            self.DMA("pool", wdn[:, c * 11:(c + 1) * 11, :], wdn_src[:, c * 11:(c + 1) * 11, :], ("wdn", c), w=[("wdn", c)])
        cw = S.sb("cw", [128, 3, NFC], F32)
        self.DMA("sp", cw[:], I["ffn_conv_w"][l].rearrange("j (fc p) -> p j fc", p=128), "cw", w=["cw"], slow=True)
        cb = S.sb("cb", [128, NFC], F32)
        self.col_load(cb[:], I["ffn_conv_b"][l:l + 1, :], "cb")
        gcol = S.sb("gcolF", [128, 8], F32)
        self.col_load(gcol[:], I["norm_ffn"][l:l + 1, :], "gcolF")
        ahalo = S.sb("ahalo", [128, NFC, 2], F32)
        self.MEMSET("pool", ahalo[:], 0.0, w=[("ahalo", fc) for fc in range(NFC)])
        if l == 1:
            wout = S.sb("woutO", [128, 8, D], BF16)
            for c in range(2):
                self.DMA("pool", wout[:, :, c * 512:(c + 1) * 512],
                         I["odd_w_out"].rearrange("(kc p) n -> p kc n", p=128)[:, :, c * 512:(c + 1) * 512],
                         ("woutO", c), w=[("woutO", c)])
            WOUT = [("woutO", c) for c in range(2)]
            fnbc = S.sb("fnbc", [128, D], F32)
            self.DMA("sp", fnbc[:], I["final_norm"].partition_broadcast(128), "fnbc", w=["fnbc"])
            yt = [S.sb(f"ytF{i}", [128, 8, TT], BF16) for i in range(2)]
            outt = S.sb("outt", [128, D], F32)
            fss = S.sb("fss", [128, 2], F32)
            frs = S.sb("frs", [128, 2], F32)
        ht = [S.sb(f"htF{i}", [128, 2, D], F32) for i in range(2)]
        hn = S.sb("hnF", [128, 2, D], BF16)
        hnT = [S.sb(f"hnTF{i}", [128, 8, TT], BF16) for i in range(2)]
        ND = 4 if l == 0 else 3
        QB = (0, 1, 6, 7)[:ND]
        abuf = [S.sb(f"abuf{i}", [128, TT + 2], F32) for i in range(ND)]
        t1 = [S.sb(f"t1{i}", [128, TT], F32) for i in range(ND)]
        mm_ = [S.sb(f"m{i}", [128, TT], BF16) for i in range(ND)]
        if l == 0:
            NT = NLOC // TT
            src_v = self.h1s.rearrange("(t j p) d -> t p j d", p=128, j=2)
            dst_v = self.hL0.rearrange("(t j p) d -> t p j d", p=128, j=2)
            src_res = lambda t: [("h1s", t)]
        else:
            NT = NOWN // TT
            src_v = self.hL0[OWN0:NLOC, :].rearrange("(t j p) d -> t p j d", p=128, j=2)
            dst_v = self.out.rearrange("(t j p) d -> t p j d", p=128, j=2)
            src_res = lambda t: []
            y_v = self.Y1T.rearrange("(kc p) n -> p kc n", p=128)
        self.final_stores = []

        def load(t):
            s = t % 2
            self.DMA("sp", ht[s][:], src_v[t], ("htF", s), r=src_res(t), w=[("F", "x", s)])
            if l == 1:
                self.DMA("sp", yt[s][:], y_v[:, :, t * TT:(t + 1) * TT], ("ytF", s), w=[("ytF", s)])
        load(0)
        for t in range(NT):
            s = t % 2
            if t + 1 < NT:
                load(t + 1)
            if l == 1:
                for j in range(2):
                    for nh in range(2):
                        k = 2 + j * 2 + nh
                        for kc in range(8):
                            self.MM(ps[k][:, :], yt[s][:, kc, j * 128:(j + 1) * 128], wout[:, kc, nh * 512:(nh + 1) * 512], kc == 0, kc == 7,
                                    r=[("ytF", s)] + WOUT, w=[("ps", k)])
                        self.TTo("dve", ht[s][:, j, nh * 512:(nh + 1) * 512], ht[s][:, j, nh * 512:(nh + 1) * 512], ps[k][:, :], ALU.add,
                                 r=[("ps", k), ("F", "x", s)], w=[("F", "x", s)])
            self.rmsnorm_T(ht[s], 2, hn, hnT[s], (gcol, "gcolF"), "F", s, 0)
            HT = [("F", "hnT", s, kc) for kc in range(8)]
            def down(fc):
                q = fc % ND
                for j in range(2):
                    for nh in range(2):
                        k = 2 + j * 2 + nh
                        self.MM(ps[k][:, :], mm_[q][:, j * 128:(j + 1) * 128], wdn[:, fc, nh * 512:(nh + 1) * 512], fc == 0, fc == NFC - 1,
                                r=[("m", q), ("wdn", fc // 11)], w=[("ps", k)])
            halo_only = (l == 1 and t == 0)
            for fc in range(NFC):
                q = fc % ND
                qb = QB[q]
                pq = ps[qb]
                ca = (fc * 128) // 512
                cg = (DFF + fc * 128) // 512
                for kc in range(8):
                    self.MM(pq[:, 0:TT], wup[:, kc, fc * 128:(fc + 1) * 128], hnT[s][:, kc, :], kc == 0, kc == 7,
                            r=[("wup", ca)] + HT, w=[("psA", qb)])
                if halo_only:
                    self.ACT(abuf[q][:, 2:TT + 2], pq[:, 0:TT], AF.Copy, r=[("psA", qb)], w=[("abuf", q)])
                    self.CP("pool", ahalo[:, fc, :], abuf[q][:, TT:TT + 2], r=[("abuf", q)], w=[("ahalo", fc)])
                    continue
                for kc in range(8):
                    self.MM(pq[:, TT:2 * TT], wup[:, kc, DFF + fc * 128:DFF + (fc + 1) * 128], hnT[s][:, kc, :], kc == 0, kc == 7,
                            r=[("wup", cg)] + HT, w=[("psG", qb)])
                if fc >= ND - 1:
                    down(fc - (ND - 1))
                self.CP("pool", abuf[q][:, 0:2], ahalo[:, fc, :], r=[("ahalo", fc)], w=[("abufh", q)])
                self.ACT(abuf[q][:, 2:TT + 2], pq[:, 0:TT], AF.Copy, r=[("psA", qb)], w=[("abuf", q)])
                self.ACT(t1[q][:], pq[:, 0:TT], AF.Identity, r=[("psA", qb), "cw", "cb"], w=[("t1", q)],
                         bias=cb[:, fc:fc + 1], scale=cw[:, 2, fc:fc + 1])
                self.STT("dve", t1[q][:], abuf[q][:, 1:TT + 1], cw[:, 1, fc:fc + 1], t1[q][:], ALU.mult, ALU.add,
                         r=[("abuf", q), ("abufh", q), ("t1", q)], w=[("t1", q)])
                self.STT("dve", t1[q][:], abuf[q][:, 0:TT], cw[:, 0, fc:fc + 1], t1[q][:], ALU.mult, ALU.add,
                         r=[("abuf", q), ("abufh", q), ("t1", q)], w=[("t1", q)])
                self.CP("pool", ahalo[:, fc, :], abuf[q][:, TT:TT + 2], r=[("abuf", q)], w=[("ahalo", fc)])
                self.ACT(t1[q][:], t1[q][:], AF.Gelu, r=[("t1", q)], w=[("t1", q)])
                self.TTo("dve", mm_[q][:], t1[q][:], pq[:, TT:2 * TT], ALU.mult, r=[("t1", q), ("psG", qb)], w=[("m", q)])
            if halo_only:
                continue
            for k_ in range(ND - 1, 0, -1):
                down(NFC - k_)
            for j in range(2):
                for nh in range(2):
                    k = 2 + j * 2 + nh
                    self.TTo("dve", ht[s][:, j, nh * 512:(nh + 1) * 512], ht[s][:, j, nh * 512:(nh + 1) * 512], ps[k][:, :], ALU.add,
                             r=[("ps", k), ("F", "x", s)], w=[("F", "x", s)])
            if l == 0:
                self.DMA("sp", dst_v[t], ht[s][:], ("hL0st", s), r=[("F", "x", s)], w=[("hL0", t)])
            elif t >= 1:
                for j in range(2):
                    self.ACT(self.junk[:], ht[s][:, j, :], AF.Square, r=[("F", "x", s)], w=["junk", ("fss", j)], accum=fss[:, j:j + 1])
                self.ACT(frs[:], fss[:], AF.Sqrt, r=[("fss", 0), ("fss", 1), "eps"], w=["frs"], bias=self.epsb[:], scale=1.0 / D)
                self.RECIP(frs[:], frs[:], r=["frs"], w=["frs"])
                for j in range(2):
                    self.STT("dve", outt[:], ht[s][:, j, :], frs[:, j:j + 1], fnbc[:], ALU.mult, ALU.mult,
                             r=[("F", "x", s), "frs", "fnbc"], w=["outt"])
                    st = self.DMA("sp", dst_v[t - 1][:, j, :], outt[:], "outst", r=["outt"], w=[("out", t, j)])
                    self.final_stores.append(st)
        S.pop()

    def phase_C(self):
        nc, S, I, C, ps = self.nc, self.S, self.I, self.C, self.ps
        psb = self.psbv[0]
        S.push()
        hn1T = S.sb("hn1T", [128, 8, NLOC], BF16)
        winO = S.sb("winO", [128, 8, 4608], BF16)
        wsrc = I["odd_w_in"].rearrange("(kc p) n -> p kc n", p=128)
        for c in range(9):
            self.DMA("pool", winO[:, :, c * 512:(c + 1) * 512], wsrc[:, :, c * 512:(c + 1) * 512], ("winO", c), w=[("winO", c)])
        gcol = S.sb("gcolC", [128, 8], F32)
        self.col_load(gcol[:], I["norm_mix"][1:2, :], "gcolC")
        posi = S.sb("posi", [128, 3, 32], I32)
        pv = I["pos"]
        self.DMA("sp", posi[:, 0, :], pv.rearrange("o (n a) -> a (o n)", a=128), "posi0", w=[("posi", 0)], slow=True)
        self.DMA("sp", posi[:, 1, :].rearrange("p (r n) -> p r n", r=4), pv.rearrange("o (n a r) -> a (o r) n", a=128, r=4),
                 "posi1", w=[("posi", 1)], slow=True)
        self.DMA("sp", posi[:, 2, :].rearrange("p (r n) -> p r n", r=16), pv.rearrange("o (n a r) -> a (o r) n", a=128, r=16),
                 "posi2", w=[("posi", 2)], slow=True)
        posf = S.sb("posf", [128, 96], F32)
        self.CP("dve", posf[:], posi[:].rearrange("p o n -> p (o n)"), r=[("posi", i) for i in range(3)], w=["posf"])
        ang = S.sb("ang", [128, 96, 8], F32)
        ki = S.sb("ki", [128, 96, 8], I32)
        kf = S.sb("kf", [128, 96, 8], F32)
        gt = S.sb("gt", [128, 96, 8], F32)
        cs = S.sb("cs", [128, 96, 8], F32)
        sn = S.sb("sn", [128, 96, 8], F32)
        posb = bass.AP(posf, 0, [[96, 128], [1, 96], [0, 8]])
        invb = bass.AP(C, C_INV, [[C_N, 128], [0, 96], [1, 8]])
        self.TTo("dve", ang[:], posb, invb, ALU.mult, r=["posf", "cst"], w=["ang"])
        for tbl, shift, nm in ((sn, 0.0, "sn"), (cs, 0.25, "cs")):
            self.TS("dve", kf[:], ang[:], 1.0 / (2 * np.pi), shift, ALU.mult, ALU.add, r=["ang"], w=["kf"])
            self.CP("dve", ki[:], kf[:], r=["kf"], w=["ki"])
            self.CP("dve", gt[:], ki[:], r=["ki"], w=["gt"])
            self.TTo("dve", kf[:], kf[:], gt[:], ALU.subtract, r=["kf", "gt"], w=["kf"])
            self.TS("dve", gt[:], kf[:], 0.5, None, ALU.is_gt, r=["kf"], w=["gt"])
            self.TTo("dve", kf[:], kf[:], gt[:], ALU.subtract, r=["kf", "gt"], w=["kf"])
            self.TS("dve", gt[:], kf[:], -0.5, None, ALU.is_lt, r=["kf"], w=["gt"])
            self.TTo("dve", kf[:], kf[:], gt[:], ALU.add, r=["kf", "gt"], w=["kf"])
            self.ACT(tbl[:], kf[:], AF.Sin, r=["kf"], w=[nm], scale=6.28318)
        ht = [S.sb(f"htC{i}", [128, 2, D], F32) for i in range(2)]
        hn = S.sb("hnC", [128, 2, D], BF16)
        src_v = self.hL0.rearrange("(t j p) d -> t p j d", p=128, j=2)
        NT = NLOC // TT
        self.DMA("sp", ht[0][:], src_v[0], ("htC", 0), w=[("C", "x", 0)])
        for t in range(NT):
            s = t % 2
            if t + 1 < NT:
                self.DMA("sp", ht[1 - s][:], src_v[t + 1], ("htC", 1 - s), w=[("C", "x", 1 - s)])
            self.rmsnorm_T(ht[s], 2, hn, hn1T[:, :, t * TT:(t + 1) * TT], (gcol, "gcolC"), "C", s, 0)
        S.barrier()
        qk = [S.sb(f"qk{i}", [128, 384], BF16) for i in range(2)]
        rts = [S.sb(f"rt{i}", [128, 6, 8], F32) for i in range(4)]
        stg = [S.sb(f"stg{i}", [128, 3, 128], BF16) for i in range(4)]
        vst = [S.sb(f"vst{i}", [128, 384], BF16) for i in range(2)]
        cnt = dict(g=0, st=0, v=0)

        pend = []

        def flush():
            while pend:
                pend.pop(0)()

        def proj_group(lhs_fn, col0, ncols, kind, tbl, dst):
            g = cnt["g"]; cnt["g"] += 1
            pt = ps[g % 6]
            wkeys = [("winO", c) for c in range(col0 // 512, (col0 + ncols - 1) // 512 + 1)]
            for kc in range(8):
                self.MM(pt[:, 0:ncols], lhs_fn(kc), winO[:, kc, col0:col0 + ncols], kc == 0, kc == 7, r=wkeys, w=[("ps", g % 6)], maxn=256)
            flush()
            if kind == "v":
                vi = cnt["v"] % 2; cnt["v"] += 1
                self.ACT(vst[vi][:, 0:ncols], pt[:, 0:ncols], AF.Copy, r=[("ps", g % 6)], w=[("vst", vi)])
                self.DMA("sp", dst, vst[vi][:, 0:ncols], ("vst", vi), r=[("vst", vi)], w=[])
                return
            qi = g % 2
            nh = ncols // 64
            self.ACT(qk[qi][:, 0:ncols], pt[:, 0:ncols], AF.Copy, r=[("ps", g % 6)], w=[("qk", qi)])
            pv3 = pt[:, 0:ncols].rearrange("p (h e) -> p h e", e=64)
            qv3 = qk[qi][:, 0:ncols].rearrange("p (h e) -> p h e", e=64)
            x1, x2 = pv3[:, :, 0:8], pv3[:, :, 8:16]
            cb_ = bass.AP(cs, tbl * 8, [[768, 128], [0, nh], [1, 8]])
            sb_ = bass.AP(sn, tbl * 8, [[768, 128], [0, nh], [1, 8]])
            R = [rt[:, 0:nh, :] for rt in rts]
            self.TTo("dve", R[0], x1, cb_, ALU.mult, r=[("ps", g % 6), "cs"], w=["r0"])
            self.TTo("dve", R[1], x2, sb_, ALU.mult, r=[("ps", g % 6), "sn"], w=["r1"])
            self.TTo("dve", R[2], x2, cb_, ALU.mult, r=[("ps", g % 6), "cs"], w=["r2"])
            self.TTo("dve", R[3], x1, sb_, ALU.mult, r=[("ps", g % 6), "sn"], w=["r3"])
            self.TTo("dve", qv3[:, :, 0:8], R[0], R[1], ALU.subtract, r=["r0", "r1"], w=[("qk", qi)])
            self.TTo("dve", qv3[:, :, 8:16], R[2], R[3], ALU.add, r=["r2", "r3"], w=[("qk", qi)])
            nch = ncols // 128
            h = cnt["st"] % 2
            si = cnt["st"] % 4; cnt["st"] += 1

            def tail():
                pbh = self.psbv[h]
                for c in range(nch):
                    self.TR(pbh[:, c * 128:(c + 1) * 128], qk[qi][:, c * 128:(c + 1) * 128], r=[("qk", qi)], w=[("psb", h)])
                self.CP("act" if si % 2 else "dve", stg[si][:, 0:nch, :].rearrange("p c t -> p (c t)"), pbh[:, 0:nch * 128],
                        r=[("psb", h)], w=[("stg", si)])
                self.DMA("sp", dst, stg[si][:, 0:nch, :], ("stg", si), r=[("stg", si)], w=[])
            pend.append(tail)

        KTc_v = self.KTc.rearrange("(c p) t -> p c t", p=128)
        QTc_v = self.QTc.rearrange("(c p) t -> p c t", p=128)
        for blk in range(32):
            lf = lambda kc, blk=blk: hn1T[:, kc, blk * 128:(blk + 1) * 128]
            tbl = 0 * 32 + blk
            for gi in range(2):
                if blk >= OWN0 // 128:
                    qb = blk - OWN0 // 128
                    proj_group(lf, gi * 384, 384, "qk", tbl, QTc_v[:, gi * 3:(gi + 1) * 3, qb * 128:(qb + 1) * 128])
                proj_group(lf, 768 + gi * 384, 384, "qk", tbl, KTc_v[:, gi * 3:(gi + 1) * 3, blk * 128:(blk + 1) * 128])
                proj_group(lf, 1536 + gi * 384, 384, "v", tbl, self.Vc[blk * 128:(blk + 1) * 128, gi * 384:(gi + 1) * 384])
        for pi, d in enumerate((1, 4, 16)):
            L = NLOC // d
            nb = L // 128
            nk_lo = (13, 2, 0)[pi]
            nq_lo = (14, 3, 0)[pi]
            KTd_v = self.KTd[pi].rearrange("(c p) t -> p c t", p=128)
            QTd_v = self.QTd[pi].rearrange("(c p) t -> p c t", p=128)
            for r in range(d):
                for n in range(nk_lo, nb):
                    st0 = r + d * n * 128
                    lf = lambda kc, st0=st0, d=d: hn1T[:, kc, st0: st0 + d * 127 + 1: d]
                    tbl = pi * 32 + r * nb + n
                    u0 = r * L + n * 128
                    if n >= nq_lo:
                        proj_group(lf, 2304 + pi * 256, 256, "qk", tbl, QTd_v[:, :, u0:u0 + 128])
                    proj_group(lf, 3072 + pi * 256, 256, "qk", tbl, KTd_v[:, :, u0:u0 + 128])
                    proj_group(lf, 3840 + pi * 256, 256, "v", tbl, self.Vd[pi][u0:u0 + 128, :])
        flush()
        S.pop()

    def phase_D(self):
        nc, S, I, C, ps = self.nc, self.S, self.I, self.C, self.ps
        S.push()
        lqk = S.sb("lqk", [128, 4, 64], F32)
        for i, nm in enumerate(("lq1", "lk1", "lq2", "lk2")):
            self.DMA("sp", lqk[:, i, :], I[nm].partition_broadcast(128), ("lqk", i), w=[("lqk", i)])
        lp = S.sb("lp", [128, 2, 64], F32)
        lsum = S.sb("lsum", [128, 2], F32)
        neglam = S.sb("neglam", [128, 1], F32)
        for i in range(2):
            self.TTo("dve", lp[:, i, :], lqk[:, 2 * i, :], lqk[:, 2 * i + 1, :], ALU.mult, r=[("lqk", 2 * i), ("lqk", 2 * i + 1)], w=[("lp", i)])
            self.S.op("dve", lambda e, i=i: e.reduce_sum(out=lsum[:, i:i + 1], in_=lp[:, i, :], axis=AX.X), [("lp", i)], [("lsum", i)])
        self.ACT(lsum[:], lsum[:], AF.Exp, r=[("lsum", 0), ("lsum", 1)], w=["lexp"])
        self.TTo("dve", neglam[:], lsum[:, 1:2], lsum[:, 0:1], ALU.subtract, r=["lexp"], w=["neglam"])
        self.TS("dve", neglam[:], neglam[:], -LAMBDA_INIT, None, ALU.add, r=["neglam"], w=["neglam"])
        sgcol = S.sb("sgcol", [128, 1], F32)
        self.col_load(sgcol[:], I["subln"], "sgcol")
        self.TS("dve", sgcol[:], sgcol[:], 1.0 - LAMBDA_INIT, None, ALU.mult, r=["sgcol"], w=["sgcol"])

        KT = [S.sb(f"KTh{i}", [128, NLOC], BF16) for i in range(2)]
        QT = [[S.sb(f"QTh{i}_{sub}", [128, NOWN], BF16) for sub in range(2)] for i in range(2)]
        for i in range(2):
            for sub in range(2):
                self.MEMSET("pool", QT[i][sub][(1 - sub) * 64:(2 - sub) * 64, :], 0.0, w=[("QTz", i, sub)])
        V = [S.sb(f"Vh{i}", [128, 32, 128], BF16) for i in range(2)]
        E = [S.sb(f"E{i}", [128, 512], BF16) for i in range(3)]
        o_ = S.sb("o_", [128, 512], F32)
        sq = S.sb("sq", [128, 512], BF16)
        rs = S.sb("rs", [128, 512], F32)
        ycst = [S.sb(f"ycst{i}", [128, 512], BF16) for i in range(2)]
        Vc_v = self.Vc.rearrange("(n p) c -> p n c", p=128)

        def load(h):
            s = h % 2
            self.DMA("sp", KT[s][:], self.KTc[h * 128:(h + 1) * 128, :], ("KTh", s), w=[("KTh", s)])
            for sub in range(2):
                self.DMA("sp", QT[s][sub][sub * 64:(sub + 1) * 64, :], self.QTc[h * 128 + sub * 64:h * 128 + (sub + 1) * 64, :],
                         ("QTh", s), w=[("QTh", s, sub)])
            self.DMA("sp", V[s][:], Vc_v[:, :, h * 128:(h + 1) * 128], ("Vh", s), w=[("Vh", s)])
        load(0)
        supers = [(1792, 256), (2048, 512), (2560, 512), (3072, 512), (3584, 512)]
        oS = [S.sb(f"oS{i}", [128, 512], F32) for i in range(2)]
        rS = [S.sb(f"rS{i}", [128, 512], F32) for i in range(2)]
        state = dict(nst=0, pending=None)

        def fin_part1(nq):
            self.ACT(oS[0][:, 0:nq], ps[4][:, 0:nq], AF.Copy, r=[("ps", 4)], w=[("oS", 0)])
            self.TS("dve", rS[0][:, 0:nq], ps[6][:, 0:nq], 1e-30, None, ALU.max, r=[("ps", 6)], w=[("rS", 0)])
            self.ACT(oS[1][:, 0:nq], ps[5][:, 0:nq], AF.Copy, r=[("ps", 5)], w=[("oS", 1)])
            self.TS("dve", rS[1][:, 0:nq], ps[7][:, 0:nq], 1e-30, None, ALU.max, r=[("ps", 7)], w=[("rS", 1)])

        def fin_part2(h, qc0, nq):
            for sub in range(2):
                self.ACT(rS[sub][:, 0:nq], rS[sub][:, 0:nq], AF.Ln, r=[("rS", sub)], w=[("rS", sub)])
                self.ACT(rS[sub][:, 0:nq], rS[sub][:, 0:nq], AF.Exp, r=[("rS", sub)], w=[("rS", sub)], scale=-1.0)
                self.TTo("dve", oS[sub][:, 0:nq], oS[sub][:, 0:nq], rS[sub][:, 0:nq], ALU.mult, r=[("oS", sub), ("rS", sub)], w=[("oS", sub)])
            self.STT("dve", o_[:, 0:nq], oS[1][:, 0:nq], neglam[:, 0:1], oS[0][:, 0:nq], ALU.mult, ALU.add,
                     r=[("oS", 0), ("oS", 1), "neglam"], w=["o_"])
            self.ACT(sq[:, 0:nq], o_[:, 0:nq], AF.Square, r=["o_"], w=["sq"])
            self.MM(ps[3][:, 0:nq], self.onesb[:], sq[:, 0:nq], True, True, r=["onesb", "sq"], w=[("ps", 3)], maxn=256)
            self.ACT(rs[:, 0:nq], ps[3][:, 0:nq], AF.Ln, r=[("ps", 3), "eps"], w=["rs"], bias=self.epsb[:], scale=1.0 / 128)
            self.ACT(rs[:, 0:nq], rs[:, 0:nq], AF.Exp, r=["rs"], w=["rs"], scale=-0.5)
            yi = state["nst"] % 2
            state["nst"] += 1
            self.STT("dve", ycst[yi][:, 0:nq], o_[:, 0:nq], sgcol[:, 0:1], rs[:, 0:nq], ALU.mult, ALU.mult,
                     r=["o_", "sgcol", "rs"], w=[("ycst", yi)])
            self.DMA("sp", self.Y1T[h * 128:(h + 1) * 128, qc0:qc0 + nq], ycst[yi][:, 0:nq], ("ycst", yi), r=[("ycst", yi)], w=[])

        for h in range(6):
            s = h % 2
            if h + 1 < 6:
                load(h + 1)
            if h == 0:
                for i in range(32):
                    self.MM(ps[i % 3][:, :], self.onesb[:], KT[0][:, (i % 8) * 512:(i % 8 + 1) * 512], True, True,
                            r=["onesb", ("KTh", 0)], w=[("ps", i % 3)])
                if getattr(self, "wup_issue", None) is not None:
                    self.wup_issue()
                    self.wup_issue = None
            for (q0, nq) in supers:
                qc0 = q0 - OWN0
                nkb = (q0 + nq) // 128
                units = [(kb, sub) for kb in range(nkb) for sub in range(2)]

                def pv(i):
                    kb, sub = units[i]
                    ei = i % 3
                    n0 = max(0, kb * 128 - q0)
                    wd = nq - n0
                    self.MM(ps[4 + sub][:, n0:nq], V[s][:, kb, :], E[ei][:, 0:wd], kb == 0, kb == nkb - 1,
                            r=[("Vh", s), ("E", ei)], w=[("ps", 4 + sub)], maxn=256)
                    self.MM(ps[6 + sub][:, n0:nq], self.onesb[:], E[ei][:, 0:wd], kb == 0, kb == nkb - 1,
                            r=["onesb", ("E", ei)], w=[("ps", 6 + sub)], maxn=256)
                for i, (kb, sub) in enumerate(units):
                    n0 = max(0, kb * 128 - q0)
                    wd = nq - n0
                    diag = kb * 128 >= q0
                    ei = i % 3
                    sc = ps[ei]
                    self.MM(sc[:, 0:wd], KT[s][:, kb * 128:(kb + 1) * 128],
                            QT[s][sub][:, qc0 + n0:qc0 + nq], True, True,
                            r=[("KTh", s), ("QTh", s, sub), ("QTz", s, sub)], w=[("ps", ei)], maxn=256)
                    if i >= 2:
                        pv(i - 2)
                    bias = C[:, C_CTXB:C_CTXB + 1] if kb < 16 else self.zerob[:]
                    self.ACT(E[ei][:, 0:wd], sc[:, 0:wd], AF.Exp, r=[("ps", ei), "cst", "zerob"], w=[("E", ei)], bias=bias, scale=0.125)
                    if diag:
                        self.TTo("pool", E[ei][:, 0:128], E[ei][:, 0:128], self.Ub[:], ALU.mult, r=[("E", ei), "Ub"], w=[("E", ei)])
                    if i == 12 and state["pending"] is not None:
                        fin_part2(*state["pending"])
                        state["pending"] = None
                pv(len(units) - 2)
                pv(len(units) - 1)
                if state["pending"] is not None:
                    fin_part2(*state["pending"])
                    state["pending"] = None
                fin_part1(nq)
                state["pending"] = (h, qc0, nq)
        fin_part2(*state["pending"])
        S.pop()

    def phase_E(self):
        nc, S, I, C, ps = self.nc, self.S, self.I, self.C, self.ps
        S.push()
        M_LU = S.sb("M_LU", [128, 256], BF16)
        M_LvU = S.sb("M_LvU", [128, 256], BF16)
        M_LvUv = S.sb("M_LvUv", [128, 256], BF16)
        for (m, a, b, nm) in ((M_LU, self.Lb, self.Ub, "M_LU"), (M_LvU, self.Lvb, self.Ub, "M_LvU"), (M_LvUv, self.Lvb, self.Uvb, "M_LvUv")):
            self.CP("dve", m[:, 0:128], a[:], r=["Lb", "Lvb"], w=[(nm, 0)])
            self.CP("dve", m[:, 128:256], b[:], r=["Ub", "Uvb"], w=[(nm, 1)])
        MR = lambda nm: [(nm, 0), (nm, 1)]
        Kh = [S.sb(f"Kh{i}", [128, NLOC], BF16) for i in range(2)]
        Qh = [S.sb(f"Qh{i}", [128, NLOC], BF16) for i in range(2)]
        for i in range(2):
            self.MEMSET("pool", Kh[i][64:128, :], 0.0, w=[("Khz", i)])
            self.MEMSET("pool", Qh[i][64:128, :], 0.0, w=[("Qhz", i)])
        Vh = [S.sb(f"Vdh{i}", [128, 32, 64], BF16) for i in range(2)]
        E = [S.sb(f"Ed{i}", [128, 256], BF16) for i in range(2)]
        accn = S.sb("accn", [64, NOWN], F32)
        accd = S.sb("accd", [64, NOWN], F32)
        ydst = [S.sb(f"ydst{i}", [64, NOWN], BF16) for i in range(2)]
        jobs = [(hh, pi) for hh in range(4) for pi in range(3)]

        def load(ji):
            hh, pi = jobs[ji]
            s = ji % 2
            self.DMA("sp", Kh[s][0:64, :], self.KTd[pi][hh * 64:(hh + 1) * 64, :], ("Kh", s), w=[("Kh", s)])
            self.DMA("sp", Qh[s][0:64, :], self.QTd[pi][hh * 64:(hh + 1) * 64, :], ("Qh", s), w=[("Qh", s)])
            self.DMA("sp", Vh[s][:], self.Vd[pi].rearrange("(n p) c -> p n c", p=128)[:, :, hh * 64:(hh + 1) * 64], ("Vdh", s), w=[("Vdh", s)])
        load(0)
        bi = 0
        pend = []

        def flush():
            while pend:
                pend.pop(0)()
        for ji, (hh, pi) in enumerate(jobs):
            s = ji % 2
            if ji + 1 < len(jobs):
                load(ji + 1)
            d = (1, 4, 16)[pi]
            L = NLOC // d
            nb = L // 128
            nq_lo = (14, 3, 0)[pi]
            nctx = 16 // d
            for r in range(d):
                for n in range(nq_lo, nb):
                    u0 = r * L + n * 128
                    kbs = [n - 1, n] if n >= 1 else [n]
                    sl = bi % 2; bi += 1
                    sc = ps[sl]
                    for i, kbn in enumerate(kbs):
                        self.MM(sc[:, i * 128:(i + 1) * 128], Kh[s][:, r * L + kbn * 128: r * L + (kbn + 1) * 128], Qh[s][:, u0:u0 + 128],
                                i == 0, True, r=[("Kh", s), ("Qh", s), ("Khz", s), ("Qhz", s)], w=[("ps", sl)])
                    flush()
                    nw = len(kbs) * 128
                    self.ACT(E[sl][:, 0:nw], sc[:, 0:nw], AF.Exp, r=[("ps", sl)], w=[("Ed", sl)], scale=0.125)
                    if len(kbs) == 2:
                        pc, cc = (n - 1) < nctx, n < nctx
                        nm = "M_LvUv" if (pc and cc) else ("M_LvU" if pc else "M_LU")
                        mt = {"M_LU": M_LU, "M_LvU": M_LvU, "M_LvUv": M_LvUv}[nm]
                        self.TTo("pool", E[sl][:, 0:256], E[sl][:, 0:256], mt[:], ALU.mult, r=[("Ed", sl)] + MR(nm), w=[("Ed", sl)])
                    else:
                        self.TTo("pool", E[sl][:, 0:128], E[sl][:, 0:128], self.Uvb[:], ALU.mult, r=[("Ed", sl), "Uvb"], w=[("Ed", sl)])

                    def tail(s=s, sl=sl, kbs=kbs, r=r, n=n, nb=nb, d=d, pi=pi):
                        pn, pd = ps[2 + sl], ps[4 + sl]
                        for i, kbn in enumerate(kbs):
                            self.MM(pn[0:64, 0:128], Vh[s][:, r * nb + kbn, :], E[sl][:, i * 128:(i + 1) * 128], i == 0, i == len(kbs) - 1,
                                    r=[("Vdh", s), ("Ed", sl)], w=[("ps", 2 + sl)])
                        for i, kbn in enumerate(kbs):
                            self.MM(pd[0:64, 0:128], self.onesb[:, 0:64], E[sl][:, i * 128:(i + 1) * 128], i == 0, i == len(kbs) - 1,
                                    r=["onesb", ("Ed", sl)], w=[("ps", 4 + sl)])
                        a0 = max(0, OWN0 // d - n * 128)
                        if a0 >= 128:
                            return
                        col0 = r + d * (n * 128 + a0) - OWN0
                        cnt_ = 128 - a0
                        cols = slice(col0, col0 + d * (cnt_ - 1) + 1, d)
                        if pi == 0:
                            self.CP("act", accn[:, cols], pn[0:64, a0:128], r=[("ps", 2 + sl)], w=["accn"])
                            self.CP("dve", accd[:, cols], pd[0:64, a0:128], r=[("ps", 4 + sl)], w=["accd"])
                        else:
                            self.TTo("dve", accn[:, cols], accn[:, cols], pn[0:64, a0:128], ALU.add, r=[("ps", 2 + sl), "accn"], w=["accn"])
                            self.TTo("dve", accd[:, cols], accd[:, cols], pd[0:64, a0:128], ALU.add, r=[("ps", 4 + sl), "accd"], w=["accd"])
                    pend.append(tail)
            flush()
            if pi == 2:
                yi = hh % 2
                self.TS("dve", accd[:], accd[:], 1e-30, None, ALU.max, r=["accd"], w=["accd"])
                self.RECIP(accd[:], accd[:], r=["accd"], w=["accd"])
                self.TTo("dve", ydst[yi][:], accn[:], accd[:], ALU.mult, r=["accn", "accd"], w=[("ydst", yi)])
                self.DMA("sp", self.Y1T[768 + hh * 64:768 + (hh + 1) * 64, :], ydst[yi][:], ("ydst", yi), r=[("ydst", yi)], w=[])
        S.pop()


_CACHE = {}


def _consts(p):
    c = np.zeros((128, C_N), np.float32)
    pi_, fi = np.meshgrid(np.arange(128), np.arange(128), indexing="ij")
    c[:, C_ID:C_ID + 128] = (pi_ == fi)
    c[:, C_L:C_L + 128] = (fi <= pi_)
    c[:, C_U:C_U + 128] = (pi_ <= fi)
    c[:, C_VIS] = 1.0 if p == 1 else 0.0
    c[:, C_CTXB] = 0.0 if p == 1 else -30000.0
    t = np.arange(16)
    for g, w in enumerate((2, 4, 8, 16)):
        true_rc = 1.0 / np.minimum(t + 1, w)
        c[:, C_RCC + g * 16:C_RCC + (g + 1) * 16] = true_rc
        c[:, C_RCO + g * 16:C_RCO + (g + 1) * 16] = true_rc if p == 0 else 1.0 / w
    inv = np.float32(500000.0) ** (-(np.arange(0, 16, 2, dtype=np.float32)) / np.float32(16))
    c[:, C_INV:C_INV + 8] = inv.astype(np.float32)
    return c


def make_in_maps(inp):
    f = lambda a: np.ascontiguousarray(np.asarray(a))
    x = f(inp["x"]); pos = f(inp["positions"]).astype(np.int32)
    shared = dict(
        norm_mix=f(inp["norm_mix"]), norm_ffn=f(inp["norm_ffn"]), final_norm=f(inp["final_norm"]).reshape(1, D),
        even_w_in=f(inp["even_w_in"])[0], gmlp_v_gain=f(inp["gmlp_v_gain"]).reshape(1, 512),
        gmlp_w_s=f(inp["gmlp_w_s"])[0], gmlp_b_s=f(inp["gmlp_b_s"]).reshape(1, 512),
        pool_w=f(inp["pool_w"])[0], pool_scale=f(inp["pool_scale"]).reshape(1, 512),
        even_w_out=f(inp["even_w_out"])[0], odd_w_in=f(inp["odd_w_in"])[0],
        lq1=f(inp["lambda_q1"]).reshape(1, 64), lk1=f(inp["lambda_k1"]).reshape(1, 64),
        lq2=f(inp["lambda_q2"]).reshape(1, 64), lk2=f(inp["lambda_k2"]).reshape(1, 64),
        subln=f(inp["subln_gain"]).reshape(1, 128), odd_w_out=f(inp["odd_w_out"])[0],
        ffn_w_up=f(inp["ffn_w_up"]), ffn_conv_w=f(inp["ffn_conv_w"]).reshape(2, 3, DFF),
        ffn_conv_b=f(inp["ffn_conv_b"]), ffn_w_down=f(inp["ffn_w_down"]),
    )
    maps = []
    for core in range(8):
        b, p = core // 2, core % 2
        if p == 1:
            xin = x[b]
            ps_ = pos[b]
        else:
            xin = np.concatenate([np.zeros((2048, D), np.float32), x[b, :2048]], axis=0)
            ps_ = np.concatenate([np.zeros((2048,), np.int32), pos[b, :2048]], axis=0)
        m = dict(shared)
        m["xin"] = np.ascontiguousarray(xin)
        m["pos"] = np.ascontiguousarray(ps_.reshape(1, NLOC))
        m["cst"] = _consts(p)
        maps.append(m)
    return maps


def kernel(**inp):
    if "nc" not in _CACHE:
        _CACHE["nc"] = KB().build()
    nc = _CACHE["nc"]
    maps = make_in_maps(inp)
    res = run_bass_kernel_spmd(nc, maps, core_ids=list(range(8)))
    out = np.zeros((4, 4096, D), np.float32)
    for core in range(8):
        b, p = core // 2, core % 2
        out[b, p * 2048:(p + 1) * 2048] = res.results[core]["out"]
    return out
```
